# Optimizing a Trainium2 kernel written in Bass

```python
import math
import jax, jax.numpy as jnp
from jax import lax
import numpy as np

D_MODEL = 1024
BATCH = 2
SEQ = 16384
DEPTH = 4

N_MEM = 256
HEAD_DIM = 64
N_MEM_HEADS = 4
MEM_WIDTH = N_MEM_HEADS * HEAD_DIM
CONV_WIDTH = D_MODEL - MEM_WIDTH
CONV_K = 3
N_Q_HEADS = CONV_WIDTH // HEAD_DIM
N_KV_HEADS = 4
GROUP = N_Q_HEADS // N_KV_HEADS
Q_WIDTH = N_Q_HEADS * HEAD_DIM
KV_WIDTH = N_KV_HEADS * HEAD_DIM
A_PROJ = 3 * CONV_WIDTH + MEM_WIDTH
B_PROJ = Q_WIDTH + MEM_WIDTH
WINDOW = 128
BLOCK = 128
REL_BUCKETS = 32
REL_MAX_DIST = 128
D_FF = ((8 * D_MODEL + 3 * 256 - 1) // (3 * 256)) * 256
N_A = DEPTH // 2
N_B = DEPTH - N_A
EPS = 1e-5

kernel_name = 'yoco_shortconv_swa_sink_hybrid'


def rmsnorm(x, g):
    x32 = x.astype(jnp.float32)
    y = x32 * lax.rsqrt(jnp.mean(x32 * x32, axis=-1, keepdims=True) + EPS)
    return (y * g.astype(jnp.float32)).astype(x.dtype)


def _rel_bucket(dist):
    max_exact = REL_BUCKETS // 2
    d = jnp.maximum(dist, 1).astype(jnp.float32)
    large = max_exact + (jnp.log(d / max_exact) / math.log(REL_MAX_DIST / max_exact)
                         * (REL_BUCKETS - max_exact)).astype(jnp.int32)
    large = jnp.minimum(large, REL_BUCKETS - 1)
    return jnp.where(dist < max_exact, dist, large)


def _band_geometry(n_blocks):
    qi = jnp.arange(BLOCK, dtype=jnp.int32)[:, None]
    kj = jnp.arange(2 * BLOCK, dtype=jnp.int32)[None, :]
    dist = qi + BLOCK - kj
    in_window = (dist >= 0) & (dist < WINDOW)
    first = (jnp.arange(n_blocks) == 0)[:, None, None]
    mask = in_window[None] & ~(first & (kj[None] < BLOCK))
    bucket = _rel_bucket(jnp.maximum(dist, 0))
    return mask, bucket


def _band(t):
    bsz, s = t.shape[0], t.shape[1]
    tb = t.reshape(bsz, s // BLOCK, BLOCK, N_KV_HEADS, HEAD_DIM)
    prev = jnp.concatenate([jnp.zeros_like(tb[:, :1]), tb[:, :-1]], axis=1)
    return jnp.concatenate([prev, tb], axis=2)


def _short_conv(u, b_gate, c_gate, w):
    v = c_gate * u
    s = v.shape[1]
    vp = jnp.pad(v, ((0, 0), (CONV_K - 1, 0), (0, 0)))
    conv = w[0] * vp[:, 0:s] + w[1] * vp[:, 1:s + 1] + w[2] * vp[:, 2:s + 2]
    return b_gate * conv


def _swa_sinks(q, k_band, v_band, sinks, rel_bias, mask, bucket):
    bsz, s, _ = q.shape
    nb = s // BLOCK
    qb = q.reshape(bsz, nb, BLOCK, N_KV_HEADS, GROUP, HEAD_DIM)
    logits = jnp.einsum('bnqhgd,bnjhd->bnhgqj', qb, k_band).astype(jnp.float32) * (HEAD_DIM ** -0.5)
    bias = jnp.transpose(rel_bias.astype(jnp.float32)[bucket], (2, 0, 1))
    bias = bias.reshape(N_KV_HEADS, GROUP, BLOCK, 2 * BLOCK)
    logits = jnp.where(mask[None, :, None, None], logits + bias, -jnp.inf)
    sink = sinks.astype(jnp.float32).reshape(N_KV_HEADS, GROUP, 1, 1)
    m = jnp.maximum(jnp.max(logits, axis=-1, keepdims=True), sink)
    p = jnp.exp(logits - m)
    denom = jnp.sum(p, axis=-1, keepdims=True) + jnp.exp(sink - m)
    probs = (p / denom).astype(v_band.dtype)
    out = jnp.einsum('bnhgqj,bnjhd->bnqhgd', probs, v_band)
    return out.reshape(bsz, s, Q_WIDTH)


def _mem_attention(q_mem, mem_k, mem_v):
    bsz, s = q_mem.shape[0], q_mem.shape[1]
    logits = jnp.einsum('bshd,bmhd->bhsm', q_mem, mem_k).astype(jnp.float32) * (HEAD_DIM ** -0.5)
    probs = jax.nn.softmax(logits, axis=-1).astype(mem_v.dtype)
    return jnp.einsum('bhsm,bmhd->bshd', probs, mem_v).reshape(bsz, s, MEM_WIDTH)


def setup_inputs(seed: int = 0) -> dict:
    key = jax.random.key(seed)
    ks = jax.random.split(key, 20)

    def nrm(k, shape, scale):
        return jax.random.normal(k, shape, jnp.float32) * scale

    def gain(k, shape):
        return 1.0 + nrm(k, shape, 0.05)

    return {
        'x': nrm(ks[0], (BATCH, SEQ, D_MODEL), 1.0),
        'mem': nrm(ks[1], (BATCH, N_MEM, D_MODEL), 1.0),
        'norm_mix': gain(ks[2], (DEPTH, D_MODEL)),
        'norm_ffn': gain(ks[3], (DEPTH, D_MODEL)),
        'a_w_in': nrm(ks[4], (N_A, D_MODEL, A_PROJ), D_MODEL ** -0.5),
        'a_conv_w': nrm(ks[5], (N_A, CONV_K, CONV_WIDTH), CONV_K ** -0.5),
        'a_w_out': nrm(ks[6], (N_A, CONV_WIDTH + MEM_WIDTH, D_MODEL), (CONV_WIDTH + MEM_WIDTH) ** -0.5),
        'kv_norm': gain(ks[7], (D_MODEL,)),
        'w_kv': nrm(ks[8], (D_MODEL, 2 * KV_WIDTH), D_MODEL ** -0.5),
        'b_w_q': nrm(ks[9], (N_B, D_MODEL, B_PROJ), D_MODEL ** -0.5),
        'b_sinks': nrm(ks[10], (N_B, N_Q_HEADS), 0.5),
        'b_w_out': nrm(ks[11], (N_B, Q_WIDTH + MEM_WIDTH, D_MODEL), (Q_WIDTH + MEM_WIDTH) ** -0.5),
        'rel_bias': nrm(ks[12], (REL_BUCKETS, N_Q_HEADS), 0.5),
        'mem_norm': gain(ks[13], (D_MODEL,)),
        'w_mem_kv': nrm(ks[14], (DEPTH, D_MODEL, 2 * MEM_WIDTH), D_MODEL ** -0.5),
        'w_gate': nrm(ks[15], (DEPTH, D_MODEL, D_FF), D_MODEL ** -0.5),
        'w_up': nrm(ks[16], (DEPTH, D_MODEL, D_FF), D_MODEL ** -0.5),
        'w_down': nrm(ks[17], (DEPTH, D_FF, D_MODEL), D_FF ** -0.5),
        'final_norm': gain(ks[18], (D_MODEL,)),
    }


def reference(x, mem, norm_mix, norm_ffn, a_w_in, a_conv_w, a_w_out, kv_norm, w_kv,
              b_w_q, b_sinks, b_w_out, rel_bias, mem_norm, w_mem_kv, w_gate, w_up, w_down,
              final_norm):
    bsz, s, _ = x.shape
    mask, bucket = _band_geometry(s // BLOCK)
    mem_n = rmsnorm(mem, mem_norm)
    k_band = None
    v_band = None
    for i in range(DEPTH):
        if i == N_A:
            kv = rmsnorm(x, kv_norm) @ w_kv
            k, v = jnp.split(kv, 2, axis=-1)
            k_band = _band(k.reshape(bsz, s, N_KV_HEADS, HEAD_DIM))
            v_band = _band(v.reshape(bsz, s, N_KV_HEADS, HEAD_DIM))
        mem_kv = mem_n @ w_mem_kv[i]
        mk, mv = jnp.split(mem_kv, 2, axis=-1)
        mk = mk.reshape(bsz, N_MEM, N_MEM_HEADS, HEAD_DIM)
        mv = mv.reshape(bsz, N_MEM, N_MEM_HEADS, HEAD_DIM)
        h = rmsnorm(x, norm_mix[i])
        if i < N_A:
            proj = h @ a_w_in[i]
            u, b_gate, c_gate, q_mem = jnp.split(
                proj, [CONV_WIDTH, 2 * CONV_WIDTH, 3 * CONV_WIDTH], axis=-1)
            y_tok = _short_conv(u, b_gate, c_gate, a_conv_w[i])
            w_out = a_w_out[i]
        else:
            j = i - N_A
            proj = h @ b_w_q[j]
            q, q_mem = jnp.split(proj, [Q_WIDTH], axis=-1)
            y_tok = _swa_sinks(q, k_band, v_band, b_sinks[j], rel_bias, mask, bucket)
            w_out = b_w_out[j]
        y_mem = _mem_attention(q_mem.reshape(bsz, s, N_MEM_HEADS, HEAD_DIM), mk, mv)
        x = x + jnp.concatenate([y_tok, y_mem], axis=-1) @ w_out
        h = rmsnorm(x, norm_ffn[i])
        x = x + (jax.nn.silu(h @ w_gate[i]) * (h @ w_up[i])) @ w_down[i]
    return rmsnorm(x, final_norm)
```

```python
import math
from contextlib import ExitStack

import numpy as np
import concourse.bass as bass
import concourse.mybir as mybir
from concourse.bass_utils import run_bass_kernel_spmd

F32 = mybir.dt.float32
BF16 = mybir.dt.bfloat16
AF = mybir.ActivationFunctionType
ALU = mybir.AluOpType

D = 1024
NCH = 8
SEQ = 16384
BATCH = 2
NCORE = 8
TOK = 4096
HALO = 136
TT = 512
NT = TOK // TT
DFF = 2816
NF = DFF // 128
N_MEM = 256
NBLK = TOK // 128 + 1
EPS = 1e-5
NEG = -30000.0
NSLOT = 5
SLOT_ELEMS = 4096

HEAD_LO = [0, 1, 2, 6, 7, 8]
HEAD_HI = [3, 4, 5, 9, 10, 11]
SLOT_HEAD = [HEAD_LO[s // 2] if s % 2 == 0 else HEAD_HI[s // 2] for s in range(12)]

G_MIX, G_FFN, G_KV, G_MEM, G_FIN = 0, 32, 64, 72, 80
V_CONV = 88
V_FLAG = 124
V_SINK = 125
NV = 152


def piece_table():
    names = []
    for i in range(4):
        names.append((f"memkv{i}", 4096))
    for l in range(2):
        for p in range(5):
            names.append((f"ain{l}_{p}", 4096))
        for p in range(2):
            names.append((f"aout{l}_{p}", 4096))
        for fg in range(6):
            n = 4096 if fg < 5 else 2048
            names.append((f"g{l}_{fg}", n))
            names.append((f"u{l}_{fg}", n))
        for m in range(8):
            names.append((f"d{l}_{m}", NF * 128))
    names.append(("kv", 4096))
    for j in range(2):
        l = 2 + j
        for p in range(2):
            names.append((f"bq{j}_{p}", 4096))
        for p in range(2):
            names.append((f"bout{j}_{p}", 4096))
        for fg in range(6):
            n = 4096 if fg < 5 else 2048
            names.append((f"g{l}_{fg}", n))
            names.append((f"u{l}_{fg}", n))
        for m in range(8):
            names.append((f"d{l}_{m}", NF * 128))
    table = {}
    off = 0
    for nm, n in names:
        table[nm] = (off, n)
        off += 128 * n
    return table, off, [nm for nm, _ in names]


def _as_piece(w2d):
    K, Fc = w2d.shape
    kc = K // 128
    return np.ascontiguousarray(w2d.reshape(kc, 128, Fc).transpose(1, 0, 2)).reshape(128, kc * Fc)


def pack_weights(inp):
    table, total, _ = piece_table()
    wflat = np.empty((total,), np.float32)

    def put(name, w2d):
        off, n = table[name]
        blk = _as_piece(np.asarray(w2d, np.float32))
        assert blk.shape == (128, n), (name, blk.shape, n)
        wflat[off:off + 128 * n] = blk.reshape(-1)

    for i in range(4):
        put(f"memkv{i}", inp["w_mem_kv"][i])
    a_chunks = []
    for j in range(6):
        a_chunks += [j, 12 + j, 6 + j]
    a_chunks += [18, 19]
    a_cols = np.concatenate([np.arange(c * 128, (c + 1) * 128) for c in a_chunks])
    qperm = np.concatenate([np.arange(h * 64, (h + 1) * 64) for h in SLOT_HEAD] + [np.arange(768, 1024)])
    for l in range(4):
        for fg in range(6):
            lo, hi = fg * 512, min((fg + 1) * 512, DFF)
            put(f"g{l}_{fg}", inp["w_gate"][l][:, lo:hi])
            put(f"u{l}_{fg}", inp["w_up"][l][:, lo:hi])
        for m in range(8):
            put(f"d{l}_{m}", inp["w_down"][l][:, m * 128:(m + 1) * 128])
    for l in range(2):
        wp = np.asarray(inp["a_w_in"][l])[:, a_cols]
        for p in range(5):
            put(f"ain{l}_{p}", wp[:, p * 512:(p + 1) * 512])
        for p in range(2):
            put(f"aout{l}_{p}", inp["a_w_out"][l][:, p * 512:(p + 1) * 512])
    put("kv", inp["w_kv"])
    for j in range(2):
        wq = np.asarray(inp["b_w_q"][j])[:, qperm]
        wo = np.asarray(inp["b_w_out"][j])[qperm, :]
        for p in range(2):
            put(f"bq{j}_{p}", wq[:, p * 512:(p + 1) * 512])
            put(f"bout{j}_{p}", wo[:, p * 512:(p + 1) * 512])
    return wflat


def _rel_bucket_np(dist):
    max_exact = 16
    d = np.maximum(dist, 1).astype(np.float32)
    large = max_exact + (np.log(d / max_exact) / math.log(128 / max_exact) * (32 - max_exact)).astype(np.int32)
    large = np.minimum(large, 31)
    return np.where(dist < max_exact, dist, large)


def make_bias_tables(rel_bias):
    rel_bias = np.asarray(rel_bias, np.float32)
    kj = np.arange(128)[:, None, None]
    kb = np.arange(2)[None, :, None]
    qi = np.arange(128)[None, None, :]
    dist = qi + 128 - (kb * 128 + kj)
    inwin = (dist >= 0) & (dist < 128)
    bucket = _rel_bucket_np(np.maximum(dist, 0))
    fill = np.float32(NEG / 8.0)
    out = np.empty((128, 12, 2, 128), np.float32)
    for s in range(12):
        g = rel_bias[bucket, SLOT_HEAD[s]]
        out[:, s] = np.where(inwin, g, fill)
    first = out.copy()
    first[:, :, 0, :] = fill
    return out.reshape(128, 12 * 256), first.reshape(128, 12 * 256)


def make_vecs(inp, flag):
    v = np.zeros((128, NV), np.float32)

    def putg(col, g):
        v[:, col:col + 8] = np.asarray(g, np.float32).reshape(8, 128).T

    for l in range(4):
        putg(G_MIX + 8 * l, inp["norm_mix"][l])
        putg(G_FFN + 8 * l, inp["norm_ffn"][l])
    putg(G_KV, inp["kv_norm"])
    putg(G_MEM, inp["mem_norm"])
    putg(G_FIN, inp["final_norm"])
    cw = np.asarray(inp["a_conv_w"], np.float32)
    for l in range(2):
        for j in range(6):
            for t in range(3):
                v[:, V_CONV + (l * 6 + j) * 3 + t] = cw[l, t, j * 128:(j + 1) * 128]
    v[:, V_FLAG] = flag
    sk = np.asarray(inp["b_sinks"], np.float32)
    for lj in range(2):
        for s in range(12):
            v[:, V_SINK + lj * 12 + s] = sk[lj, SLOT_HEAD[s]]
    return v


class Prog:
    ENGS = ("pe", "act", "dve", "pool", "sp")
    EPOCH = 24000

    def __init__(self):
        self.ops = {e: [] for e in self.ENGS}
        self.cnt = {e: 0 for e in self.ENGS}
        self.scnt = {}
        self.lastw = {}
        self.readers = {}

    def _deps(self, reads, writes):
        deps = {}

        def add(tok):
            key = (tok[0], tok[1])
            if deps.get(key, -1) < tok[2]:
                deps[key] = tok[2]

        for r in reads:
            t = self.lastw.get(r)
            if t is not None:
                add(t)
        for w in writes:
            t = self.lastw.get(w)
            if t is not None:
                add(t)
            for key, idx in self.readers.get(w, {}).items():
                add((key[0], key[1], idx))
        return deps

    def _update(self, tok, reads, writes):
        key = (tok[0], tok[1])
        for r in reads:
            d = self.readers.setdefault(r, {})
            if d.get(key, -1) < tok[2]:
                d[key] = tok[2]
        for w in writes:
            self.lastw[w] = tok
            self.readers[w] = {}

    def op(self, eng, fn, reads=(), writes=()):
        deps = self._deps(reads, writes)
        idx = self.cnt[eng]
        self.cnt[eng] += 1
        tok = ("e", eng, idx)
        self.ops[eng].append((fn, deps, tok))
        self._update(tok, reads, writes)

    def dma(self, eng, stream, fn, reads=(), writes=()):
        deps = self._deps(reads, writes)
        idx = self.scnt.get(stream, 0)
        self.scnt[stream] = idx + 1
        if idx > 0:
            key = ("s", stream)
            if deps.get(key, -1) < idx - 1:
                deps[key] = idx - 1
        tok = ("s", stream, idx)
        self.ops[eng].append((fn, deps, tok))
        self._update(tok, reads, writes)

    def wait_all(self, eng, reads):
        deps = self._deps(reads, ())
        self.ops[eng].append((None, deps, None))

    def emit(self, nc, stack):
        n_ep = {e: max(1, -(-self.cnt[e] // self.EPOCH)) for e in self.ENGS}
        esem = {e: [stack.enter_context(nc.semaphore(f"e_{e}_{i}")) for i in range(n_ep[e])]
                for e in self.ENGS if self.cnt[e] > 0}
        ssem = {s: stack.enter_context(nc.semaphore(f"s_{s}")) for s in self.scnt}
        block = stack.enter_context(nc.Block())
        prog = self

        def run(engname, eng):
            known = {}
            for fn, deps, tok in prog.ops[engname]:
                for key, idx in deps.items():
                    if key[0] == "e" and key[1] == "pe" and engname == "pe":
                        continue
                    if known.get(key, -1) >= idx:
                        continue
                    known[key] = idx
                    if key[0] == "e":
                        eng.wait_ge(esem[key[1]][idx // prog.EPOCH], idx % prog.EPOCH + 1)
                    else:
                        eng.wait_ge(ssem[key[1]], 16 * (idx + 1))
                if fn is None:
                    continue
                ins = fn(eng)
                if tok[0] == "e":
                    ins.then_inc(esem[engname][tok[2] // prog.EPOCH], 1)
                else:
                    ins.then_inc(ssem[tok[1]], 16)

        @block.tensor
        def _(e):
            run("pe", e)

        @block.scalar
        def _(e):
            run("act", e)

        @block.vector
        def _(e):
            run("dve", e)

        @block.gpsimd
        def _(e):
            run("pool", e)

        @block.sync
        def _(e):
            run("sp", e)


class StopTile(Exception):
    pass


class NullProg:
    def op(self, *a, **k):
        pass

    def dma(self, *a, **k):
        pass

    def wait_all(self, *a, **k):
        pass


class Builder:
    def __init__(self, nc, T, nlayers=4, final=True, ntiles=NT):
        self.nc = nc
        self.T = T
        self.nlayers = nlayers
        self.final = final
        self.ntiles = ntiles
        self.table, self.total, self.order = piece_table()

    debug = None

    def dump(self, name, src, n, nch=8):
        if self.debug != name:
            return
        P, T = self.P, self.T
        P.op("act", lambda e: e.activation(out=T["xb"][:, 0:nch, 0:n], in_=src, func=AF.Copy),
             reads=[("x", c) for c in range(8)] + [("h", c) for c in range(8)] + [("y", c) for c in range(8)]
             + [("q", c) for c in range(8)] + [("act", c) for c in range(NF)],
             writes=[("x", c) for c in range(8)])
        raise StopTile()

    def bank(self):
        b = self._bank
        self._bank = (b + 1) % 8
        return b

    def wget(self, name):
        if self.recording:
            self.seq.append(name)
            return 0
        i = self.wpos
        assert self.seq[i] == name, (i, self.seq[i], name)
        P, Tn = self.P, self.T
        hi = min(i + NSLOT - 1, len(self.seq))
        while self.wloaded < hi:
            j = self.wloaded
            s = j % NSLOT
            nm = self.seq[j]
            off, n = self.table[nm]
            src = Tn["wscr"][off:off + 128 * n].rearrange("(p n) -> p n", p=128)
            dst = Tn["slots"][s][:, 0:n]
            P.dma("sp", f"ld{s}", lambda e, dst=dst, src=src: e.dma_start(out=dst, in_=src),
                  reads=[("wscr", nm)], writes=[("slot", s)])
            self.wloaded += 1
        self.wpos += 1
        return i % NSLOT

    def run(self, P, recording, seq=None):
        self.P = P
        self.recording = recording
        self.seq = [] if recording else seq
        self.wpos = 0
        self.wloaded = 0
        self._bank = 0
        self.setup()
        self.tile(-1)
        for ti in range(self.ntiles):
            self.tile(ti)
        P.wait_all("act", [("x", c) for c in range(8)] + ["yT"])
        return self.seq

    def setup(self):
        P, T, nc = self.P, self.T, self.nc
        for i, nm in enumerate(self.order):
            off, n = self.table[nm]
            src = T["wflat"][off:off + 128 * n].rearrange("(p n) -> p n", p=128)
            dst = T["wscr"][off:off + 128 * n].rearrange("(p n) -> p n", p=128)
            P.dma("pool", f"cv{i % 4}", lambda e, dst=dst, src=src: e.dma_start(out=dst, in_=src),
                  reads=[], writes=[("wscr", nm)])
        P.op("dve", lambda e: e.memset(T["onesD"][:, :], 1.0 / D),
             reads=[("wscr", nm) for nm in self.order], writes=["onesD"])
        P.op("dve", lambda e: e.memset(T["ones1"][:, :], 1.0), writes=["ones1"])
        P.op("dve", lambda e: e.memset(T["carry"][:, :, :], 0.0), writes=[("carry", i) for i in range(12)])
        P.dma("act", "misc", lambda e: e.dma_start(out=T["vecs"][:, :], in_=T["vecs_d"][:, :]), writes=["vecs"])
        xflat = T["xb"][:, :, :].rearrange("p c t -> p (c t)")
        P.dma("act", "misc", lambda e: e.dma_start(out=xflat[:, 0:128], in_=T["ident_d"][:, :]),
              writes=[("x", c) for c in range(8)])
        P.op("act", lambda e: e.activation(out=T["ident"][:, :], in_=xflat[:, 0:128], func=AF.Copy),
             reads=[("x", c) for c in range(8)], writes=["ident"])
        for src_name, dst_name in (("bias_d", "bias8"), ("biasf_d", "biasF")):
            P.dma("act", "misc", lambda e, s=src_name: e.dma_start(out=xflat[:, 0:3072], in_=T[s][:, :]),
                  writes=[("x", c) for c in range(8)])
            dstf = T[dst_name][:, :, :].rearrange("p s n -> p (s n)")
            P.op("act", lambda e, dstf=dstf: e.activation(out=dstf, in_=xflat[:, 0:3072], func=AF.Copy, scale=8.0),
                 reads=[("x", c) for c in range(8)], writes=[dst_name])
        P.op("act", lambda e: e.activation(out=T["esink"][:, :], in_=T["vecs"][:, V_SINK:V_SINK + 24], func=AF.Exp),
             reads=["vecs"], writes=["esink"])
        P.dma("act", "misc",
              lambda e: e.dma_start(out=T["xb"][:, :, 0:N_MEM],
                                    in_=T["memT"].rearrange("(c p) t -> p c t", p=128)),
              writes=[("x", c) for c in range(8)])
        self.rmsnorm(N_MEM, G_MEM)
        hb, ps = T["hb"], T["ps"]
        for l in range(4):
            s = self.wget(f"memkv{l}")
            sl = T["slots"][s]
            for i in range(2):
                b = self.bank()
                for k in range(8):
                    P.op("pe", lambda e, b=b, k=k, i=i, sl=sl: e.matmul(
                        ps[b][:, 0:N_MEM], sl[:, k * 512 + i * 128:k * 512 + (i + 1) * 128], hb[:, k, 0:N_MEM],
                        start=(k == 0), stop=(k == 7)),
                        reads=[("slot", s), ("h", k)], writes=[("ps", b)])
                P.op("act", lambda e, b=b, l=l, i=i: e.activation(out=T["mkT"][:, l, i, :], in_=ps[b][:, 0:N_MEM], func=AF.Copy),
                     reads=[("ps", b)], writes=["mk"])
            for mc in range(2):
                b = self.bank()
                for k in range(8):
                    P.op("pe", lambda e, b=b, k=k, mc=mc, sl=sl: e.matmul(
                        ps[b][:, 0:256], hb[:, k, mc * 128:(mc + 1) * 128], sl[:, k * 512 + 256:k * 512 + 512],
                        start=(k == 0), stop=(k == 7)),
                        reads=[("slot", s), ("h", k)], writes=[("ps", b)])
                P.op("act", lambda e, b=b, l=l, mc=mc: e.activation(out=T["mv"][:, l, mc, :], in_=ps[b][:, 0:256], func=AF.Copy),
                     reads=[("ps", b)], writes=["mv"])

    def rmsnorm(self, n, gcol, to_x=False):
        P, T = self.P, self.T
        xb, hb, actb, ps, rstd, vecs = T["xb"], T["hb"], T["actb"], T["ps"], T["rstd"], T["vecs"]
        P.op("act", lambda e: e.activation(out=actb[:, 0:8, 0:n], in_=xb[:, :, 0:n], func=AF.Square),
             reads=[("x", c) for c in range(8)], writes=[("act", c) for c in range(8)])
        b = self.bank()
        for c in range(8):
            P.op("pe", lambda e, c=c: e.matmul(ps[b][:, 0:n], T["onesD"][:, :], actb[:, c, 0:n],
                                               start=(c == 0), stop=(c == 7)),
                 reads=[("act", c), "onesD"], writes=[("ps", b)])
        P.op("act", lambda e: e.activation(out=rstd[:, 0:n], in_=ps[b][:, 0:n], func=AF.Ln, bias=T["epsc"][:, 0:1]),
             reads=[("ps", b), "epsc"], writes=["rstd"])
        P.op("act", lambda e: e.activation(out=rstd[:, 0:n], in_=rstd[:, 0:n], func=AF.Exp, scale=-0.5),
             reads=["rstd"], writes=["rstd"])
        for c in range(8):
            if to_x:
                P.op("dve", lambda e, c=c: e.scalar_tensor_tensor(
                    out=xb[:, c, 0:n], in0=xb[:, c, 0:n], scalar=vecs[:, gcol + c:gcol + c + 1], in1=rstd[:, 0:n],
                    op0=ALU.mult, op1=ALU.mult),
                    reads=[("x", c), "rstd", "vecs"], writes=[("x", c)])
            else:
                P.op("dve", lambda e, c=c: e.scalar_tensor_tensor(
                    out=hb[:, c, 0:n], in0=xb[:, c, 0:n], scalar=vecs[:, gcol + c:gcol + c + 1], in1=rstd[:, 0:n],
                    op0=ALU.mult, op1=ALU.mult),
                    reads=[("x", c), "rstd", "vecs"], writes=[("h", c)])

    def proj_chunk(self, s, col0, kstride, nk, rhs_t, rhs_res, n):
        P, T = self.P, self.T
        ps, sl = T["ps"], T["slots"][s]
        b = self.bank()
        for k in range(nk):
            P.op("pe", lambda e, k=k: e.matmul(ps[b][:, 0:n], sl[:, k * kstride + col0:k * kstride + col0 + 128],
                                               rhs_t[:, k, 0:n], start=(k == 0), stop=(k == nk - 1)),
                 reads=[("slot", s), (rhs_res, k)], writes=[("ps", b)])
        return b

    def residual_add(self, b, m, n):
        P, T = self.P, self.T
        xb, ps = T["xb"], T["ps"]
        P.op("dve", lambda e: e.tensor_tensor(out=xb[:, m, 0:n], in0=ps[b][:, 0:n], in1=xb[:, m, 0:n], op=ALU.add),
             reads=[("ps", b), ("x", m)], writes=[("x", m)])

    def a_mixer(self, l, n, halo):
        P, T = self.P, self.T
        ps, hb, qb, yb, vecs = T["ps"], T["hb"], T["qb"], T["yb"], T["vecs"]
        order = []
        for j in range(6):
            order += [("u", j), ("C", j), ("B", j)]
        order += [("q", 0), ("q", 1)]
        self.rmsnorm(n, G_MIX + 8 * l)
        self.dump(f"h{l}", hb[:, :, 0:n], n)
        for p in range(5):
            s = self.wget(f"ain{l}_{p}")
            for jj in range(4):
                kind, j = order[p * 4 + jj]
                b = self.proj_chunk(s, jj * 128, 512, 8, hb, "h", n)
                r = j % 2
                usb, vb, accb = T["usb"][r], T["vb"][r], T["accb"][r]
                if kind == "u":
                    P.op("act", lambda e, b=b, usb=usb: e.activation(out=usb[:, 0:n], in_=ps[b][:, 0:n], func=AF.Copy),
                         reads=[("ps", b)], writes=[("usb", r)])
                elif kind == "C":
                    ci = l * 6 + j
                    cw = V_CONV + ci * 3
                    P.op("dve", lambda e, vb=vb, ci=ci: e.tensor_copy(out=vb[:, 0:2], in_=T["carry"][:, ci, :]),
                         reads=[("carry", ci)], writes=[("vb", r)])
                    P.op("dve", lambda e, b=b, vb=vb, usb=usb: e.tensor_tensor(
                        out=vb[:, 2:2 + n], in0=ps[b][:, 0:n], in1=usb[:, 0:n], op=ALU.mult),
                        reads=[("ps", b), ("usb", r)], writes=[("vb", r)])
                    if halo:
                        P.op("dve", lambda e, vb=vb, ci=ci: e.tensor_scalar(
                            out=T["carry"][:, ci, :], in0=vb[:, n:n + 2], scalar1=vecs[:, V_FLAG:V_FLAG + 1],
                            scalar2=None, op0=ALU.mult),
                            reads=[("vb", r), "vecs"], writes=[("carry", ci)])
                    else:
                        P.op("dve", lambda e, vb=vb, ci=ci: e.tensor_copy(out=T["carry"][:, ci, :], in_=vb[:, n:n + 2]),
                             reads=[("vb", r)], writes=[("carry", ci)])
                    P.op("dve", lambda e, vb=vb, accb=accb, cw=cw: e.tensor_scalar(
                        out=accb[:, 0:n], in0=vb[:, 2:2 + n], scalar1=vecs[:, cw + 2:cw + 3], scalar2=None, op0=ALU.mult),
                        reads=[("vb", r), "vecs"], writes=[("accb", r)])
                    P.op("dve", lambda e, vb=vb, accb=accb, cw=cw: e.scalar_tensor_tensor(
                        out=accb[:, 0:n], in0=vb[:, 1:1 + n], scalar=vecs[:, cw + 1:cw + 2], in1=accb[:, 0:n],
                        op0=ALU.mult, op1=ALU.add),
                        reads=[("vb", r), ("accb", r), "vecs"], writes=[("accb", r)])
                    P.op("dve", lambda e, vb=vb, accb=accb, cw=cw: e.scalar_tensor_tensor(
                        out=accb[:, 0:n], in0=vb[:, 0:n], scalar=vecs[:, cw:cw + 1], in1=accb[:, 0:n],
                        op0=ALU.mult, op1=ALU.add),
                        reads=[("vb", r), ("accb", r), "vecs"], writes=[("accb", r)])
                elif kind == "B":
                    P.op("dve", lambda e, b=b, j=j, accb=accb: e.tensor_tensor(
                        out=yb[:, j, 0:n], in0=ps[b][:, 0:n], in1=accb[:, 0:n], op=ALU.mult),
                        reads=[("ps", b), ("accb", r)], writes=[("y", j)])
                else:
                    P.op("act", lambda e, b=b, j=j: e.activation(out=qb[:, 6 + j, 0:n], in_=ps[b][:, 0:n], func=AF.Copy),
                         reads=[("ps", b)], writes=[("q", 6 + j)])

    def mem_attn(self, l, n):
        P, T = self.P, self.T
        ps, qb, yb, pm, rc = T["ps"], T["qb"], T["yb"], T["pm"], T["rc"]
        mkT, mv = T["mkT"], T["mv"]

        def scores(hm):
            i, half = hm // 2, hm % 2
            r = hm % 2
            lo = half * 64
            for mc in range(2):
                b = self.bank()
                P.op("pe", lambda e, b=b, mc=mc: e.matmul(
                    ps[b][:, 0:n], mkT[lo:lo + 64, l, i, mc * 128:(mc + 1) * 128], qb[lo:lo + 64, 6 + i, 0:n],
                    start=True, stop=True),
                    reads=["mk", ("q", 6 + i)], writes=[("ps", b)])
                P.op("act", lambda e, b=b, mc=mc: e.activation(out=pm[r][:, mc, 0:n], in_=ps[b][:, 0:n],
                                                               func=AF.Exp, scale=0.125),
                     reads=[("ps", b)], writes=[("pm", r, mc)])

        def pv(hm):
            i, half = hm // 2, hm % 2
            r = hm % 2
            lo = half * 64
            bo, bd = self.bank(), self.bank()
            for mc in range(2):
                P.op("pe", lambda e, mc=mc: e.matmul(ps[bo][:, 0:n], mv[:, l, mc, i * 128:(i + 1) * 128],
                                                     pm[r][:, mc, 0:n], start=(mc == 0), stop=(mc == 1)),
                     reads=["mv", ("pm", r, mc)], writes=[("ps", bo)])
            for mc in range(2):
                P.op("pe", lambda e, mc=mc: e.matmul(ps[bd][:, 0:n], T["ones1"][:, :], pm[r][:, mc, 0:n],
                                                     start=(mc == 0), stop=(mc == 1)),
                     reads=["ones1", ("pm", r, mc)], writes=[("ps", bd)])
            P.op("dve", lambda e: e.reciprocal(out=rc[r][lo:lo + 64, 0:n], in_=ps[bd][lo:lo + 64, 0:n]),
                 reads=[("ps", bd)], writes=[("rc", r)])
            P.op("dve", lambda e: e.tensor_tensor(out=yb[lo:lo + 64, 6 + i, 0:n], in0=ps[bo][lo:lo + 64, 0:n],
                                                  in1=rc[r][lo:lo + 64, 0:n], op=ALU.mult),
                 reads=[("ps", bo), ("rc", r)], writes=[("y", 6 + i)])

        for hm in range(5):
            if hm < 4:
                scores(hm)
            if hm >= 1:
                pv(hm - 1)

    def out_proj(self, prefix, n):
        T = self.T
        for p in range(2):
            s = self.wget(f"{prefix}_{p}")
            for jj in range(4):
                m = p * 4 + jj
                b = self.proj_chunk(s, jj * 128, 512, 8, T["yb"], "y", n)
                self.residual_add(b, m, n)

    def ffn(self, l, n):
        P, T = self.P, self.T
        ps, hb, actb = T["ps"], T["hb"], T["actb"]
        self.rmsnorm(n, G_FFN + 8 * l)
        self.dump(f"ffnh{l}", hb[:, :, 0:n], n)
        for fg in range(6):
            ncol = 512 if fg < 5 else 256
            sg = self.wget(f"g{l}_{fg}")
            su = self.wget(f"u{l}_{fg}")
            for jj in range(ncol // 128):
                f = fg * 4 + jj
                bg = self.proj_chunk(sg, jj * 128, ncol, 8, hb, "h", n)
                bu = self.proj_chunk(su, jj * 128, ncol, 8, hb, "h", n)
                r = f % 2
                usb = T["usb"][r]
                P.op("act", lambda e, bg=bg, usb=usb: e.activation(out=usb[:, 0:n], in_=ps[bg][:, 0:n], func=AF.Silu),
                     reads=[("ps", bg)], writes=[("usb", r)])
                P.op("dve", lambda e, bu=bu, usb=usb, f=f: e.tensor_tensor(
                    out=actb[:, f, 0:n], in0=ps[bu][:, 0:n], in1=usb[:, 0:n], op=ALU.mult),
                    reads=[("ps", bu), ("usb", r)], writes=[("act", f)])
        self.dump(f"act{l}", actb[:, 0:8, 0:n], n)
        for m in range(8):
            s = self.wget(f"d{l}_{m}")
            b = self.proj_chunk(s, 0, 128, NF, actb, "act", n)
            self.residual_add(b, m, n)
            self.dump(f"down{l}_{m}", T["xb"][:, :, 0:n], n)

    def kv_proj(self, n, c0, blk0):
        P, T = self.P, self.T
        ps, hb, kT, vS = T["ps"], T["hb"], T["kT"], T["vS"]
        self.rmsnorm(n, G_KV)
        s = self.wget("kv")
        sl = T["slots"][s]
        nvalid = n - c0
        nblk = nvalid // 128
        for pr in range(2):
            b = self.proj_chunk(s, pr * 128, 512, 8, hb, "h", n)
            P.op("act", lambda e, b=b, pr=pr: e.activation(
                out=kT[:, pr, blk0 * 128:blk0 * 128 + nvalid], in_=ps[b][:, c0:n], func=AF.Copy),
                reads=[("ps", b)], writes=[("kT", blk0 + t) for t in range(nblk)])
        for tb in range(nblk):
            b = self.bank()
            for k in range(8):
                P.op("pe", lambda e, b=b, k=k, tb=tb: e.matmul(
                    ps[b][:, 0:256], hb[:, k, c0 + tb * 128:c0 + (tb + 1) * 128], sl[:, k * 512 + 256:k * 512 + 512],
                    start=(k == 0), stop=(k == 7)),
                    reads=[("slot", s), ("h", k)], writes=[("ps", b)])
            P.op("act", lambda e, b=b, tb=tb: e.activation(out=vS[:, blk0 + tb, :], in_=ps[b][:, 0:256], func=AF.Copy),
                 reads=[("ps", b)], writes=[("vS", blk0 + tb)])

    def swa(self, lj, ti):
        P, T = self.P, self.T
        ps, qb, yb, pS, rc = T["ps"], T["qb"], T["yb"], T["pS"], T["rc"]
        kT, vS, bias8, biasF, ident = T["kT"], T["vS"], T["bias8"], T["biasF"], T["ident"]

        def scores(s):
            c, half = s // 2, s % 2
            lo = half * 64
            pair = c // 3
            r = s % 2
            for hbk in range(2):
                b = self.bank()
                if ti == 0 and hbk == 0:
                    P.op("pe", lambda e, b=b: e.matmul(ps[b][:, 0:256], ident[:, :], biasF[:, s, :],
                                                       start=True, stop=False, skip_group_check=True),
                         reads=["ident", "biasF"], writes=[("ps", b)])
                    P.op("pe", lambda e, b=b: e.matmul(ps[b][:, 256:512], ident[:, :], bias8[:, s, :],
                                                       start=False, stop=False, skip_group_check=True),
                         reads=["ident", "bias8"], writes=[("ps", b)])
                    sgc = True
                else:
                    P.op("pe", lambda e, b=b: e.matmul(
                        ps[b][:, :].rearrange("p (a n) -> p a n", a=2), ident[:, :],
                        bias8[:, s:s + 1, :].broadcast_to([128, 2, 256]),
                        start=True, stop=False),
                        reads=["ident", "bias8"], writes=[("ps", b)])
                    sgc = False
                for qq in range(2):
                    qi = 2 * hbk + qq
                    nblk = ti * 4 + qi
                    for kb in range(2):
                        blk = nblk + kb
                        last = (qq == 1 and kb == 1)
                        P.op("pe", lambda e, b=b, qq=qq, kb=kb, blk=blk, qi=qi, last=last, sgc=sgc: e.matmul(
                            ps[b][:, qq * 256 + kb * 128:qq * 256 + (kb + 1) * 128],
                            kT[lo:lo + 64, pair, blk * 128:(blk + 1) * 128],
                            qb[lo:lo + 64, c, qi * 128:(qi + 1) * 128],
                            start=False, stop=last, skip_group_check=sgc),
                            reads=[("kT", blk), ("q", c)], writes=[("ps", b)])
                P.op("act", lambda e, b=b, hbk=hbk: e.activation(out=pS[r][:, hbk * 512:(hbk + 1) * 512], in_=ps[b][:, :],
                                                                 func=AF.Exp, scale=0.125),
                     reads=[("ps", b)], writes=[("pS", r, hbk)])

        def pv(s):
            c, half = s // 2, s % 2
            lo = half * 64
            pair = c // 3
            r = s % 2
            bo, bd = self.bank(), self.bank()
            for qi in range(4):
                nblk = ti * 4 + qi
                for kb in range(2):
                    blk = nblk + kb
                    P.op("pe", lambda e, qi=qi, kb=kb, blk=blk: e.matmul(
                        ps[bo][:, qi * 128:(qi + 1) * 128], vS[:, blk, pair * 128:(pair + 1) * 128],
                        pS[r][:, qi * 256 + kb * 128:qi * 256 + (kb + 1) * 128],
                        start=(kb == 0), stop=(kb == 1)),
                        reads=[("vS", blk), ("pS", r, qi // 2)], writes=[("ps", bo)])
            for kb in range(2):
                P.op("pe", lambda e, kb=kb: e.matmul(
                    ps[bd][:, :].rearrange("p (a q) -> p a q", a=4), T["ones1"][:, :],
                    pS[r][:, :].rearrange("p (a b q) -> p a b q", a=4, b=2)[:, :, kb, :],
                    start=(kb == 0), stop=(kb == 1)),
                    reads=["ones1", ("pS", r, 0), ("pS", r, 1)], writes=[("ps", bd)])
            es = T["esink"][lo:lo + 64, lj * 12 + s:lj * 12 + s + 1]
            P.op("dve", lambda e: e.tensor_scalar(out=rc[r][lo:lo + 64, :], in0=ps[bd][lo:lo + 64, :], scalar1=es,
                                                  scalar2=None, op0=ALU.add),
                 reads=[("ps", bd), "esink"], writes=[("rc", r)])
            P.op("dve", lambda e: e.reciprocal(out=rc[r][lo:lo + 64, :], in_=rc[r][lo:lo + 64, :]),
                 reads=[("rc", r)], writes=[("rc", r)])
            P.op("dve", lambda e: e.tensor_tensor(out=yb[lo:lo + 64, c, :], in0=ps[bo][lo:lo + 64, :],
                                                  in1=rc[r][lo:lo + 64, :], op=ALU.mult),
                 reads=[("ps", bo), ("rc", r)], writes=[("y", c)])

        for s in range(13):
            if s < 12:
                scores(s)
            if s >= 1:
                pv(s - 1)

    def b_layer(self, lj, ti):
        P, T = self.P, self.T
        l = 2 + lj
        n = TT
        ps, hb, qb = T["ps"], T["hb"], T["qb"]
        self.rmsnorm(n, G_MIX + 8 * l)
        for p in range(2):
            s = self.wget(f"bq{lj}_{p}")
            for jj in range(4):
                c = p * 4 + jj
                b = self.proj_chunk(s, jj * 128, 512, 8, hb, "h", n)
                P.op("act", lambda e, b=b, c=c: e.activation(out=qb[:, c, 0:n], in_=ps[b][:, 0:n], func=AF.Copy),
                     reads=[("ps", b)], writes=[("q", c)])
        self.swa(lj, ti)
        self.mem_attn(l, n)
        self.out_proj(f"bout{lj}", n)
        self.ffn(l, n)

    def tile_body(self, ti, halo, n):
        T = self.T
        na = min(self.nlayers, 2)
        for l in range(na):
            self.a_mixer(l, n, halo)
            self.dump(f"amix{l}", T["yb"][:, :, 0:n], n)
            self.mem_attn(l, n)
            self.dump(f"y{l}", T["yb"][:, :, 0:n], n)
            self.out_proj(f"aout{l}", n)
            self.dump(f"xmix{l}", T["xb"][:, :, 0:n], n)
            self.ffn(l, n)
        if self.nlayers > 2:
            if halo:
                self.kv_proj(n, HALO - 128, 0)
            else:
                self.kv_proj(n, 0, 1 + ti * 4)
        if halo:
            return
        for lj in range(self.nlayers - 2):
            self.b_layer(lj, ti)
        if self.final:
            self.rmsnorm(n, G_FIN, to_x=True)

    def tile(self, ti):
        P, T = self.P, self.T
        halo = ti < 0
        n = HALO if halo else TT
        t0 = 0 if halo else HALO + ti * TT
        xsrc = T["xT"].rearrange("(c p) t -> p c t", p=128)[:, :, t0:t0 + n]
        P.dma("act", "xl", lambda e: e.dma_start(out=T["xb"][:, :, 0:n], in_=xsrc),
              reads=[], writes=[("x", c) for c in range(8)])
        try:
            self.tile_body(ti, halo, n)
        except StopTile:
            pass
        if halo:
            return
        ydst = T["yT"].rearrange("(c p) t -> p c t", p=128)[:, :, ti * TT:(ti + 1) * TT]
        P.dma("act", "st", lambda e: e.dma_start(out=ydst, in_=T["xb"][:, :, 0:n]),
              reads=[("x", c) for c in range(8)], writes=["yT"])


def build_nc(nlayers=4, final=True, ntiles=NT, debug=None):
    nc = bass.Bass("TRN2", target_bir_lowering=False)
    table, total, order = piece_table()
    T = {}
    T["xT"] = nc.dram_tensor("xT", [D, HALO + TOK], F32, kind="ExternalInput").ap()
    T["memT"] = nc.dram_tensor("memT", [D, N_MEM], F32, kind="ExternalInput").ap()
    T["vecs_d"] = nc.dram_tensor("vecs", [128, NV], F32, kind="ExternalInput").ap()
    T["bias_d"] = nc.dram_tensor("biasT", [128, 3072], F32, kind="ExternalInput").ap()
    T["biasf_d"] = nc.dram_tensor("biasF", [128, 3072], F32, kind="ExternalInput").ap()
    T["ident_d"] = nc.dram_tensor("ident", [128, 128], F32, kind="ExternalInput").ap()
    T["wflat"] = nc.dram_tensor("wflat", [total], F32, kind="ExternalInput").ap()
    T["wscr"] = nc.dram_tensor("wscr", [total], BF16, kind="Internal").ap()
    T["yT"] = nc.dram_tensor("yT", [D, TOK], F32, kind="ExternalOutput").ap()

    stack = ExitStack()
    with stack:
        def sb(name, shape, dt):
            return stack.enter_context(nc.sbuf_tensor(name, shape, dt))

        T["xb"] = sb("xb", [128, 8, TT], F32)
        T["hb"] = sb("hb", [128, 8, TT], BF16)
        T["qb"] = sb("qb", [128, 8, TT], BF16)
        T["yb"] = sb("yb", [128, 8, TT], BF16)
        T["actb"] = sb("actb", [128, NF, TT], BF16)
        T["usb"] = [sb(f"usb{i}", [128, TT], F32) for i in range(2)]
        T["vb"] = [sb(f"vb{i}", [128, TT + 2], F32) for i in range(2)]
        T["accb"] = [sb(f"accb{i}", [128, TT], F32) for i in range(2)]
        T["rstd"] = sb("rstd", [128, TT], F32)
        T["pS"] = [sb(f"pS{i}", [128, 1024], BF16) for i in range(2)]
        T["pm"] = [sb(f"pm{i}", [128, 2, TT], BF16) for i in range(2)]
        T["rc"] = [sb(f"rc{i}", [128, TT], F32) for i in range(2)]
        T["kT"] = sb("kT", [128, 2, NBLK * 128], BF16)
        T["vS"] = sb("vS", [128, NBLK, 256], BF16)
        T["bias8"] = sb("bias8_sb", [128, 12, 256], BF16)
        T["biasF"] = sb("biasF_sb", [128, 12, 256], BF16)
        T["mkT"] = sb("mkT", [128, 4, 2, N_MEM], BF16)
        T["mv"] = sb("mv", [128, 4, 2, 256], BF16)
        T["vecs"] = sb("vecs_sb", [128, NV], F32)
        T["esink"] = sb("esink", [128, 24], F32)
        T["carry"] = sb("carry", [128, 12, 2], F32)
        T["onesD"] = sb("onesD", [128, 128], BF16)
        T["ones1"] = sb("ones1", [128, 128], BF16)
        T["ident"] = sb("ident_sb", [128, 128], BF16)
        T["epsc"] = sb("epsc", [128, 1], F32)
        T["slots"] = [sb(f"slot{i}", [128, SLOT_ELEMS], BF16) for i in range(NSLOT)]
        T["ps"] = [stack.enter_context(nc.psum_tensor(f"ps{i}", [128, 512], F32)) for i in range(8)]

        bld = Builder(nc, T, nlayers=nlayers, final=final, ntiles=ntiles)
        bld.debug = debug
        seq = bld.run(NullProg(), recording=True)
        P = Prog()
        P.op("dve", lambda e: e.memset(T["epsc"][:, :], EPS), writes=["epsc"])
        bld.run(P, recording=False, seq=seq)
        P.emit(nc, stack)
    return nc


def make_in_maps(inp):
    x = np.asarray(inp["x"], np.float32)
    mem = np.asarray(inp["mem"], np.float32)
    wflat = pack_weights(inp)
    biasT, biasFirst = make_bias_tables(inp["rel_bias"])
    ident = np.eye(128, dtype=np.float32)
    in_maps = []
    for c in range(NCORE):
        b, qtr = c // 4, c % 4
        xt = np.zeros((D, HALO + TOK), np.float32)
        lo = qtr * TOK
        if qtr > 0:
            xt[:, :] = x[b, lo - HALO:lo + TOK, :].T
        else:
            xt[:, HALO:] = x[b, lo:lo + TOK, :].T
        in_maps.append({
            "xT": xt,
            "memT": np.ascontiguousarray(mem[b].T),
            "vecs": make_vecs(inp, 0.0 if qtr == 0 else 1.0),
            "biasT": biasT,
            "biasF": biasFirst if qtr == 0 else biasT,
            "ident": ident,
            "wflat": wflat,
        })
    return in_maps


_NC_CACHE = {}


def kernel(**inputs):
    in_maps = make_in_maps(inputs)
    if "nc" not in _NC_CACHE:
        _NC_CACHE["nc"] = build_nc()
    nc = _NC_CACHE["nc"]
    res = run_bass_kernel_spmd(nc, in_maps, core_ids=list(range(NCORE)))
    out = np.empty((BATCH, SEQ, D), np.float32)
    for c in range(NCORE):
        b, qtr = c // 4, c % 4
        out[b, qtr * TOK:(qtr + 1) * TOK, :] = res.results[c]["yT"].T
    return out
```

```python
import math
from contextlib import ExitStack

import numpy as np
import concourse.bass as bass
import concourse.mybir as mybir
from concourse.bass_utils import run_bass_kernel_spmd

F32 = mybir.dt.float32
BF16 = mybir.dt.bfloat16
AF = mybir.ActivationFunctionType
ALU = mybir.AluOpType

D = 1024
NCH = 8
SEQ = 16384
BATCH = 2
NCORE = 8
TOK = 4096
HALO = 136
TT = 512
NT = TOK // TT
DFF = 2816
NF = DFF // 128
N_MEM = 256
NBLK = TOK // 128 + 1
EPS = 1e-5
NEG = -30000.0
NSLOT = 5
SLOT_ELEMS = 4096
STG = 2048

HEAD_LO = [0, 1, 2, 6, 7, 8]
HEAD_HI = [3, 4, 5, 9, 10, 11]
SLOT_HEAD = [HEAD_LO[s // 2] if s % 2 == 0 else HEAD_HI[s // 2] for s in range(12)]

G_MIX, G_FFN, G_KV, G_MEM, G_FIN = 0, 32, 64, 72, 80
V_CONV = 88
V_FLAG = 124
V_SINK = 125
NV = 152


def piece_table():
    names = []
    for i in range(4):
        names.append((f"memkv{i}", 4096))
    for l in range(2):
        for p in range(5):
            names.append((f"ain{l}_{p}", 4096))
        for p in range(2):
            names.append((f"aout{l}_{p}", 4096))
        for fg in range(6):
            n = 4096 if fg < 5 else 2048
            names.append((f"g{l}_{fg}", n))
            names.append((f"u{l}_{fg}", n))
        for m in range(8):
            names.append((f"d{l}_{m}", NF * 128))
    names.append(("kv", 4096))
    for j in range(2):
        l = 2 + j
        for p in range(2):
            names.append((f"bq{j}_{p}", 4096))
        for p in range(2):
            names.append((f"bout{j}_{p}", 4096))
        for fg in range(6):
            n = 4096 if fg < 5 else 2048
            names.append((f"g{l}_{fg}", n))
            names.append((f"u{l}_{fg}", n))
        for m in range(8):
            names.append((f"d{l}_{m}", NF * 128))
    table = {}
    off = 0
    for nm, n in names:
        table[nm] = (off, n)
        off += 128 * n
    return table, off, [nm for nm, _ in names]


def _as_piece(w2d):
    K, Fc = w2d.shape
    kc = K // 128
    return np.ascontiguousarray(w2d.reshape(kc, 128, Fc).transpose(1, 0, 2)).reshape(128, kc * Fc)


def pack_weights(inp):
    table, total, _ = piece_table()
    wflat = np.empty((total,), np.float32)

    def put(name, w2d):
        off, n = table[name]
        blk = _as_piece(np.asarray(w2d, np.float32))
        assert blk.shape == (128, n), (name, blk.shape, n)
        wflat[off:off + 128 * n] = blk.reshape(-1)

    for i in range(4):
        put(f"memkv{i}", inp["w_mem_kv"][i])
    a_chunks = []
    for j in range(6):
        a_chunks += [j, 12 + j, 6 + j]
    a_chunks += [18, 19]
    a_cols = np.concatenate([np.arange(c * 128, (c + 1) * 128) for c in a_chunks])
    qperm = np.concatenate([np.arange(h * 64, (h + 1) * 64) for h in SLOT_HEAD] + [np.arange(768, 1024)])
    for l in range(4):
        for fg in range(6):
            lo, hi = fg * 512, min((fg + 1) * 512, DFF)
            put(f"g{l}_{fg}", inp["w_gate"][l][:, lo:hi])
            put(f"u{l}_{fg}", inp["w_up"][l][:, lo:hi])
        for m in range(8):
            put(f"d{l}_{m}", inp["w_down"][l][:, m * 128:(m + 1) * 128])
    for l in range(2):
        wp = np.asarray(inp["a_w_in"][l])[:, a_cols]
        for p in range(5):
            put(f"ain{l}_{p}", wp[:, p * 512:(p + 1) * 512])
        for p in range(2):
            put(f"aout{l}_{p}", inp["a_w_out"][l][:, p * 512:(p + 1) * 512])
    put("kv", inp["w_kv"])
    for j in range(2):
        wq = np.asarray(inp["b_w_q"][j])[:, qperm]
        wo = np.asarray(inp["b_w_out"][j])[qperm, :]
        for p in range(2):
            put(f"bq{j}_{p}", wq[:, p * 512:(p + 1) * 512])
            put(f"bout{j}_{p}", wo[:, p * 512:(p + 1) * 512])
    return wflat


def _rel_bucket_np(dist):
    max_exact = 16
    d = np.maximum(dist, 1).astype(np.float32)
    large = max_exact + (np.log(d / max_exact) / math.log(128 / max_exact) * (32 - max_exact)).astype(np.int32)
    large = np.minimum(large, 31)
    return np.where(dist < max_exact, dist, large)


def make_bias_tables(rel_bias):
    rel_bias = np.asarray(rel_bias, np.float32)
    kj = np.arange(128)[:, None, None]
    kb = np.arange(2)[None, :, None]
    qi = np.arange(128)[None, None, :]
    dist = qi + 128 - (kb * 128 + kj)
    inwin = (dist >= 0) & (dist < 128)
    bucket = _rel_bucket_np(np.maximum(dist, 0))
    fill = np.float32(NEG / 8.0)
    out = np.empty((128, 12, 2, 128), np.float32)
    for s in range(12):
        g = rel_bias[bucket, SLOT_HEAD[s]]
        out[:, s] = np.where(inwin, g, fill)
    first = out.copy()
    first[:, :, 0, :] = fill
    return out.reshape(128, 12 * 256), first.reshape(128, 12 * 256)


def make_vecs(inp, flag):
    v = np.zeros((128, NV), np.float32)

    def putg(col, g):
        v[:, col:col + 8] = np.asarray(g, np.float32).reshape(8, 128).T

    for l in range(4):
        putg(G_MIX + 8 * l, inp["norm_mix"][l])
        putg(G_FFN + 8 * l, inp["norm_ffn"][l])
    putg(G_KV, inp["kv_norm"])
    putg(G_MEM, inp["mem_norm"])
    putg(G_FIN, inp["final_norm"])
    cw = np.asarray(inp["a_conv_w"], np.float32)
    for l in range(2):
        for j in range(6):
            for t in range(3):
                v[:, V_CONV + (l * 6 + j) * 3 + t] = cw[l, t, j * 128:(j + 1) * 128]
    v[:, V_FLAG] = flag
    sk = np.asarray(inp["b_sinks"], np.float32)
    for lj in range(2):
        for s in range(12):
            v[:, V_SINK + lj * 12 + s] = sk[lj, SLOT_HEAD[s]]
    return v


class Prog:
    ENGS = ("pe", "act", "dve", "pool", "sp")
    EPOCH = 24000

    def __init__(self):
        self.ops = {e: [] for e in self.ENGS}
        self.cnt = {e: 0 for e in self.ENGS}
        self.scnt = {}
        self.lastw = {}
        self.readers = {}

    def _deps(self, reads, writes):
        deps = {}

        def add(tok):
            key = (tok[0], tok[1])
            if deps.get(key, -1) < tok[2]:
                deps[key] = tok[2]

        for r in reads:
            t = self.lastw.get(r)
            if t is not None:
                add(t)
        for w in writes:
            t = self.lastw.get(w)
            if t is not None:
                add(t)
            for key, idx in self.readers.get(w, {}).items():
                add((key[0], key[1], idx))
        return deps

    def _update(self, tok, reads, writes):
        key = (tok[0], tok[1])
        for r in reads:
            d = self.readers.setdefault(r, {})
            if d.get(key, -1) < tok[2]:
                d[key] = tok[2]
        for w in writes:
            self.lastw[w] = tok
            self.readers[w] = {}

    def op(self, eng, fn, reads=(), writes=()):
        deps = self._deps(reads, writes)
        idx = self.cnt[eng]
        self.cnt[eng] += 1
        tok = ("e", eng, idx)
        self.ops[eng].append((fn, deps, tok))
        self._update(tok, reads, writes)

    def dma(self, eng, stream, fn, reads=(), writes=()):
        deps = self._deps(reads, writes)
        idx = self.scnt.get(stream, 0)
        self.scnt[stream] = idx + 1
        if idx > 0:
            key = ("s", stream)
            if deps.get(key, -1) < idx - 1:
                deps[key] = idx - 1
        tok = ("s", stream, idx)
        self.ops[eng].append((fn, deps, tok))
        self._update(tok, reads, writes)

    def wait_all(self, eng, reads):
        deps = self._deps(reads, ())
        self.ops[eng].append((None, deps, None))

    def emit(self, nc, stack):
        n_ep = {e: max(1, -(-self.cnt[e] // self.EPOCH)) for e in self.ENGS}
        esem = {e: [stack.enter_context(nc.semaphore(f"e_{e}_{i}")) for i in range(n_ep[e])]
                for e in self.ENGS if self.cnt[e] > 0}
        ssem = {s: stack.enter_context(nc.semaphore(f"s_{s}")) for s in self.scnt}
        block = stack.enter_context(nc.Block())
        prog = self

        def run(engname, eng):
            known = {}
            for fn, deps, tok in prog.ops[engname]:
                for key, idx in deps.items():
                    if key[0] == "e" and key[1] == "pe" and engname == "pe":
                        continue
                    if known.get(key, -1) >= idx:
                        continue
                    known[key] = idx
                    if key[0] == "e":
                        eng.wait_ge(esem[key[1]][idx // prog.EPOCH], idx % prog.EPOCH + 1)
                    else:
                        eng.wait_ge(ssem[key[1]], 16 * (idx + 1))
                if fn is None:
                    continue
                ins = fn(eng)
                if tok[0] == "e":
                    ins.then_inc(esem[engname][tok[2] // prog.EPOCH], 1)
                else:
                    ins.then_inc(ssem[tok[1]], 16)

        @block.tensor
        def _(e):
            run("pe", e)

        @block.scalar
        def _(e):
            run("act", e)

        @block.vector
        def _(e):
            run("dve", e)

        @block.gpsimd
        def _(e):
            run("pool", e)

        @block.sync
        def _(e):
            run("sp", e)


class StopTile(Exception):
    pass


class NullProg:
    def op(self, *a, **k):
        pass

    def dma(self, *a, **k):
        pass

    def wait_all(self, *a, **k):
        pass


class Builder:
    def __init__(self, nc, T, nlayers=4, final=True, ntiles=NT):
        self.nc = nc
        self.T = T
        self.nlayers = nlayers
        self.final = final
        self.ntiles = ntiles
        self.table, self.total, self.order = piece_table()

    debug = None

    def dump(self, name, src, n, nch=8):
        if self.debug != name:
            return
        P, T = self.P, self.T
        P.op("act", lambda e: e.activation(out=T["xb"][:, 0:nch, 0:n], in_=src, func=AF.Copy),
             reads=[("x", c) for c in range(8)] + [("h", c) for c in range(8)] + [("y", c) for c in range(8)]
             + [("q", c) for c in range(8)] + [("act", c) for c in range(NF)],
             writes=[("x", c) for c in range(8)])
        raise StopTile()

    def bank(self):
        b = self._bank
        self._bank = (b + 1) % 8
        return b

    def wget(self, name):
        if self.recording:
            self.seq.append(name)
            return 0
        i = self.wpos
        assert self.seq[i] == name, (i, self.seq[i], name)
        P, Tn = self.P, self.T
        hi = min(i + NSLOT - 1, len(self.seq))
        while self.wloaded < hi:
            j = self.wloaded
            s = j % NSLOT
            nm = self.seq[j]
            off, n = self.table[nm]
            slot = Tn["slots"][s]
            if nm in self.converted:
                src = Tn["wscr"][off:off + 128 * n].rearrange("(p n) -> p n", p=128)
                P.dma("sp", f"ld{s}", lambda e, dst=slot[:, 0:n], src=src: e.dma_start(out=dst, in_=src),
                      reads=[("wscr", nm)], writes=[("slot", s)])
            else:
                self.converted.add(nm)
                src32 = Tn["wflat"][off:off + 128 * n].rearrange("(p n) -> p n", p=128)
                for c0 in range(0, n, STG):
                    c1 = min(c0 + STG, n)
                    k = self.stgk
                    self.stgk = (k + 1) % 2
                    stg = Tn["stg"][k]
                    P.dma("sp", f"sg{k}", lambda e, stg=stg, src32=src32, c0=c0, c1=c1: e.dma_start(
                        out=stg[:, 0:c1 - c0], in_=src32[:, c0:c1]),
                        reads=[], writes=[("stg", k)])
                    P.op("act", lambda e, stg=stg, slot=slot, c0=c0, c1=c1: e.activation(
                        out=slot[:, c0:c1], in_=stg[:, 0:c1 - c0], func=AF.Copy),
                        reads=[("stg", k)], writes=[("slot", s)])
                dst = Tn["wscr"][off:off + 128 * n].rearrange("(p n) -> p n", p=128)
                P.dma("act", f"ws{self.wsk}", lambda e, dst=dst, slot=slot, n=n: e.dma_start(out=dst, in_=slot[:, 0:n]),
                      reads=[("slot", s)], writes=[("wscr", nm)])
                self.wsk = (self.wsk + 1) % 2
            self.wloaded += 1
        self.wpos += 1
        return i % NSLOT

    def run(self, P, recording, seq=None):
        self.P = P
        self.recording = recording
        self.seq = [] if recording else seq
        self.wpos = 0
        self.wloaded = 0
        self._bank = 0
        self.converted = set()
        self.stgk = 0
        self.wsk = 0
        self.setup()
        self.tile(-1)
        for ti in range(self.ntiles):
            self.tile(ti)
        P.wait_all("act", [("x", c) for c in range(8)] + ["yT"])
        return self.seq

    def setup(self):
        P, T, nc = self.P, self.T, self.nc
        P.op("dve", lambda e: e.memset(T["onesD"][:, :], 1.0 / D), writes=["onesD"])
        P.op("dve", lambda e: e.memset(T["ones1"][:, :], 1.0), writes=["ones1"])
        P.op("dve", lambda e: e.memset(T["carry"][:, :, :], 0.0), writes=[("carry", i) for i in range(12)])
        P.dma("act", "misc", lambda e: e.dma_start(out=T["vecs"][:, :], in_=T["vecs_d"][:, :]), writes=["vecs"])
        xflat = T["xb"][:, :, :].rearrange("p c t -> p (c t)")
        P.dma("act", "misc", lambda e: e.dma_start(out=xflat[:, 0:128], in_=T["ident_d"][:, :]),
              writes=[("x", c) for c in range(8)])
        P.op("act", lambda e: e.activation(out=T["ident"][:, :], in_=xflat[:, 0:128], func=AF.Copy),
             reads=[("x", c) for c in range(8)], writes=["ident"])
        for src_name, dst_name in (("bias_d", "bias8"), ("biasf_d", "biasF")):
            P.dma("act", "misc", lambda e, s=src_name: e.dma_start(out=xflat[:, 0:3072], in_=T[s][:, :]),
                  writes=[("x", c) for c in range(8)])
            dstf = T[dst_name][:, :, :].rearrange("p s n -> p (s n)")
            P.op("act", lambda e, dstf=dstf: e.activation(out=dstf, in_=xflat[:, 0:3072], func=AF.Copy, scale=8.0),
                 reads=[("x", c) for c in range(8)], writes=[dst_name])
        P.op("act", lambda e: e.activation(out=T["esink"][:, :], in_=T["vecs"][:, V_SINK:V_SINK + 24], func=AF.Exp),
             reads=["vecs"], writes=["esink"])
        P.dma("act", "misc",
              lambda e: e.dma_start(out=T["xb"][:, :, 0:N_MEM],
                                    in_=T["memT"].rearrange("(c p) t -> p c t", p=128)),
              writes=[("x", c) for c in range(8)])
        self.rmsnorm(N_MEM, G_MEM)
        hb, ps = T["hb"], T["ps"]
        for l in range(4):
            s = self.wget(f"memkv{l}")
            sl = T["slots"][s]
            for i in range(2):
                b = self.bank()
                for k in range(8):
                    P.op("pe", lambda e, b=b, k=k, i=i, sl=sl: e.matmul(
                        ps[b][:, 0:N_MEM], sl[:, k * 512 + i * 128:k * 512 + (i + 1) * 128], hb[:, k, 0:N_MEM],
                        start=(k == 0), stop=(k == 7)),
                        reads=[("slot", s), ("h", k)], writes=[("ps", b)])
                P.op("act", lambda e, b=b, l=l, i=i: e.activation(out=T["mkT"][:, l, i, :], in_=ps[b][:, 0:N_MEM], func=AF.Copy),
                     reads=[("ps", b)], writes=["mk"])
            for mc in range(2):
                b = self.bank()
                for k in range(8):
                    P.op("pe", lambda e, b=b, k=k, mc=mc, sl=sl: e.matmul(
                        ps[b][:, 0:256], hb[:, k, mc * 128:(mc + 1) * 128], sl[:, k * 512 + 256:k * 512 + 512],
                        start=(k == 0), stop=(k == 7)),
                        reads=[("slot", s), ("h", k)], writes=[("ps", b)])
                P.op("act", lambda e, b=b, l=l, mc=mc: e.activation(out=T["mv"][:, l, mc, :], in_=ps[b][:, 0:256], func=AF.Copy),
                     reads=[("ps", b)], writes=["mv"])

    def rmsnorm(self, n, gcol, to_x=False):
        P, T = self.P, self.T
        xb, hb, actb, ps, rstd, vecs = T["xb"], T["hb"], T["actb"], T["ps"], T["rstd"], T["vecs"]
        P.op("act", lambda e: e.activation(out=actb[:, 0:8, 0:n], in_=xb[:, :, 0:n], func=AF.Square),
             reads=[("x", c) for c in range(8)], writes=[("act", c) for c in range(8)])
        b = self.bank()
        for c in range(8):
            P.op("pe", lambda e, c=c: e.matmul(ps[b][:, 0:n], T["onesD"][:, :], actb[:, c, 0:n],
                                               start=(c == 0), stop=(c == 7)),
                 reads=[("act", c), "onesD"], writes=[("ps", b)])
        P.op("act", lambda e: e.activation(out=rstd[:, 0:n], in_=ps[b][:, 0:n], func=AF.Ln, bias=T["epsc"][:, 0:1]),
             reads=[("ps", b), "epsc"], writes=["rstd"])
        P.op("act", lambda e: e.activation(out=rstd[:, 0:n], in_=rstd[:, 0:n], func=AF.Exp, scale=-0.5),
             reads=["rstd"], writes=["rstd"])
        for c in range(8):
            if to_x:
                P.op("dve", lambda e, c=c: e.scalar_tensor_tensor(
                    out=xb[:, c, 0:n], in0=xb[:, c, 0:n], scalar=vecs[:, gcol + c:gcol + c + 1], in1=rstd[:, 0:n],
                    op0=ALU.mult, op1=ALU.mult),
                    reads=[("x", c), "rstd", "vecs"], writes=[("x", c)])
            else:
                P.op("dve", lambda e, c=c: e.scalar_tensor_tensor(
                    out=hb[:, c, 0:n], in0=xb[:, c, 0:n], scalar=vecs[:, gcol + c:gcol + c + 1], in1=rstd[:, 0:n],
                    op0=ALU.mult, op1=ALU.mult),
                    reads=[("x", c), "rstd", "vecs"], writes=[("h", c)])

    def proj_chunk(self, s, col0, kstride, nk, rhs_t, rhs_res, n):
        P, T = self.P, self.T
        ps, sl = T["ps"], T["slots"][s]
        b = self.bank()
        for k in range(nk):
            P.op("pe", lambda e, k=k: e.matmul(ps[b][:, 0:n], sl[:, k * kstride + col0:k * kstride + col0 + 128],
                                               rhs_t[:, k, 0:n], start=(k == 0), stop=(k == nk - 1)),
                 reads=[("slot", s), (rhs_res, k)], writes=[("ps", b)])
        return b

    def residual_add(self, b, m, n):
        P, T = self.P, self.T
        xb, ps = T["xb"], T["ps"]
        P.op("dve", lambda e: e.tensor_tensor(out=xb[:, m, 0:n], in0=ps[b][:, 0:n], in1=xb[:, m, 0:n], op=ALU.add),
             reads=[("ps", b), ("x", m)], writes=[("x", m)])

    def a_mixer(self, l, n, halo):
        P, T = self.P, self.T
        ps, hb, qb, yb, vecs = T["ps"], T["hb"], T["qb"], T["yb"], T["vecs"]
        order = []
        for j in range(6):
            order += [("u", j), ("C", j), ("B", j)]
        order += [("q", 0), ("q", 1)]
        self.rmsnorm(n, G_MIX + 8 * l)
        self.dump(f"h{l}", hb[:, :, 0:n], n)
        for p in range(5):
            s = self.wget(f"ain{l}_{p}")
            for jj in range(4):
                kind, j = order[p * 4 + jj]
                b = self.proj_chunk(s, jj * 128, 512, 8, hb, "h", n)
                r = j % 2
                usb, vb, accb = T["usb"][r], T["vb"][r], T["accb"][r]
                if kind == "u":
                    P.op("act", lambda e, b=b, usb=usb: e.activation(out=usb[:, 0:n], in_=ps[b][:, 0:n], func=AF.Copy),
                         reads=[("ps", b)], writes=[("usb", r)])
                elif kind == "C":
                    ci = l * 6 + j
                    cw = V_CONV + ci * 3
                    P.op("dve", lambda e, vb=vb, ci=ci: e.tensor_copy(out=vb[:, 0:2], in_=T["carry"][:, ci, :]),
                         reads=[("carry", ci)], writes=[("vb", r)])
                    P.op("dve", lambda e, b=b, vb=vb, usb=usb: e.tensor_tensor(
                        out=vb[:, 2:2 + n], in0=ps[b][:, 0:n], in1=usb[:, 0:n], op=ALU.mult),
                        reads=[("ps", b), ("usb", r)], writes=[("vb", r)])
                    if halo:
                        P.op("dve", lambda e, vb=vb, ci=ci: e.tensor_scalar(
                            out=T["carry"][:, ci, :], in0=vb[:, n:n + 2], scalar1=vecs[:, V_FLAG:V_FLAG + 1],
                            scalar2=None, op0=ALU.mult),
                            reads=[("vb", r), "vecs"], writes=[("carry", ci)])
                    else:
                        P.op("dve", lambda e, vb=vb, ci=ci: e.tensor_copy(out=T["carry"][:, ci, :], in_=vb[:, n:n + 2]),
                             reads=[("vb", r)], writes=[("carry", ci)])
                    P.op("dve", lambda e, vb=vb, accb=accb, cw=cw: e.tensor_scalar(
                        out=accb[:, 0:n], in0=vb[:, 2:2 + n], scalar1=vecs[:, cw + 2:cw + 3], scalar2=None, op0=ALU.mult),
                        reads=[("vb", r), "vecs"], writes=[("accb", r)])
                    P.op("dve", lambda e, vb=vb, accb=accb, cw=cw: e.scalar_tensor_tensor(
                        out=accb[:, 0:n], in0=vb[:, 1:1 + n], scalar=vecs[:, cw + 1:cw + 2], in1=accb[:, 0:n],
                        op0=ALU.mult, op1=ALU.add),
                        reads=[("vb", r), ("accb", r), "vecs"], writes=[("accb", r)])
                    P.op("dve", lambda e, vb=vb, accb=accb, cw=cw: e.scalar_tensor_tensor(
                        out=accb[:, 0:n], in0=vb[:, 0:n], scalar=vecs[:, cw:cw + 1], in1=accb[:, 0:n],
                        op0=ALU.mult, op1=ALU.add),
                        reads=[("vb", r), ("accb", r), "vecs"], writes=[("accb", r)])
                elif kind == "B":
                    P.op("dve", lambda e, b=b, j=j, accb=accb: e.tensor_tensor(
                        out=yb[:, j, 0:n], in0=ps[b][:, 0:n], in1=accb[:, 0:n], op=ALU.mult),
                        reads=[("ps", b), ("accb", r)], writes=[("y", j)])
                else:
                    P.op("act", lambda e, b=b, j=j: e.activation(out=qb[:, 6 + j, 0:n], in_=ps[b][:, 0:n], func=AF.Copy),
                         reads=[("ps", b)], writes=[("q", 6 + j)])

    def mem_attn(self, l, n):
        P, T = self.P, self.T
        ps, qb, yb, pm, rc = T["ps"], T["qb"], T["yb"], T["pm"], T["rc"]
        mkT, mv = T["mkT"], T["mv"]

        def scores(hm):
            i, half = hm // 2, hm % 2
            r = hm % 2
            lo = half * 64
            for mc in range(2):
                b = self.bank()
                P.op("pe", lambda e, b=b, mc=mc: e.matmul(
                    ps[b][:, 0:n], mkT[lo:lo + 64, l, i, mc * 128:(mc + 1) * 128], qb[lo:lo + 64, 6 + i, 0:n],
                    start=True, stop=True),
                    reads=["mk", ("q", 6 + i)], writes=[("ps", b)])
                P.op("act", lambda e, b=b, mc=mc: e.activation(out=pm[r][:, mc, 0:n], in_=ps[b][:, 0:n],
                                                               func=AF.Exp, scale=0.125),
                     reads=[("ps", b)], writes=[("pm", r, mc)])

        def pv(hm):
            i, half = hm // 2, hm % 2
            r = hm % 2
            lo = half * 64
            bo, bd = self.bank(), self.bank()
            for mc in range(2):
                P.op("pe", lambda e, mc=mc: e.matmul(ps[bo][:, 0:n], mv[:, l, mc, i * 128:(i + 1) * 128],
                                                     pm[r][:, mc, 0:n], start=(mc == 0), stop=(mc == 1)),
                     reads=["mv", ("pm", r, mc)], writes=[("ps", bo)])
            for mc in range(2):
                P.op("pe", lambda e, mc=mc: e.matmul(ps[bd][:, 0:n], T["ones1"][:, :], pm[r][:, mc, 0:n],
                                                     start=(mc == 0), stop=(mc == 1)),
                     reads=["ones1", ("pm", r, mc)], writes=[("ps", bd)])
            P.op("act", lambda e: e.activation(out=rc[r][lo:lo + 64, 0:n], in_=ps[bd][lo:lo + 64, 0:n], func=AF.Ln),
                 reads=[("ps", bd)], writes=[("rc", r)])
            P.op("act", lambda e: e.activation(out=rc[r][lo:lo + 64, 0:n], in_=rc[r][lo:lo + 64, 0:n], func=AF.Exp,
                                               scale=-1.0),
                 reads=[("rc", r)], writes=[("rc", r)])
            P.op("dve", lambda e: e.tensor_tensor(out=yb[lo:lo + 64, 6 + i, 0:n], in0=ps[bo][lo:lo + 64, 0:n],
                                                  in1=rc[r][lo:lo + 64, 0:n], op=ALU.mult),
                 reads=[("ps", bo), ("rc", r)], writes=[("y", 6 + i)])

        for hm in range(5):
            if hm < 4:
                scores(hm)
            if hm >= 1:
                pv(hm - 1)

    def out_proj(self, prefix, n):
        T = self.T
        for p in range(2):
            s = self.wget(f"{prefix}_{p}")
            for jj in range(4):
                m = p * 4 + jj
                b = self.proj_chunk(s, jj * 128, 512, 8, T["yb"], "y", n)
                self.residual_add(b, m, n)

    def ffn(self, l, n):
        P, T = self.P, self.T
        ps, hb, actb = T["ps"], T["hb"], T["actb"]
        self.rmsnorm(n, G_FFN + 8 * l)
        self.dump(f"ffnh{l}", hb[:, :, 0:n], n)
        for fg in range(6):
            ncol = 512 if fg < 5 else 256
            sg = self.wget(f"g{l}_{fg}")
            su = self.wget(f"u{l}_{fg}")
            for jj in range(ncol // 128):
                f = fg * 4 + jj
                bg = self.proj_chunk(sg, jj * 128, ncol, 8, hb, "h", n)
                bu = self.proj_chunk(su, jj * 128, ncol, 8, hb, "h", n)
                r = f % 2
                usb = T["usb"][r]
                P.op("act", lambda e, bg=bg, usb=usb: e.activation(out=usb[:, 0:n], in_=ps[bg][:, 0:n], func=AF.Silu),
                     reads=[("ps", bg)], writes=[("usb", r)])
                P.op("dve", lambda e, bu=bu, usb=usb, f=f: e.tensor_tensor(
                    out=actb[:, f, 0:n], in0=ps[bu][:, 0:n], in1=usb[:, 0:n], op=ALU.mult),
                    reads=[("ps", bu), ("usb", r)], writes=[("act", f)])
        self.dump(f"act{l}", actb[:, 0:8, 0:n], n)
        for m in range(8):
            s = self.wget(f"d{l}_{m}")
            b = self.proj_chunk(s, 0, 128, NF, actb, "act", n)
            self.residual_add(b, m, n)
            self.dump(f"down{l}_{m}", T["xb"][:, :, 0:n], n)

    def kv_proj(self, n, c0, blk0):
        P, T = self.P, self.T
        ps, hb, kT, vS = T["ps"], T["hb"], T["kT"], T["vS"]
        self.rmsnorm(n, G_KV)
        s = self.wget("kv")
        sl = T["slots"][s]
        nvalid = n - c0
        nblk = nvalid // 128
        for pr in range(2):
            b = self.proj_chunk(s, pr * 128, 512, 8, hb, "h", n)
            P.op("act", lambda e, b=b, pr=pr: e.activation(
                out=kT[:, pr, blk0 * 128:blk0 * 128 + nvalid], in_=ps[b][:, c0:n], func=AF.Copy),
                reads=[("ps", b)], writes=[("kT", blk0 + t) for t in range(nblk)])
        for tb in range(nblk):
            b = self.bank()
            for k in range(8):
                P.op("pe", lambda e, b=b, k=k, tb=tb: e.matmul(
                    ps[b][:, 0:256], hb[:, k, c0 + tb * 128:c0 + (tb + 1) * 128], sl[:, k * 512 + 256:k * 512 + 512],
                    start=(k == 0), stop=(k == 7)),
                    reads=[("slot", s), ("h", k)], writes=[("ps", b)])
            P.op("act", lambda e, b=b, tb=tb: e.activation(out=vS[:, blk0 + tb, :], in_=ps[b][:, 0:256], func=AF.Copy),
                 reads=[("ps", b)], writes=[("vS", blk0 + tb)])

    def swa(self, lj, ti):
        P, T = self.P, self.T
        ps, qb, yb, pS, rc = T["ps"], T["qb"], T["yb"], T["pS"], T["rc"]
        kT, vS, bias8, biasF, ident = T["kT"], T["vS"], T["bias8"], T["biasF"], T["ident"]

        def scores(s):
            c, half = s // 2, s % 2
            lo = half * 64
            pair = c // 3
            r = s % 3
            for hbk in range(2):
                b = self.bank()
                if ti == 0 and hbk == 0:
                    P.op("pe", lambda e, b=b: e.matmul(ps[b][:, 0:256], ident[:, :], biasF[:, s, :],
                                                       start=True, stop=False, skip_group_check=True),
                         reads=["ident", "biasF"], writes=[("ps", b)])
                    P.op("pe", lambda e, b=b: e.matmul(ps[b][:, 256:512], ident[:, :], bias8[:, s, :],
                                                       start=False, stop=False, skip_group_check=True),
                         reads=["ident", "bias8"], writes=[("ps", b)])
                    sgc = True
                else:
                    P.op("pe", lambda e, b=b: e.matmul(
                        ps[b][:, :].rearrange("p (a n) -> p a n", a=2), ident[:, :],
                        bias8[:, s:s + 1, :].broadcast_to([128, 2, 256]),
                        start=True, stop=False),
                        reads=["ident", "bias8"], writes=[("ps", b)])
                    sgc = False
                for qq in range(2):
                    qi = 2 * hbk + qq
                    nblk = ti * 4 + qi
                    for kb in range(2):
                        blk = nblk + kb
                        last = (qq == 1 and kb == 1)
                        P.op("pe", lambda e, b=b, qq=qq, kb=kb, blk=blk, qi=qi, last=last, sgc=sgc: e.matmul(
                            ps[b][:, qq * 256 + kb * 128:qq * 256 + (kb + 1) * 128],
                            kT[lo:lo + 64, pair, blk * 128:(blk + 1) * 128],
                            qb[lo:lo + 64, c, qi * 128:(qi + 1) * 128],
                            start=False, stop=last, skip_group_check=sgc),
                            reads=[("kT", blk), ("q", c)], writes=[("ps", b)])
                P.op("act", lambda e, b=b, hbk=hbk: e.activation(out=pS[r][:, hbk * 512:(hbk + 1) * 512], in_=ps[b][:, :],
                                                                 func=AF.Exp, scale=0.125),
                     reads=[("ps", b)], writes=[("pS", r, hbk)])

        def pv(s):
            c, half = s // 2, s % 2
            lo = half * 64
            pair = c // 3
            r = s % 2
            rp = s % 3
            bo, bd = self.bank(), self.bank()
            for qi in range(4):
                nblk = ti * 4 + qi
                for kb in range(2):
                    blk = nblk + kb
                    P.op("pe", lambda e, qi=qi, kb=kb, blk=blk: e.matmul(
                        ps[bo][:, qi * 128:(qi + 1) * 128], vS[:, blk, pair * 128:(pair + 1) * 128],
                        pS[rp][:, qi * 256 + kb * 128:qi * 256 + (kb + 1) * 128],
                        start=(kb == 0), stop=(kb == 1)),
                        reads=[("vS", blk), ("pS", rp, qi // 2)], writes=[("ps", bo)])
            for kb in range(2):
                P.op("pe", lambda e, kb=kb: e.matmul(
                    ps[bd][:, :].rearrange("p (a q) -> p a q", a=4), T["ones1"][:, :],
                    pS[rp][:, :].rearrange("p (a b q) -> p a b q", a=4, b=2)[:, :, kb, :],
                    start=(kb == 0), stop=(kb == 1)),
                    reads=["ones1", ("pS", rp, 0), ("pS", rp, 1)], writes=[("ps", bd)])
            es = T["esink"][lo:lo + 64, lj * 12 + s:lj * 12 + s + 1]
            P.op("act", lambda e: e.activation(out=rc[r][lo:lo + 64, :], in_=ps[bd][lo:lo + 64, :], func=AF.Ln, bias=es),
                 reads=[("ps", bd), "esink"], writes=[("rc", r)])
            P.op("act", lambda e: e.activation(out=rc[r][lo:lo + 64, :], in_=rc[r][lo:lo + 64, :], func=AF.Exp,
                                               scale=-1.0),
                 reads=[("rc", r)], writes=[("rc", r)])
            P.op("dve", lambda e: e.tensor_tensor(out=yb[lo:lo + 64, c, :], in0=ps[bo][lo:lo + 64, :],
                                                  in1=rc[r][lo:lo + 64, :], op=ALU.mult),
                 reads=[("ps", bo), ("rc", r)], writes=[("y", c)])

        for s in range(14):
            if s < 12:
                scores(s)
            if s >= 2:
                pv(s - 2)

    def b_layer(self, lj, ti):
        P, T = self.P, self.T
        l = 2 + lj
        n = TT
        ps, hb, qb = T["ps"], T["hb"], T["qb"]
        self.rmsnorm(n, G_MIX + 8 * l)
        for p in range(2):
            s = self.wget(f"bq{lj}_{p}")
            for jj in range(4):
                c = p * 4 + jj
                b = self.proj_chunk(s, jj * 128, 512, 8, hb, "h", n)
                P.op("act", lambda e, b=b, c=c: e.activation(out=qb[:, c, 0:n], in_=ps[b][:, 0:n], func=AF.Copy),
                     reads=[("ps", b)], writes=[("q", c)])
        self.swa(lj, ti)
        self.mem_attn(l, n)
        self.out_proj(f"bout{lj}", n)
        self.ffn(l, n)

    def tile_body(self, ti, halo, n):
        T = self.T
        na = min(self.nlayers, 2)
        for l in range(na):
            self.a_mixer(l, n, halo)
            self.dump(f"amix{l}", T["yb"][:, :, 0:n], n)
            self.mem_attn(l, n)
            self.dump(f"y{l}", T["yb"][:, :, 0:n], n)
            self.out_proj(f"aout{l}", n)
            self.dump(f"xmix{l}", T["xb"][:, :, 0:n], n)
            self.ffn(l, n)
        if self.nlayers > 2:
            if halo:
                self.kv_proj(n, HALO - 128, 0)
            else:
                self.kv_proj(n, 0, 1 + ti * 4)
        if halo:
            return
        for lj in range(self.nlayers - 2):
            self.b_layer(lj, ti)
        if self.final:
            self.rmsnorm(n, G_FIN, to_x=True)

    def tile(self, ti):
        P, T = self.P, self.T
        halo = ti < 0
        n = HALO if halo else TT
        t0 = 0 if halo else HALO + ti * TT
        xsrc = T["xT"].rearrange("(c p) t -> p c t", p=128)[:, :, t0:t0 + n]
        P.dma("act", "xl", lambda e: e.dma_start(out=T["xb"][:, :, 0:n], in_=xsrc),
              reads=[], writes=[("x", c) for c in range(8)])
        try:
            self.tile_body(ti, halo, n)
        except StopTile:
            pass
        if halo:
            return
        ydst = T["yT"].rearrange("(c p) t -> p c t", p=128)[:, :, ti * TT:(ti + 1) * TT]
        P.dma("act", "st", lambda e: e.dma_start(out=ydst, in_=T["xb"][:, :, 0:n]),
              reads=[("x", c) for c in range(8)], writes=["yT"])


def build_nc(nlayers=4, final=True, ntiles=NT, debug=None):
    nc = bass.Bass("TRN2", target_bir_lowering=False)
    table, total, order = piece_table()
    T = {}
    T["xT"] = nc.dram_tensor("xT", [D, HALO + TOK], F32, kind="ExternalInput").ap()
    T["memT"] = nc.dram_tensor("memT", [D, N_MEM], F32, kind="ExternalInput").ap()
    T["vecs_d"] = nc.dram_tensor("vecs", [128, NV], F32, kind="ExternalInput").ap()
    T["bias_d"] = nc.dram_tensor("biasT", [128, 3072], F32, kind="ExternalInput").ap()
    T["biasf_d"] = nc.dram_tensor("biasF", [128, 3072], F32, kind="ExternalInput").ap()
    T["ident_d"] = nc.dram_tensor("ident", [128, 128], F32, kind="ExternalInput").ap()
    T["wflat"] = nc.dram_tensor("wflat", [total], F32, kind="ExternalInput").ap()
    T["wscr"] = nc.dram_tensor("wscr", [total], BF16, kind="Internal").ap()
    T["yT"] = nc.dram_tensor("yT", [D, TOK], F32, kind="ExternalOutput").ap()

    stack = ExitStack()
    with stack:
        def sb(name, shape, dt):
            return stack.enter_context(nc.sbuf_tensor(name, shape, dt))

        T["xb"] = sb("xb", [128, 8, TT], F32)
        T["hb"] = sb("hb", [128, 8, TT], BF16)
        T["qb"] = sb("qb", [128, 8, TT], BF16)
        T["yb"] = sb("yb", [128, 8, TT], BF16)
        T["actb"] = sb("actb", [128, NF, TT], BF16)
        T["usb"] = [sb(f"usb{i}", [128, TT], F32) for i in range(2)]
        T["vb"] = [sb(f"vb{i}", [128, TT + 2], F32) for i in range(2)]
        T["accb"] = [sb(f"accb{i}", [128, TT], F32) for i in range(2)]
        T["rstd"] = sb("rstd", [128, TT], F32)
        T["pS"] = [sb(f"pS{i}", [128, 1024], BF16) for i in range(3)]
        T["pm"] = [sb(f"pm{i}", [128, 2, TT], BF16) for i in range(2)]
        T["rc"] = [sb(f"rc{i}", [128, TT], F32) for i in range(2)]
        T["kT"] = sb("kT", [128, 2, NBLK * 128], BF16)
        T["vS"] = sb("vS", [128, NBLK, 256], BF16)
        T["bias8"] = sb("bias8_sb", [128, 12, 256], BF16)
        T["biasF"] = sb("biasF_sb", [128, 12, 256], BF16)
        T["mkT"] = sb("mkT", [128, 4, 2, N_MEM], BF16)
        T["mv"] = sb("mv", [128, 4, 2, 256], BF16)
        T["vecs"] = sb("vecs_sb", [128, NV], F32)
        T["esink"] = sb("esink", [128, 24], F32)
        T["carry"] = sb("carry", [128, 12, 2], F32)
        T["onesD"] = sb("onesD", [128, 128], BF16)
        T["ones1"] = sb("ones1", [128, 128], BF16)
        T["ident"] = sb("ident_sb", [128, 128], BF16)
        T["epsc"] = sb("epsc", [128, 1], F32)
        T["slots"] = [sb(f"slot{i}", [128, SLOT_ELEMS], BF16) for i in range(NSLOT)]
        T["stg"] = [sb(f"stg{i}", [128, STG], F32) for i in range(2)]
        T["ps"] = [stack.enter_context(nc.psum_tensor(f"ps{i}", [128, 512], F32)) for i in range(8)]

        bld = Builder(nc, T, nlayers=nlayers, final=final, ntiles=ntiles)
        bld.debug = debug
        seq = bld.run(NullProg(), recording=True)
        P = Prog()
        P.op("dve", lambda e: e.memset(T["epsc"][:, :], EPS), writes=["epsc"])
        bld.run(P, recording=False, seq=seq)
        P.emit(nc, stack)
    return nc


def make_in_maps(inp):
    x = np.asarray(inp["x"], np.float32)
    mem = np.asarray(inp["mem"], np.float32)
    wflat = pack_weights(inp)
    biasT, biasFirst = make_bias_tables(inp["rel_bias"])
    ident = np.eye(128, dtype=np.float32)
    in_maps = []
    for c in range(NCORE):
        b, qtr = c // 4, c % 4
        xt = np.zeros((D, HALO + TOK), np.float32)
        lo = qtr * TOK
        if qtr > 0:
            xt[:, :] = x[b, lo - HALO:lo + TOK, :].T
        else:
            xt[:, HALO:] = x[b, lo:lo + TOK, :].T
        in_maps.append({
            "xT": xt,
            "memT": np.ascontiguousarray(mem[b].T),
            "vecs": make_vecs(inp, 0.0 if qtr == 0 else 1.0),
            "biasT": biasT,
            "biasF": biasFirst if qtr == 0 else biasT,
            "ident": ident,
            "wflat": wflat,
        })
    return in_maps


_NC_CACHE = {}


def kernel(**inputs):
    in_maps = make_in_maps(inputs)
    if "nc" not in _NC_CACHE:
        _NC_CACHE["nc"] = build_nc()
    nc = _NC_CACHE["nc"]
    res = run_bass_kernel_spmd(nc, in_maps, core_ids=list(range(NCORE)))
    out = np.empty((BATCH, SEQ, D), np.float32)
    for c in range(NCORE):
        b, qtr = c // 4, c % 4
        out[b, qtr * TOK:(qtr + 1) * TOK, :] = res.results[c]["yT"].T
    return out
```

```python
import math
from contextlib import ExitStack

import numpy as np
import concourse.bass as bass
import concourse.mybir as mybir
from concourse.bass_utils import run_bass_kernel_spmd

F32 = mybir.dt.float32
BF16 = mybir.dt.bfloat16
AF = mybir.ActivationFunctionType
ALU = mybir.AluOpType

D = 1024
NCH = 8
SEQ = 16384
BATCH = 2
NCORE = 8
TOK = 4096
HALO = 136
TT = 512
NT = TOK // TT
DFF = 2816
NF = DFF // 128
N_MEM = 256
NBLK = TOK // 128 + 1
EPS = 1e-5
NEG = -30000.0
NSLOT = 5
SLOT_ELEMS = 4096
STG = 2048

HEAD_LO = [0, 1, 2, 6, 7, 8]
HEAD_HI = [3, 4, 5, 9, 10, 11]
SLOT_HEAD = [HEAD_LO[s // 2] if s % 2 == 0 else HEAD_HI[s // 2] for s in range(12)]

G_MIX, G_FFN, G_KV, G_MEM, G_FIN = 0, 32, 64, 72, 80
V_CONV = 88
V_FLAG = 124
V_SINK = 125
NV = 152


def piece_table():
    names = []
    for i in range(4):
        names.append((f"memkv{i}", 4096))
    for l in range(2):
        for p in range(5):
            names.append((f"ain{l}_{p}", 4096))
        for p in range(2):
            names.append((f"aout{l}_{p}", 4096))
        for fg in range(6):
            n = 4096 if fg < 5 else 2048
            names.append((f"g{l}_{fg}", n))
            names.append((f"u{l}_{fg}", n))
        for m in range(8):
            names.append((f"d{l}_{m}", NF * 128))
    names.append(("kv", 4096))
    for j in range(2):
        l = 2 + j
        for p in range(2):
            names.append((f"bq{j}_{p}", 4096))
        for p in range(2):
            names.append((f"bout{j}_{p}", 4096))
        for fg in range(6):
            n = 4096 if fg < 5 else 2048
            names.append((f"g{l}_{fg}", n))
            names.append((f"u{l}_{fg}", n))
        for m in range(8):
            names.append((f"d{l}_{m}", NF * 128))
    table = {}
    off = 0
    for nm, n in names:
        table[nm] = (off, n)
        off += 128 * n
    return table, off, [nm for nm, _ in names]


def _as_piece(w2d):
    K, Fc = w2d.shape
    kc = K // 128
    return np.ascontiguousarray(w2d.reshape(kc, 128, Fc).transpose(1, 0, 2)).reshape(128, kc * Fc)


def pack_weights(inp):
    table, total, _ = piece_table()
    wflat = np.empty((total,), np.float32)

    def put(name, w2d):
        off, n = table[name]
        blk = _as_piece(np.asarray(w2d, np.float32))
        assert blk.shape == (128, n), (name, blk.shape, n)
        wflat[off:off + 128 * n] = blk.reshape(-1)

    for i in range(4):
        put(f"memkv{i}", inp["w_mem_kv"][i])
    a_chunks = []
    for j in range(6):
        a_chunks += [j, 12 + j, 6 + j]
    a_chunks += [18, 19]
    a_cols = np.concatenate([np.arange(c * 128, (c + 1) * 128) for c in a_chunks])
    qperm = np.concatenate([np.arange(h * 64, (h + 1) * 64) for h in SLOT_HEAD] + [np.arange(768, 1024)])
    for l in range(4):
        for fg in range(6):
            lo, hi = fg * 512, min((fg + 1) * 512, DFF)
            put(f"g{l}_{fg}", inp["w_gate"][l][:, lo:hi])
            put(f"u{l}_{fg}", inp["w_up"][l][:, lo:hi])
        for m in range(8):
            put(f"d{l}_{m}", inp["w_down"][l][:, m * 128:(m + 1) * 128])
    for l in range(2):
        wp = np.asarray(inp["a_w_in"][l])[:, a_cols]
        for p in range(5):
            put(f"ain{l}_{p}", wp[:, p * 512:(p + 1) * 512])
        for p in range(2):
            put(f"aout{l}_{p}", inp["a_w_out"][l][:, p * 512:(p + 1) * 512])
    put("kv", inp["w_kv"])
    for j in range(2):
        wq = np.asarray(inp["b_w_q"][j])[:, qperm]
        wo = np.asarray(inp["b_w_out"][j])[qperm, :]
        for p in range(2):
            put(f"bq{j}_{p}", wq[:, p * 512:(p + 1) * 512])
            put(f"bout{j}_{p}", wo[:, p * 512:(p + 1) * 512])
    return wflat


def _rel_bucket_np(dist):
    max_exact = 16
    d = np.maximum(dist, 1).astype(np.float32)
    large = max_exact + (np.log(d / max_exact) / math.log(128 / max_exact) * (32 - max_exact)).astype(np.int32)
    large = np.minimum(large, 31)
    return np.where(dist < max_exact, dist, large)


def make_bias_tables(rel_bias):
    rel_bias = np.asarray(rel_bias, np.float32)
    kj = np.arange(128)[:, None, None]
    kb = np.arange(2)[None, :, None]
    qi = np.arange(128)[None, None, :]
    dist = qi + 128 - (kb * 128 + kj)
    inwin = (dist >= 0) & (dist < 128)
    bucket = _rel_bucket_np(np.maximum(dist, 0))
    fill = np.float32(NEG / 8.0)
    out = np.empty((128, 12, 2, 128), np.float32)
    for s in range(12):
        g = rel_bias[bucket, SLOT_HEAD[s]]
        out[:, s] = np.where(inwin, g, fill)
    first = out.copy()
    first[:, :, 0, :] = fill
    return out.reshape(128, 12 * 256), first.reshape(128, 12 * 256)


def make_vecs(inp, flag):
    v = np.zeros((128, NV), np.float32)

    def putg(col, g):
        v[:, col:col + 8] = np.asarray(g, np.float32).reshape(8, 128).T

    for l in range(4):
        putg(G_MIX + 8 * l, inp["norm_mix"][l])
        putg(G_FFN + 8 * l, inp["norm_ffn"][l])
    putg(G_KV, inp["kv_norm"])
    putg(G_MEM, inp["mem_norm"])
    putg(G_FIN, inp["final_norm"])
    cw = np.asarray(inp["a_conv_w"], np.float32)
    for l in range(2):
        for j in range(6):
            for t in range(3):
                v[:, V_CONV + (l * 6 + j) * 3 + t] = cw[l, t, j * 128:(j + 1) * 128]
    v[:, V_FLAG] = flag
    sk = np.asarray(inp["b_sinks"], np.float32)
    for lj in range(2):
        for s in range(12):
            v[:, V_SINK + lj * 12 + s] = sk[lj, SLOT_HEAD[s]]
    return v


class Prog:
    ENGS = ("pe", "act", "dve", "pool", "sp")
    EPOCH = 24000

    def __init__(self):
        self.ops = {e: [] for e in self.ENGS}
        self.cnt = {e: 0 for e in self.ENGS}
        self.scnt = {}
        self.lastw = {}
        self.readers = {}

    def _deps(self, reads, writes):
        deps = {}

        def add(tok):
            key = (tok[0], tok[1])
            if deps.get(key, -1) < tok[2]:
                deps[key] = tok[2]

        for r in reads:
            t = self.lastw.get(r)
            if t is not None:
                add(t)
        for w in writes:
            t = self.lastw.get(w)
            if t is not None:
                add(t)
            for key, idx in self.readers.get(w, {}).items():
                add((key[0], key[1], idx))
        return deps

    def _update(self, tok, reads, writes):
        key = (tok[0], tok[1])
        for r in reads:
            d = self.readers.setdefault(r, {})
            if d.get(key, -1) < tok[2]:
                d[key] = tok[2]
        for w in writes:
            self.lastw[w] = tok
            self.readers[w] = {}

    def op(self, eng, fn, reads=(), writes=()):
        deps = self._deps(reads, writes)
        idx = self.cnt[eng]
        self.cnt[eng] += 1
        tok = ("e", eng, idx)
        self.ops[eng].append((fn, deps, tok))
        self._update(tok, reads, writes)

    def dma(self, eng, stream, fn, reads=(), writes=()):
        deps = self._deps(reads, writes)
        idx = self.scnt.get(stream, 0)
        self.scnt[stream] = idx + 1
        if idx > 0:
            key = ("s", stream)
            if deps.get(key, -1) < idx - 1:
                deps[key] = idx - 1
        tok = ("s", stream, idx)
        self.ops[eng].append((fn, deps, tok))
        self._update(tok, reads, writes)

    def wait_all(self, eng, reads):
        deps = self._deps(reads, ())
        self.ops[eng].append((None, deps, None))

    def emit(self, nc, stack):
        n_ep = {e: max(1, -(-self.cnt[e] // self.EPOCH)) for e in self.ENGS}
        esem = {e: [stack.enter_context(nc.semaphore(f"e_{e}_{i}")) for i in range(n_ep[e])]
                for e in self.ENGS if self.cnt[e] > 0}
        ssem = {s: stack.enter_context(nc.semaphore(f"s_{s}")) for s in self.scnt}
        block = stack.enter_context(nc.Block())
        prog = self

        def run(engname, eng):
            known = {}
            for fn, deps, tok in prog.ops[engname]:
                for key, idx in deps.items():
                    if key[0] == "e" and key[1] == "pe" and engname == "pe":
                        continue
                    if known.get(key, -1) >= idx:
                        continue
                    known[key] = idx
                    if key[0] == "e":
                        eng.wait_ge(esem[key[1]][idx // prog.EPOCH], idx % prog.EPOCH + 1)
                    else:
                        eng.wait_ge(ssem[key[1]], 16 * (idx + 1))
                if fn is None:
                    continue
                ins = fn(eng)
                if tok[0] == "e":
                    ins.then_inc(esem[engname][tok[2] // prog.EPOCH], 1)
                else:
                    ins.then_inc(ssem[tok[1]], 16)

        @block.tensor
        def _(e):
            run("pe", e)

        @block.scalar
        def _(e):
            run("act", e)

        @block.vector
        def _(e):
            run("dve", e)

        @block.gpsimd
        def _(e):
            run("pool", e)

        @block.sync
        def _(e):
            run("sp", e)


class StopTile(Exception):
    pass


class NullProg:
    def op(self, *a, **k):
        pass

    def dma(self, *a, **k):
        pass

    def wait_all(self, *a, **k):
        pass


class Builder:
    def __init__(self, nc, T, nlayers=4, final=True, ntiles=NT):
        self.nc = nc
        self.T = T
        self.nlayers = nlayers
        self.final = final
        self.ntiles = ntiles
        self.table, self.total, self.order = piece_table()

    debug = None

    def dump(self, name, src, n, nch=8):
        if self.debug != name:
            return
        P, T = self.P, self.T
        xb = self.xb
        P.op("act", lambda e: e.activation(out=xb[:, 0:nch, 0:n], in_=src, func=AF.Copy),
             reads=[(self.xr, c) for c in range(8)] + [("h", c) for c in range(8)] + [("y", c) for c in range(8)]
             + [("q", c) for c in range(8)] + [("act", c) for c in range(NF)],
             writes=[(self.xr, c) for c in range(8)])
        raise StopTile()

    def bank(self):
        b = self._bank
        self._bank = (b + 1) % 7
        return b

    def wget(self, name):
        if self.recording:
            self.seq.append(name)
            return 0
        i = self.wpos
        assert self.seq[i] == name, (i, self.seq[i], name)
        P, Tn = self.P, self.T
        hi = min(i + NSLOT - 1, len(self.seq))
        while self.wloaded < hi:
            j = self.wloaded
            s = j % NSLOT
            nm = self.seq[j]
            off, n = self.table[nm]
            slot = Tn["slots"][s]
            if nm in self.converted:
                src = Tn["wscr"][off:off + 128 * n].rearrange("(p n) -> p n", p=128)
                P.dma("sp", f"ld{s}", lambda e, dst=slot[:, 0:n], src=src: e.dma_start(out=dst, in_=src),
                      reads=[("wscr", nm)], writes=[("slot", s)])
            else:
                self.converted.add(nm)
                src32 = Tn["wflat"][off:off + 128 * n].rearrange("(p n) -> p n", p=128)
                for c0 in range(0, n, STG):
                    c1 = min(c0 + STG, n)
                    k = self.stgk
                    self.stgk = (k + 1) % 2
                    stg = Tn["stg"][k]
                    assert self.cur_tile <= 0, "staging aliases the second x buffer"
                    sres = [("xB", 4 * k + q) for q in range(4)]
                    P.dma("sp", f"sg{k}", lambda e, stg=stg, src32=src32, c0=c0, c1=c1: e.dma_start(
                        out=stg[:, 0:c1 - c0], in_=src32[:, c0:c1]),
                        reads=[], writes=sres)
                    P.op("act", lambda e, stg=stg, slot=slot, c0=c0, c1=c1: e.activation(
                        out=slot[:, c0:c1], in_=stg[:, 0:c1 - c0], func=AF.Copy),
                        reads=sres, writes=[("slot", s)])
                dst = Tn["wscr"][off:off + 128 * n].rearrange("(p n) -> p n", p=128)
                P.dma("act", f"ws{self.wsk}", lambda e, dst=dst, slot=slot, n=n: e.dma_start(out=dst, in_=slot[:, 0:n]),
                      reads=[("slot", s)], writes=[("wscr", nm)])
                self.wsk = (self.wsk + 1) % 2
            self.wloaded += 1
        self.wpos += 1
        return i % NSLOT

    def run(self, P, recording, seq=None):
        self.P = P
        self.recording = recording
        self.seq = [] if recording else seq
        self.wpos = 0
        self.wloaded = 0
        self._bank = 0
        self.converted = set()
        self.stgk = 0
        self.wsk = 0
        self.xb, self.xr = self.T["xA"], "xA"
        self.stats_ready = False
        self.rstd_valid = False
        self.cur_tile = -1
        self.setup()
        self.tile(-1)
        for ti in range(self.ntiles):
            self.tile(ti)
        P.wait_all("act", [(self.xr, c) for c in range(8)] + ["yT"])
        return self.seq

    def setup(self):
        P, T, nc = self.P, self.T, self.nc
        P.op("dve", lambda e: e.memset(T["onesD"][:, :], 1.0 / D), writes=["onesD"])
        P.op("dve", lambda e: e.memset(T["ones1"][:, :], 1.0), writes=["ones1"])
        P.op("dve", lambda e: e.memset(T["carry"][:, :, :], 0.0), writes=[("carry", i) for i in range(12)])
        P.dma("act", "misc", lambda e: e.dma_start(out=T["vecs"][:, :], in_=T["vecs_d"][:, :]), writes=["vecs"])
        xflat = self.xb[:, :, :].rearrange("p c t -> p (c t)")
        P.dma("act", "misc", lambda e: e.dma_start(out=xflat[:, 0:128], in_=T["ident_d"][:, :]),
              writes=[(self.xr, c) for c in range(8)])
        P.op("act", lambda e: e.activation(out=T["ident"][:, :], in_=xflat[:, 0:128], func=AF.Copy),
             reads=[(self.xr, c) for c in range(8)], writes=["ident"])
        for src_name, dst_name in (("bias_d", "bias8"), ("biasf_d", "biasF")):
            P.dma("act", "misc", lambda e, s=src_name: e.dma_start(out=xflat[:, 0:3072], in_=T[s][:, :]),
                  writes=[(self.xr, c) for c in range(8)])
            dstf = T[dst_name][:, :, :].rearrange("p s n -> p (s n)")
            P.op("act", lambda e, dstf=dstf: e.activation(out=dstf, in_=xflat[:, 0:3072], func=AF.Copy, scale=8.0),
                 reads=[(self.xr, c) for c in range(8)], writes=[dst_name])
        P.op("act", lambda e: e.activation(out=T["esink"][:, :], in_=T["vecs"][:, V_SINK:V_SINK + 24], func=AF.Exp),
             reads=["vecs"], writes=["esink"])
        P.dma("act", "misc",
              lambda e, xb=self.xb: e.dma_start(out=xb[:, :, 0:N_MEM],
                                    in_=T["memT"].rearrange("(c p) t -> p c t", p=128)),
              writes=[(self.xr, c) for c in range(8)])
        self.rmsnorm(N_MEM, G_MEM)
        hb, ps = T["hb"], T["ps"]
        for l in range(4):
            s = self.wget(f"memkv{l}")
            sl = T["slots"][s]
            for i in range(2):
                b = self.bank()
                for k in range(8):
                    P.op("pe", lambda e, b=b, k=k, i=i, sl=sl: e.matmul(
                        ps[b][:, 0:N_MEM], sl[:, k * 512 + i * 128:k * 512 + (i + 1) * 128], hb[:, k, 0:N_MEM],
                        start=(k == 0), stop=(k == 7)),
                        reads=[("slot", s), ("h", k)], writes=[("ps", b)])
                P.op("act", lambda e, b=b, l=l, i=i: e.activation(out=T["mkT"][:, l, i, :], in_=ps[b][:, 0:N_MEM], func=AF.Copy),
                     reads=[("ps", b)], writes=["mk"])
            for mc in range(2):
                b = self.bank()
                for k in range(8):
                    P.op("pe", lambda e, b=b, k=k, mc=mc, sl=sl: e.matmul(
                        ps[b][:, 0:256], hb[:, k, mc * 128:(mc + 1) * 128], sl[:, k * 512 + 256:k * 512 + 512],
                        start=(k == 0), stop=(k == 7)),
                        reads=[("slot", s), ("h", k)], writes=[("ps", b)])
                P.op("act", lambda e, b=b, l=l, mc=mc: e.activation(out=T["mv"][:, l, mc, :], in_=ps[b][:, 0:256], func=AF.Copy),
                     reads=[("ps", b)], writes=["mv"])

    def stat_square(self, m, n):
        P, T = self.P, self.T
        xb, hb = self.xb, T["hb"]
        P.op("act", lambda e: e.activation(out=hb[:, m, 0:n], in_=xb[:, m, 0:n], func=AF.Square),
             reads=[(self.xr, m)], writes=[("h", m)])

    def stat_mm(self, m, n):
        P, T = self.P, self.T
        P.op("pe", lambda e: e.matmul(T["ps"][7][:, 0:n], T["onesD"][:, :], T["hb"][:, m, 0:n],
                                      start=(m == 0), stop=(m == 7)),
             reads=[("h", m), "onesD"], writes=[("ps", 7)])

    def rmsnorm(self, n, gcol, to_x=False):
        P, T = self.P, self.T
        xb, hb, actb, ps, rstd, vecs = self.xb, T["hb"], T["actb"], T["ps"], T["rstd"], T["vecs"]
        xr = self.xr
        if not self.rstd_valid:
            if not self.stats_ready:
                P.op("act", lambda e: e.activation(out=actb[:, 0:8, 0:n], in_=xb[:, :, 0:n], func=AF.Square),
                     reads=[(xr, c) for c in range(8)], writes=[("act", c) for c in range(8)])
                for c in range(8):
                    P.op("pe", lambda e, c=c: e.matmul(ps[7][:, 0:n], T["onesD"][:, :], actb[:, c, 0:n],
                                                       start=(c == 0), stop=(c == 7)),
                         reads=[("act", c), "onesD"], writes=[("ps", 7)])
            P.op("act", lambda e: e.activation(out=rstd[:, 0:n], in_=ps[7][:, 0:n], func=AF.Ln, bias=T["epsc"][:, 0:1]),
                 reads=[("ps", 7), "epsc"], writes=["rstd"])
            P.op("act", lambda e: e.activation(out=rstd[:, 0:n], in_=rstd[:, 0:n], func=AF.Exp, scale=-0.5),
                 reads=["rstd"], writes=["rstd"])
        self.stats_ready = False
        self.rstd_valid = False
        for c in range(8):
            if to_x:
                P.op("dve", lambda e, c=c: e.scalar_tensor_tensor(
                    out=xb[:, c, 0:n], in0=xb[:, c, 0:n], scalar=vecs[:, gcol + c:gcol + c + 1], in1=rstd[:, 0:n],
                    op0=ALU.mult, op1=ALU.mult),
                    reads=[(xr, c), "rstd", "vecs"], writes=[(xr, c)])
            else:
                P.op("dve", lambda e, c=c: e.scalar_tensor_tensor(
                    out=hb[:, c, 0:n], in0=xb[:, c, 0:n], scalar=vecs[:, gcol + c:gcol + c + 1], in1=rstd[:, 0:n],
                    op0=ALU.mult, op1=ALU.mult),
                    reads=[(xr, c), "rstd", "vecs"], writes=[("h", c)])

    def proj_chunk(self, s, col0, kstride, nk, rhs_t, rhs_res, n):
        P, T = self.P, self.T
        ps, sl = T["ps"], T["slots"][s]
        b = self.bank()
        for k in range(nk):
            P.op("pe", lambda e, k=k: e.matmul(ps[b][:, 0:n], sl[:, k * kstride + col0:k * kstride + col0 + 128],
                                               rhs_t[:, k, 0:n], start=(k == 0), stop=(k == nk - 1)),
                 reads=[("slot", s), (rhs_res, k)], writes=[("ps", b)])
        return b

    def residual_add(self, b, m, n):
        P, T = self.P, self.T
        xb, ps = self.xb, T["ps"]
        xr = self.xr
        P.op("dve", lambda e: e.tensor_tensor(out=xb[:, m, 0:n], in0=ps[b][:, 0:n], in1=xb[:, m, 0:n], op=ALU.add),
             reads=[("ps", b), (xr, m)], writes=[(xr, m)])

    def a_mixer(self, l, n, halo):
        P, T = self.P, self.T
        ps, hb, qb, yb, vecs = T["ps"], T["hb"], T["qb"], T["yb"], T["vecs"]
        order = []
        for j in range(6):
            order += [("u", j), ("C", j), ("B", j)]
        order += [("q", 0), ("q", 1)]
        self.rmsnorm(n, G_MIX + 8 * l)
        self.dump(f"h{l}", hb[:, :, 0:n], n)
        for p in range(5):
            s = self.wget(f"ain{l}_{p}")
            for jj in range(4):
                kind, j = order[p * 4 + jj]
                b = self.proj_chunk(s, jj * 128, 512, 8, hb, "h", n)
                r = j % 2
                usb, vb, accb = T["usb"][r], T["vb"][r], T["accb"][r]
                if kind == "u":
                    P.op("act", lambda e, b=b, usb=usb: e.activation(out=usb[:, 0:n], in_=ps[b][:, 0:n], func=AF.Copy),
                         reads=[("ps", b)], writes=[("usb", r)])
                elif kind == "C":
                    ci = l * 6 + j
                    cw = V_CONV + ci * 3
                    P.op("dve", lambda e, vb=vb, ci=ci: e.tensor_copy(out=vb[:, 0:2], in_=T["carry"][:, ci, :]),
                         reads=[("carry", ci)], writes=[("vb", r)])
                    P.op("dve", lambda e, b=b, vb=vb, usb=usb: e.tensor_tensor(
                        out=vb[:, 2:2 + n], in0=ps[b][:, 0:n], in1=usb[:, 0:n], op=ALU.mult),
                        reads=[("ps", b), ("usb", r)], writes=[("vb", r)])
                    if halo:
                        P.op("dve", lambda e, vb=vb, ci=ci: e.tensor_scalar(
                            out=T["carry"][:, ci, :], in0=vb[:, n:n + 2], scalar1=vecs[:, V_FLAG:V_FLAG + 1],
                            scalar2=None, op0=ALU.mult),
                            reads=[("vb", r), "vecs"], writes=[("carry", ci)])
                    else:
                        P.op("dve", lambda e, vb=vb, ci=ci: e.tensor_copy(out=T["carry"][:, ci, :], in_=vb[:, n:n + 2]),
                             reads=[("vb", r)], writes=[("carry", ci)])
                    P.op("dve", lambda e, vb=vb, accb=accb, cw=cw: e.tensor_scalar(
                        out=accb[:, 0:n], in0=vb[:, 2:2 + n], scalar1=vecs[:, cw + 2:cw + 3], scalar2=None, op0=ALU.mult),
                        reads=[("vb", r), "vecs"], writes=[("accb", r)])
                    P.op("dve", lambda e, vb=vb, accb=accb, cw=cw: e.scalar_tensor_tensor(
                        out=accb[:, 0:n], in0=vb[:, 1:1 + n], scalar=vecs[:, cw + 1:cw + 2], in1=accb[:, 0:n],
                        op0=ALU.mult, op1=ALU.add),
                        reads=[("vb", r), ("accb", r), "vecs"], writes=[("accb", r)])
                    P.op("dve", lambda e, vb=vb, accb=accb, cw=cw: e.scalar_tensor_tensor(
                        out=accb[:, 0:n], in0=vb[:, 0:n], scalar=vecs[:, cw:cw + 1], in1=accb[:, 0:n],
                        op0=ALU.mult, op1=ALU.add),
                        reads=[("vb", r), ("accb", r), "vecs"], writes=[("accb", r)])
                elif kind == "B":
                    P.op("dve", lambda e, b=b, j=j, accb=accb: e.tensor_tensor(
                        out=yb[:, j, 0:n], in0=ps[b][:, 0:n], in1=accb[:, 0:n], op=ALU.mult),
                        reads=[("ps", b), ("accb", r)], writes=[("y", j)])
                else:
                    P.op("act", lambda e, b=b, j=j: e.activation(out=qb[:, 6 + j, 0:n], in_=ps[b][:, 0:n], func=AF.Copy),
                         reads=[("ps", b)], writes=[("q", 6 + j)])

    def mem_attn(self, l, n):
        P, T = self.P, self.T
        ps, qb, yb, pm, rc = T["ps"], T["qb"], T["yb"], T["pm"], T["rc"]
        mkT, mv = T["mkT"], T["mv"]

        def scores(hm):
            i, half = hm // 2, hm % 2
            r = hm % 2
            lo = half * 64
            for mc in range(2):
                b = self.bank()
                P.op("pe", lambda e, b=b, mc=mc: e.matmul(
                    ps[b][:, 0:n], mkT[lo:lo + 64, l, i, mc * 128:(mc + 1) * 128], qb[lo:lo + 64, 6 + i, 0:n],
                    start=True, stop=True),
                    reads=["mk", ("q", 6 + i)], writes=[("ps", b)])
                P.op("act", lambda e, b=b, mc=mc: e.activation(out=pm[r][:, mc, 0:n], in_=ps[b][:, 0:n],
                                                               func=AF.Exp, scale=0.125),
                     reads=[("ps", b)], writes=[("pm", r, mc)])

        def pv(hm):
            i, half = hm // 2, hm % 2
            r = hm % 2
            lo = half * 64
            bo, bd = self.bank(), self.bank()
            for mc in range(2):
                P.op("pe", lambda e, mc=mc: e.matmul(ps[bo][:, 0:n], mv[:, l, mc, i * 128:(i + 1) * 128],
                                                     pm[r][:, mc, 0:n], start=(mc == 0), stop=(mc == 1)),
                     reads=["mv", ("pm", r, mc)], writes=[("ps", bo)])
            for mc in range(2):
                P.op("pe", lambda e, mc=mc: e.matmul(ps[bd][:, 0:n], T["ones1"][:, :], pm[r][:, mc, 0:n],
                                                     start=(mc == 0), stop=(mc == 1)),
                     reads=["ones1", ("pm", r, mc)], writes=[("ps", bd)])
            P.op("act", lambda e: e.activation(out=rc[r][lo:lo + 64, 0:n], in_=ps[bd][lo:lo + 64, 0:n], func=AF.Ln),
                 reads=[("ps", bd)], writes=[("rc", r)])
            P.op("act", lambda e: e.activation(out=rc[r][lo:lo + 64, 0:n], in_=rc[r][lo:lo + 64, 0:n], func=AF.Exp,
                                               scale=-1.0),
                 reads=[("rc", r)], writes=[("rc", r)])
            P.op("dve", lambda e: e.tensor_tensor(out=yb[lo:lo + 64, 6 + i, 0:n], in0=ps[bo][lo:lo + 64, 0:n],
                                                  in1=rc[r][lo:lo + 64, 0:n], op=ALU.mult),
                 reads=[("ps", bo), ("rc", r)], writes=[("y", 6 + i)])

        for hm in range(5):
            if hm < 4:
                scores(hm)
            if hm >= 1:
                pv(hm - 1)

    def out_proj(self, prefix, n):
        T = self.T
        for p in range(2):
            s = self.wget(f"{prefix}_{p}")
            for jj in range(4):
                m = p * 4 + jj
                b = self.proj_chunk(s, jj * 128, 512, 8, T["yb"], "y", n)
                if m >= 1:
                    self.stat_mm(m - 1, n)
                self.residual_add(b, m, n)
                self.stat_square(m, n)
        self.stat_mm(7, n)
        self.stats_ready = True

    def ffn(self, l, n):
        P, T = self.P, self.T
        ps, hb, actb = T["ps"], T["hb"], T["actb"]
        self.rmsnorm(n, G_FFN + 8 * l)
        self.dump(f"ffnh{l}", hb[:, :, 0:n], n)
        for fg in range(6):
            ncol = 512 if fg < 5 else 256
            sg = self.wget(f"g{l}_{fg}")
            su = self.wget(f"u{l}_{fg}")
            for jj in range(ncol // 128):
                f = fg * 4 + jj
                bg = self.proj_chunk(sg, jj * 128, ncol, 8, hb, "h", n)
                bu = self.proj_chunk(su, jj * 128, ncol, 8, hb, "h", n)
                r = f % 2
                usb = T["usb"][r]
                P.op("act", lambda e, bg=bg, usb=usb: e.activation(out=usb[:, 0:n], in_=ps[bg][:, 0:n], func=AF.Silu),
                     reads=[("ps", bg)], writes=[("usb", r)])
                P.op("dve", lambda e, bu=bu, usb=usb, f=f: e.tensor_tensor(
                    out=actb[:, f, 0:n], in0=ps[bu][:, 0:n], in1=usb[:, 0:n], op=ALU.mult),
                    reads=[("ps", bu), ("usb", r)], writes=[("act", f)])
        self.dump(f"act{l}", actb[:, 0:8, 0:n], n)
        for m in range(8):
            s = self.wget(f"d{l}_{m}")
            b = self.proj_chunk(s, 0, 128, NF, actb, "act", n)
            if m >= 1:
                self.stat_mm(m - 1, n)
            self.residual_add(b, m, n)
            self.stat_square(m, n)
        self.stat_mm(7, n)
        self.stats_ready = True

    def kv_proj(self, n, c0, blk0):
        P, T = self.P, self.T
        ps, hb, kT, vS = T["ps"], T["hb"], T["kT"], T["vS"]
        self.rmsnorm(n, G_KV)
        self.rstd_valid = True
        s = self.wget("kv")
        sl = T["slots"][s]
        nvalid = n - c0
        nblk = nvalid // 128
        for pr in range(2):
            b = self.proj_chunk(s, pr * 128, 512, 8, hb, "h", n)
            P.op("act", lambda e, b=b, pr=pr: e.activation(
                out=kT[:, pr, blk0 * 128:blk0 * 128 + nvalid], in_=ps[b][:, c0:n], func=AF.Copy),
                reads=[("ps", b)], writes=[("kT", blk0 + t) for t in range(nblk)])
        for tb in range(nblk):
            b = self.bank()
            for k in range(8):
                P.op("pe", lambda e, b=b, k=k, tb=tb: e.matmul(
                    ps[b][:, 0:256], hb[:, k, c0 + tb * 128:c0 + (tb + 1) * 128], sl[:, k * 512 + 256:k * 512 + 512],
                    start=(k == 0), stop=(k == 7)),
                    reads=[("slot", s), ("h", k)], writes=[("ps", b)])
            P.op("act", lambda e, b=b, tb=tb: e.activation(out=vS[:, blk0 + tb, :], in_=ps[b][:, 0:256], func=AF.Copy),
                 reads=[("ps", b)], writes=[("vS", blk0 + tb)])

    def swa(self, lj, ti):
        P, T = self.P, self.T
        ps, qb, yb, pS, rc = T["ps"], T["qb"], T["yb"], T["pS"], T["rc"]
        kT, vS, bias8, biasF, ident = T["kT"], T["vS"], T["bias8"], T["biasF"], T["ident"]

        def scores(s):
            c, half = s // 2, s % 2
            lo = half * 64
            pair = c // 3
            r = s % 3
            for hbk in range(2):
                b = self.bank()
                if ti == 0 and hbk == 0:
                    P.op("pe", lambda e, b=b: e.matmul(ps[b][:, 0:256], ident[:, :], biasF[:, s, :],
                                                       start=True, stop=False, skip_group_check=True),
                         reads=["ident", "biasF"], writes=[("ps", b)])
                    P.op("pe", lambda e, b=b: e.matmul(ps[b][:, 256:512], ident[:, :], bias8[:, s, :],
                                                       start=False, stop=False, skip_group_check=True),
                         reads=["ident", "bias8"], writes=[("ps", b)])
                    sgc = True
                else:
                    P.op("pe", lambda e, b=b: e.matmul(
                        ps[b][:, :].rearrange("p (a n) -> p a n", a=2), ident[:, :],
                        bias8[:, s:s + 1, :].broadcast_to([128, 2, 256]),
                        start=True, stop=False),
                        reads=["ident", "bias8"], writes=[("ps", b)])
                    sgc = False
                for qq in range(2):
                    qi = 2 * hbk + qq
                    nblk = ti * 4 + qi
                    for kb in range(2):
                        blk = nblk + kb
                        last = (qq == 1 and kb == 1)
                        P.op("pe", lambda e, b=b, qq=qq, kb=kb, blk=blk, qi=qi, last=last, sgc=sgc: e.matmul(
                            ps[b][:, qq * 256 + kb * 128:qq * 256 + (kb + 1) * 128],
                            kT[lo:lo + 64, pair, blk * 128:(blk + 1) * 128],
                            qb[lo:lo + 64, c, qi * 128:(qi + 1) * 128],
                            start=False, stop=last, skip_group_check=sgc),
                            reads=[("kT", blk), ("q", c)], writes=[("ps", b)])
                P.op("act", lambda e, b=b, hbk=hbk: e.activation(out=pS[r][:, hbk * 512:(hbk + 1) * 512], in_=ps[b][:, :],
                                                                 func=AF.Exp, scale=0.125),
                     reads=[("ps", b)], writes=[("pS", r, hbk)])

        def pv(s):
            c, half = s // 2, s % 2
            lo = half * 64
            pair = c // 3
            r = s % 2
            rp = s % 3
            bo, bd = self.bank(), self.bank()
            for qi in range(4):
                nblk = ti * 4 + qi
                for kb in range(2):
                    blk = nblk + kb
                    P.op("pe", lambda e, qi=qi, kb=kb, blk=blk: e.matmul(
                        ps[bo][:, qi * 128:(qi + 1) * 128], vS[:, blk, pair * 128:(pair + 1) * 128],
                        pS[rp][:, qi * 256 + kb * 128:qi * 256 + (kb + 1) * 128],
                        start=(kb == 0), stop=(kb == 1)),
                        reads=[("vS", blk), ("pS", rp, qi // 2)], writes=[("ps", bo)])
            for kb in range(2):
                P.op("pe", lambda e, kb=kb: e.matmul(
                    ps[bd][:, :].rearrange("p (a q) -> p a q", a=4), T["ones1"][:, :],
                    pS[rp][:, :].rearrange("p (a b q) -> p a b q", a=4, b=2)[:, :, kb, :],
                    start=(kb == 0), stop=(kb == 1)),
                    reads=["ones1", ("pS", rp, 0), ("pS", rp, 1)], writes=[("ps", bd)])
            es = T["esink"][lo:lo + 64, lj * 12 + s:lj * 12 + s + 1]
            P.op("act", lambda e: e.activation(out=rc[r][lo:lo + 64, :], in_=ps[bd][lo:lo + 64, :], func=AF.Ln, bias=es),
                 reads=[("ps", bd), "esink"], writes=[("rc", r)])
            P.op("act", lambda e: e.activation(out=rc[r][lo:lo + 64, :], in_=rc[r][lo:lo + 64, :], func=AF.Exp,
                                               scale=-1.0),
                 reads=[("rc", r)], writes=[("rc", r)])
            P.op("dve", lambda e: e.tensor_tensor(out=yb[lo:lo + 64, c, :], in0=ps[bo][lo:lo + 64, :],
                                                  in1=rc[r][lo:lo + 64, :], op=ALU.mult),
                 reads=[("ps", bo), ("rc", r)], writes=[("y", c)])

        for s in range(14):
            if s < 12:
                scores(s)
            if s >= 2:
                pv(s - 2)

    def b_layer(self, lj, ti):
        P, T = self.P, self.T
        l = 2 + lj
        n = TT
        ps, hb, qb = T["ps"], T["hb"], T["qb"]
        self.rmsnorm(n, G_MIX + 8 * l)
        for p in range(2):
            s = self.wget(f"bq{lj}_{p}")
            for jj in range(4):
                c = p * 4 + jj
                b = self.proj_chunk(s, jj * 128, 512, 8, hb, "h", n)
                P.op("act", lambda e, b=b, c=c: e.activation(out=qb[:, c, 0:n], in_=ps[b][:, 0:n], func=AF.Copy),
                     reads=[("ps", b)], writes=[("q", c)])
        self.swa(lj, ti)
        self.mem_attn(l, n)
        self.out_proj(f"bout{lj}", n)
        self.ffn(l, n)

    def tile_body(self, ti, halo, n):
        T = self.T
        na = min(self.nlayers, 2)
        for l in range(na):
            self.a_mixer(l, n, halo)
            self.dump(f"amix{l}", T["yb"][:, :, 0:n], n)
            self.mem_attn(l, n)
            self.dump(f"y{l}", T["yb"][:, :, 0:n], n)
            self.out_proj(f"aout{l}", n)
            self.dump(f"xmix{l}", self.xb[:, :, 0:n], n)
            self.ffn(l, n)
        if self.nlayers > 2:
            if halo:
                self.kv_proj(n, HALO - 128, 0)
            else:
                self.kv_proj(n, 0, 1 + ti * 4)
        if halo:
            return
        for lj in range(self.nlayers - 2):
            self.b_layer(lj, ti)
        if self.final:
            self.rmsnorm(n, G_FIN, to_x=True)

    def xbuf_for(self, ti):
        if ti <= 0 or ti % 2 == 0:
            return self.T["xA"], "xA"
        return self.T["xB"], "xB"

    def emit_xload(self, ti):
        P, T = self.P, self.T
        halo = ti < 0
        n = HALO if halo else TT
        t0 = 0 if halo else HALO + ti * TT
        xb, xr = self.xbuf_for(ti)
        xsrc = T["xT"].rearrange("(c p) t -> p c t", p=128)[:, :, t0:t0 + n]
        P.dma("sp", "xl", lambda e: e.dma_start(out=xb[:, :, 0:n], in_=xsrc),
              reads=[], writes=[(xr, c) for c in range(8)])

    def tile(self, ti):
        P, T = self.P, self.T
        halo = ti < 0
        n = HALO if halo else TT
        self.cur_tile = ti
        self.xb, self.xr = self.xbuf_for(ti)
        self.stats_ready = False
        self.rstd_valid = False
        if halo:
            self.emit_xload(-1)
        if ti >= 1 and ti + 1 < self.ntiles:
            self.emit_xload(ti + 1)
        try:
            self.tile_body(ti, halo, n)
        except StopTile:
            pass
        if halo:
            self.emit_xload(0)
            return
        xb, xr = self.xb, self.xr
        ydst = T["yT"].rearrange("(c p) t -> p c t", p=128)[:, :, ti * TT:(ti + 1) * TT]
        P.dma("act", "st", lambda e: e.dma_start(out=ydst, in_=xb[:, :, 0:n]),
              reads=[(xr, c) for c in range(8)], writes=["yT"])
        if ti == 0 and self.ntiles > 1:
            self.emit_xload(1)


def build_nc(nlayers=4, final=True, ntiles=NT, debug=None):
    nc = bass.Bass("TRN2", target_bir_lowering=False)
    table, total, order = piece_table()
    T = {}
    T["xT"] = nc.dram_tensor("xT", [D, HALO + TOK], F32, kind="ExternalInput").ap()
    T["memT"] = nc.dram_tensor("memT", [D, N_MEM], F32, kind="ExternalInput").ap()
    T["vecs_d"] = nc.dram_tensor("vecs", [128, NV], F32, kind="ExternalInput").ap()
    T["bias_d"] = nc.dram_tensor("biasT", [128, 3072], F32, kind="ExternalInput").ap()
    T["biasf_d"] = nc.dram_tensor("biasF", [128, 3072], F32, kind="ExternalInput").ap()
    T["ident_d"] = nc.dram_tensor("ident", [128, 128], F32, kind="ExternalInput").ap()
    T["wflat"] = nc.dram_tensor("wflat", [total], F32, kind="ExternalInput").ap()
    T["wscr"] = nc.dram_tensor("wscr", [total], BF16, kind="Internal").ap()
    T["yT"] = nc.dram_tensor("yT", [D, TOK], F32, kind="ExternalOutput").ap()

    stack = ExitStack()
    with stack:
        def sb(name, shape, dt):
            return stack.enter_context(nc.sbuf_tensor(name, shape, dt))

        T["xA"] = sb("xb", [128, 8, TT], F32)
        T["hb"] = sb("hb", [128, 8, TT], BF16)
        T["qb"] = sb("qb", [128, 8, TT], BF16)
        T["yb"] = sb("yb", [128, 8, TT], BF16)
        T["actb"] = sb("actb", [128, NF, TT], BF16)
        T["usb"] = [sb(f"usb{i}", [128, TT], F32) for i in range(2)]
        T["vb"] = [sb(f"vb{i}", [128, TT + 2], F32) for i in range(2)]
        T["accb"] = [sb(f"accb{i}", [128, TT], F32) for i in range(2)]
        T["rstd"] = sb("rstd", [128, TT], F32)
        T["pS"] = [sb(f"pS{i}", [128, 1024], BF16) for i in range(3)]
        T["pm"] = [sb(f"pm{i}", [128, 2, TT], BF16) for i in range(2)]
        T["rc"] = [sb(f"rc{i}", [128, TT], F32) for i in range(2)]
        T["kT"] = sb("kT", [128, 2, NBLK * 128], BF16)
        T["vS"] = sb("vS", [128, NBLK, 256], BF16)
        T["bias8"] = sb("bias8_sb", [128, 12, 256], BF16)
        T["biasF"] = sb("biasF_sb", [128, 12, 256], BF16)
        T["mkT"] = sb("mkT", [128, 4, 2, N_MEM], BF16)
        T["mv"] = sb("mv", [128, 4, 2, 256], BF16)
        T["vecs"] = sb("vecs_sb", [128, NV], F32)
        T["esink"] = sb("esink", [128, 24], F32)
        T["carry"] = sb("carry", [128, 12, 2], F32)
        T["onesD"] = sb("onesD", [128, 128], BF16)
        T["ones1"] = sb("ones1", [128, 128], BF16)
        T["ident"] = sb("ident_sb", [128, 128], BF16)
        T["epsc"] = sb("epsc", [128, 1], F32)
        T["slots"] = [sb(f"slot{i}", [128, SLOT_ELEMS], BF16) for i in range(NSLOT)]
        T["xB"] = sb("xB", [128, 8, TT], F32)
        xbf = T["xB"][:, :, :].rearrange("p c t -> p (c t)")
        T["stg"] = [xbf[:, i * STG:(i + 1) * STG] for i in range(2)]
        T["ps"] = [stack.enter_context(nc.psum_tensor(f"ps{i}", [128, 512], F32)) for i in range(8)]

        bld = Builder(nc, T, nlayers=nlayers, final=final, ntiles=ntiles)
        bld.debug = debug
        seq = bld.run(NullProg(), recording=True)
        P = Prog()
        P.op("dve", lambda e: e.memset(T["epsc"][:, :], EPS), writes=["epsc"])
        bld.run(P, recording=False, seq=seq)
        P.emit(nc, stack)
    return nc


def make_in_maps(inp):
    x = np.asarray(inp["x"], np.float32)
    mem = np.asarray(inp["mem"], np.float32)
    wflat = pack_weights(inp)
    biasT, biasFirst = make_bias_tables(inp["rel_bias"])
    ident = np.eye(128, dtype=np.float32)
    in_maps = []
    for c in range(NCORE):
        b, qtr = c // 4, c % 4
        xt = np.zeros((D, HALO + TOK), np.float32)
        lo = qtr * TOK
        if qtr > 0:
            xt[:, :] = x[b, lo - HALO:lo + TOK, :].T
        else:
            xt[:, HALO:] = x[b, lo:lo + TOK, :].T
        in_maps.append({
            "xT": xt,
            "memT": np.ascontiguousarray(mem[b].T),
            "vecs": make_vecs(inp, 0.0 if qtr == 0 else 1.0),
            "biasT": biasT,
            "biasF": biasFirst if qtr == 0 else biasT,
            "ident": ident,
            "wflat": wflat,
        })
    return in_maps


_NC_CACHE = {}


def kernel(**inputs):
    in_maps = make_in_maps(inputs)
    if "nc" not in _NC_CACHE:
        _NC_CACHE["nc"] = build_nc()
    nc = _NC_CACHE["nc"]
    res = run_bass_kernel_spmd(nc, in_maps, core_ids=list(range(NCORE)))
    out = np.empty((BATCH, SEQ, D), np.float32)
    for c in range(NCORE):
        b, qtr = c // 4, c % 4
        out[b, qtr * TOK:(qtr + 1) * TOK, :] = res.results[c]["yT"].T
    return out
```

```python
import math
from contextlib import ExitStack

import numpy as np
import concourse.bass as bass
import concourse.mybir as mybir
from concourse.bass_utils import run_bass_kernel_spmd

F32 = mybir.dt.float32
BF16 = mybir.dt.bfloat16
AF = mybir.ActivationFunctionType
ALU = mybir.AluOpType

D = 1024
NCH = 8
SEQ = 16384
BATCH = 2
NCORE = 8
TOK = 4096
HALO = 136
TT = 512
NT = TOK // TT
DFF = 2816
NF = DFF // 128
N_MEM = 256
NBLK = TOK // 128 + 1
EPS = 1e-5
NEG = -30000.0
NSLOT = 5
SLOT_ELEMS = 4096
STG = 2048

HEAD_LO = [0, 1, 2, 6, 7, 8]
HEAD_HI = [3, 4, 5, 9, 10, 11]
SLOT_HEAD = [HEAD_LO[s // 2] if s % 2 == 0 else HEAD_HI[s // 2] for s in range(12)]

G_MIX, G_FFN, G_KV, G_MEM, G_FIN = 0, 32, 64, 72, 80
V_CONV = 88
V_FLAG = 124
V_SINK = 125
NV = 152


def piece_table():
    names = []
    for i in range(4):
        names.append((f"memkv{i}", 4096))
    for l in range(2):
        for p in range(5):
            names.append((f"ain{l}_{p}", 4096))
        for p in range(2):
            names.append((f"aout{l}_{p}", 4096))
        for fg in range(6):
            n = 4096 if fg < 5 else 2048
            names.append((f"g{l}_{fg}", n))
            names.append((f"u{l}_{fg}", n))
        for m in range(8):
            names.append((f"d{l}_{m}", NF * 128))
    names.append(("kv", 4096))
    for j in range(2):
        l = 2 + j
        for p in range(2):
            names.append((f"bq{j}_{p}", 4096))
        for p in range(2):
            names.append((f"bout{j}_{p}", 4096))
        for fg in range(6):
            n = 4096 if fg < 5 else 2048
            names.append((f"g{l}_{fg}", n))
            names.append((f"u{l}_{fg}", n))
        for m in range(8):
            names.append((f"d{l}_{m}", NF * 128))
    table = {}
    off = 0
    for nm, n in names:
        table[nm] = (off, n)
        off += 128 * n
    return table, off, [nm for nm, _ in names]


def _as_piece(w2d):
    K, Fc = w2d.shape
    kc = K // 128
    return np.ascontiguousarray(w2d.reshape(kc, 128, Fc).transpose(1, 0, 2)).reshape(128, kc * Fc)


def pack_weights(inp):
    table, total, _ = piece_table()
    wflat = np.empty((total,), np.float32)

    def put(name, w2d):
        off, n = table[name]
        blk = _as_piece(np.asarray(w2d, np.float32))
        assert blk.shape == (128, n), (name, blk.shape, n)
        wflat[off:off + 128 * n] = blk.reshape(-1)

    for i in range(4):
        put(f"memkv{i}", inp["w_mem_kv"][i])
    a_chunks = []
    for j in range(6):
        a_chunks += [j, 12 + j, 6 + j]
    a_chunks += [18, 19]
    a_cols = np.concatenate([np.arange(c * 128, (c + 1) * 128) for c in a_chunks])
    qperm = np.concatenate([np.arange(h * 64, (h + 1) * 64) for h in SLOT_HEAD] + [np.arange(768, 1024)])
    for l in range(4):
        for fg in range(6):
            lo, hi = fg * 512, min((fg + 1) * 512, DFF)
            put(f"g{l}_{fg}", inp["w_gate"][l][:, lo:hi])
            put(f"u{l}_{fg}", inp["w_up"][l][:, lo:hi])
        for m in range(8):
            put(f"d{l}_{m}", inp["w_down"][l][:, m * 128:(m + 1) * 128])
    for l in range(2):
        wp = np.asarray(inp["a_w_in"][l])[:, a_cols]
        for p in range(5):
            put(f"ain{l}_{p}", wp[:, p * 512:(p + 1) * 512])
        for p in range(2):
            put(f"aout{l}_{p}", inp["a_w_out"][l][:, p * 512:(p + 1) * 512])
    put("kv", inp["w_kv"])
    for j in range(2):
        wq = np.asarray(inp["b_w_q"][j])[:, qperm]
        wo = np.asarray(inp["b_w_out"][j])[qperm, :]
        for p in range(2):
            put(f"bq{j}_{p}", wq[:, p * 512:(p + 1) * 512])
            put(f"bout{j}_{p}", wo[:, p * 512:(p + 1) * 512])
    return wflat


def _rel_bucket_np(dist):
    max_exact = 16
    d = np.maximum(dist, 1).astype(np.float32)
    large = max_exact + (np.log(d / max_exact) / math.log(128 / max_exact) * (32 - max_exact)).astype(np.int32)
    large = np.minimum(large, 31)
    return np.where(dist < max_exact, dist, large)


def make_bias_tables(rel_bias):
    rel_bias = np.asarray(rel_bias, np.float32)
    kj = np.arange(128)[:, None, None]
    kb = np.arange(2)[None, :, None]
    qi = np.arange(128)[None, None, :]
    dist = qi + 128 - (kb * 128 + kj)
    inwin = (dist >= 0) & (dist < 128)
    bucket = _rel_bucket_np(np.maximum(dist, 0))
    fill = np.float32(NEG / 8.0)
    out = np.empty((128, 12, 2, 128), np.float32)
    for s in range(12):
        g = rel_bias[bucket, SLOT_HEAD[s]]
        out[:, s] = np.where(inwin, g, fill)
    first = out.copy()
    first[:, :, 0, :] = fill
    return out.reshape(128, 12 * 256), first.reshape(128, 12 * 256)


def make_vecs(inp, flag):
    v = np.zeros((128, NV), np.float32)

    def putg(col, g):
        v[:, col:col + 8] = np.asarray(g, np.float32).reshape(8, 128).T

    for l in range(4):
        putg(G_MIX + 8 * l, inp["norm_mix"][l])
        putg(G_FFN + 8 * l, inp["norm_ffn"][l])
    putg(G_KV, inp["kv_norm"])
    putg(G_MEM, inp["mem_norm"])
    putg(G_FIN, inp["final_norm"])
    cw = np.asarray(inp["a_conv_w"], np.float32)
    for l in range(2):
        for j in range(6):
            for t in range(3):
                v[:, V_CONV + (l * 6 + j) * 3 + t] = cw[l, t, j * 128:(j + 1) * 128]
    v[:, V_FLAG] = flag
    sk = np.asarray(inp["b_sinks"], np.float32)
    for lj in range(2):
        for s in range(12):
            v[:, V_SINK + lj * 12 + s] = sk[lj, SLOT_HEAD[s]]
    return v


class Prog:
    ENGS = ("pe", "act", "dve", "pool", "sp")
    EPOCH = 24000

    def __init__(self):
        self.ops = {e: [] for e in self.ENGS}
        self.cnt = {e: 0 for e in self.ENGS}
        self.scnt = {}
        self.lastw = {}
        self.readers = {}

    def _deps(self, reads, writes):
        deps = {}

        def add(tok):
            key = (tok[0], tok[1])
            if deps.get(key, -1) < tok[2]:
                deps[key] = tok[2]

        for r in reads:
            t = self.lastw.get(r)
            if t is not None:
                add(t)
        for w in writes:
            t = self.lastw.get(w)
            if t is not None:
                add(t)
            for key, idx in self.readers.get(w, {}).items():
                add((key[0], key[1], idx))
        return deps

    def _update(self, tok, reads, writes):
        key = (tok[0], tok[1])
        for r in reads:
            d = self.readers.setdefault(r, {})
            if d.get(key, -1) < tok[2]:
                d[key] = tok[2]
        for w in writes:
            self.lastw[w] = tok
            self.readers[w] = {}

    def op(self, eng, fn, reads=(), writes=()):
        deps = self._deps(reads, writes)
        idx = self.cnt[eng]
        self.cnt[eng] += 1
        tok = ("e", eng, idx)
        self.ops[eng].append((fn, deps, tok))
        self._update(tok, reads, writes)

    def dma(self, eng, stream, fn, reads=(), writes=()):
        deps = self._deps(reads, writes)
        idx = self.scnt.get(stream, 0)
        self.scnt[stream] = idx + 1
        if idx > 0:
            key = ("s", stream)
            if deps.get(key, -1) < idx - 1:
                deps[key] = idx - 1
        tok = ("s", stream, idx)
        self.ops[eng].append((fn, deps, tok))
        self._update(tok, reads, writes)

    def wait_all(self, eng, reads):
        deps = self._deps(reads, ())
        self.ops[eng].append((None, deps, None))

    def emit(self, nc, stack):
        n_ep = {e: max(1, -(-self.cnt[e] // self.EPOCH)) for e in self.ENGS}
        esem = {e: [stack.enter_context(nc.semaphore(f"e_{e}_{i}")) for i in range(n_ep[e])]
                for e in self.ENGS if self.cnt[e] > 0}
        ssem = {s: stack.enter_context(nc.semaphore(f"s_{s}")) for s in self.scnt}
        block = stack.enter_context(nc.Block())
        prog = self

        def run(engname, eng):
            known = {}
            for fn, deps, tok in prog.ops[engname]:
                for key, idx in deps.items():
                    if key[0] == "e" and key[1] == "pe" and engname == "pe":
                        continue
                    if known.get(key, -1) >= idx:
                        continue
                    known[key] = idx
                    if key[0] == "e":
                        eng.wait_ge(esem[key[1]][idx // prog.EPOCH], idx % prog.EPOCH + 1)
                    else:
                        eng.wait_ge(ssem[key[1]], 16 * (idx + 1))
                if fn is None:
                    continue
                ins = fn(eng)
                if tok[0] == "e":
                    ins.then_inc(esem[engname][tok[2] // prog.EPOCH], 1)
                else:
                    ins.then_inc(ssem[tok[1]], 16)

        @block.tensor
        def _(e):
            run("pe", e)

        @block.scalar
        def _(e):
            run("act", e)

        @block.vector
        def _(e):
            run("dve", e)

        @block.gpsimd
        def _(e):
            run("pool", e)

        @block.sync
        def _(e):
            run("sp", e)


class StopTile(Exception):
    pass


class NullProg:
    def op(self, *a, **k):
        pass

    def dma(self, *a, **k):
        pass

    def wait_all(self, *a, **k):
        pass


class Builder:
    def __init__(self, nc, T, nlayers=4, final=True, ntiles=NT):
        self.nc = nc
        self.T = T
        self.nlayers = nlayers
        self.final = final
        self.ntiles = ntiles
        self.table, self.total, self.order = piece_table()

    debug = None

    def dump(self, name, src, n, nch=8):
        if self.debug != name:
            return
        P, T = self.P, self.T
        xb = self.xb
        P.op("act", lambda e: e.activation(out=xb[:, 0:nch, 0:n], in_=src, func=AF.Copy),
             reads=[(self.xr, c) for c in range(8)] + [("h", c) for c in range(8)] + [("y", c) for c in range(8)]
             + [("q", c) for c in range(8)] + [("act", c) for c in range(NF)],
             writes=[(self.xr, c) for c in range(8)])
        raise StopTile()

    def bank(self):
        b = self._bank
        self._bank = (b + 1) % 7
        return b

    def wget(self, name):
        if self.recording:
            self.seq.append(name)
            return 0
        i = self.wpos
        assert self.seq[i] == name, (i, self.seq[i], name)
        P, Tn = self.P, self.T
        hi = min(i + NSLOT - 1, len(self.seq))
        while self.wloaded < hi:
            j = self.wloaded
            s = j % NSLOT
            nm = self.seq[j]
            off, n = self.table[nm]
            slot = Tn["slots"][s]
            if nm in self.converted:
                src = Tn["wscr"][off:off + 128 * n].rearrange("(p n) -> p n", p=128)
                P.dma("sp", f"ld{s}", lambda e, dst=slot[:, 0:n], src=src: e.dma_start(out=dst, in_=src),
                      reads=[("wscr", nm)], writes=[("slot", s)])
            else:
                self.converted.add(nm)
                src32 = Tn["wflat"][off:off + 128 * n].rearrange("(p n) -> p n", p=128)
                for c0 in range(0, n, STG):
                    c1 = min(c0 + STG, n)
                    k = self.stgk
                    self.stgk = (k + 1) % 2
                    stg = Tn["stg"][k]
                    assert self.cur_tile <= 0, "staging aliases the second x buffer"
                    sres = [("xB", 4 * k + q) for q in range(4)]
                    P.dma("sp", f"sg{k}", lambda e, stg=stg, src32=src32, c0=c0, c1=c1: e.dma_start(
                        out=stg[:, 0:c1 - c0], in_=src32[:, c0:c1]),
                        reads=[], writes=sres)
                    P.op("act", lambda e, stg=stg, slot=slot, c0=c0, c1=c1: e.activation(
                        out=slot[:, c0:c1], in_=stg[:, 0:c1 - c0], func=AF.Copy),
                        reads=sres, writes=[("slot", s)])
                dst = Tn["wscr"][off:off + 128 * n].rearrange("(p n) -> p n", p=128)
                P.dma("act", f"ws{self.wsk}", lambda e, dst=dst, slot=slot, n=n: e.dma_start(out=dst, in_=slot[:, 0:n]),
                      reads=[("slot", s)], writes=[("wscr", nm)])
                self.wsk = (self.wsk + 1) % 2
            self.wloaded += 1
        self.wpos += 1
        return i % NSLOT

    def run(self, P, recording, seq=None):
        self.P = P
        self.recording = recording
        self.seq = [] if recording else seq
        self.wpos = 0
        self.wloaded = 0
        self._bank = 0
        self.converted = set()
        self.stgk = 0
        self.wsk = 0
        self.xb, self.xr = self.T["xA"], "xA"
        self.stats_ready = False
        self.rstd_valid = False
        self.hp_ready = False
        self.cur_tile = -1
        self.setup()
        self.tile(-1)
        for ti in range(self.ntiles):
            self.tile(ti)
        P.wait_all("act", [(self.xr, c) for c in range(8)] + ["yT"])
        return self.seq

    def setup(self):
        P, T, nc = self.P, self.T, self.nc
        P.op("dve", lambda e: e.memset(T["onesD"][:, :], 1.0 / D), writes=["onesD"])
        P.op("dve", lambda e: e.memset(T["ones1"][:, :], 1.0), writes=["ones1"])
        P.op("dve", lambda e: e.memset(T["carry"][:, :, :], 0.0), writes=[("carry", i) for i in range(12)])
        P.dma("act", "misc", lambda e: e.dma_start(out=T["vecs"][:, :], in_=T["vecs_d"][:, :]), writes=["vecs"])
        xflat = self.xb[:, :, :].rearrange("p c t -> p (c t)")
        P.dma("act", "misc", lambda e: e.dma_start(out=xflat[:, 0:128], in_=T["ident_d"][:, :]),
              writes=[(self.xr, c) for c in range(8)])
        P.op("act", lambda e: e.activation(out=T["ident"][:, :], in_=xflat[:, 0:128], func=AF.Copy),
             reads=[(self.xr, c) for c in range(8)], writes=["ident"])
        for src_name, dst_name in (("bias_d", "bias8"), ("biasf_d", "biasF")):
            P.dma("act", "misc", lambda e, s=src_name: e.dma_start(out=xflat[:, 0:3072], in_=T[s][:, :]),
                  writes=[(self.xr, c) for c in range(8)])
            dstf = T[dst_name][:, :, :].rearrange("p s n -> p (s n)")
            P.op("act", lambda e, dstf=dstf: e.activation(out=dstf, in_=xflat[:, 0:3072], func=AF.Copy, scale=8.0),
                 reads=[(self.xr, c) for c in range(8)], writes=[dst_name])
        P.op("act", lambda e: e.activation(out=T["esink"][:, :], in_=T["vecs"][:, V_SINK:V_SINK + 24], func=AF.Exp),
             reads=["vecs"], writes=["esink"])
        P.dma("act", "misc",
              lambda e, xb=self.xb: e.dma_start(out=xb[:, :, 0:N_MEM],
                                    in_=T["memT"].rearrange("(c p) t -> p c t", p=128)),
              writes=[(self.xr, c) for c in range(8)])
        self.rmsnorm(N_MEM, G_MEM)
        hb, ps = T["hb"], T["ps"]
        for l in range(4):
            s = self.wget(f"memkv{l}")
            sl = T["slots"][s]
            for i in range(2):
                b = self.bank()
                for k in range(8):
                    P.op("pe", lambda e, b=b, k=k, i=i, sl=sl: e.matmul(
                        ps[b][:, 0:N_MEM], sl[:, k * 512 + i * 128:k * 512 + (i + 1) * 128], hb[:, k, 0:N_MEM],
                        start=(k == 0), stop=(k == 7)),
                        reads=[("slot", s), ("h", k)], writes=[("ps", b)])
                P.op("act", lambda e, b=b, l=l, i=i: e.activation(out=T["mkT"][:, l, i, :], in_=ps[b][:, 0:N_MEM], func=AF.Copy),
                     reads=[("ps", b)], writes=["mk"])
            for mc in range(2):
                b = self.bank()
                for k in range(8):
                    P.op("pe", lambda e, b=b, k=k, mc=mc, sl=sl: e.matmul(
                        ps[b][:, 0:256], hb[:, k, mc * 128:(mc + 1) * 128], sl[:, k * 512 + 256:k * 512 + 512],
                        start=(k == 0), stop=(k == 7)),
                        reads=[("slot", s), ("h", k)], writes=[("ps", b)])
                P.op("act", lambda e, b=b, l=l, mc=mc: e.activation(out=T["mv"][:, l, mc, :], in_=ps[b][:, 0:256], func=AF.Copy),
                     reads=[("ps", b)], writes=["mv"])

    def stat_square(self, m, n):
        P, T = self.P, self.T
        xb, hb = self.xb, T["hb"]
        P.op("act", lambda e: e.activation(out=hb[:, m, 0:n], in_=xb[:, m, 0:n], func=AF.Square),
             reads=[(self.xr, m)], writes=[("h", m)])

    def stat_mm(self, m, n):
        P, T = self.P, self.T
        P.op("pe", lambda e: e.matmul(T["ps"][7][:, 0:n], T["onesD"][:, :], T["hb"][:, m, 0:n],
                                      start=(m == 0), stop=(m == 7)),
             reads=[("h", m), "onesD"], writes=[("ps", 7)])

    def hprime(self, m, n, gcol):
        P, T = self.P, self.T
        xb, qb, vecs = self.xb, T["qb"], T["vecs"]
        P.op("dve", lambda e: e.tensor_scalar(out=qb[:, m, 0:n], in0=xb[:, m, 0:n],
                                              scalar1=vecs[:, gcol + m:gcol + m + 1], scalar2=None, op0=ALU.mult),
             reads=[(self.xr, m), "vecs"], writes=[("q", m)])

    def rmsnorm(self, n, gcol, to_x=False, pre=None, post=None):
        P, T = self.P, self.T
        xb, hb, actb, ps, rstd, vecs = self.xb, T["hb"], T["actb"], T["ps"], T["rstd"], T["vecs"]
        xr = self.xr
        used_hp = False
        if self.hp_ready:
            assert not self.rstd_valid and not to_x
            if pre is not None:
                pre()
            self.stat_mm(7, n)
            if post is not None:
                post()
            self.stats_ready = True
            self.hp_ready = False
            used_hp = pre is not None
        if not self.rstd_valid:
            if not self.stats_ready:
                P.op("act", lambda e: e.activation(out=actb[:, 0:8, 0:n], in_=xb[:, :, 0:n], func=AF.Square),
                     reads=[(xr, c) for c in range(8)], writes=[("act", c) for c in range(8)])
                for c in range(8):
                    P.op("pe", lambda e, c=c: e.matmul(ps[7][:, 0:n], T["onesD"][:, :], actb[:, c, 0:n],
                                                       start=(c == 0), stop=(c == 7)),
                         reads=[("act", c), "onesD"], writes=[("ps", 7)])
            P.op("act", lambda e: e.activation(out=rstd[:, 0:n], in_=ps[7][:, 0:n], func=AF.Ln, bias=T["epsc"][:, 0:1]),
                 reads=[("ps", 7), "epsc"], writes=["rstd"])
            P.op("act", lambda e: e.activation(out=rstd[:, 0:n], in_=rstd[:, 0:n], func=AF.Exp, scale=-0.5),
                 reads=["rstd"], writes=["rstd"])
        self.stats_ready = False
        self.rstd_valid = False
        for c in range(8):
            if to_x:
                P.op("dve", lambda e, c=c: e.scalar_tensor_tensor(
                    out=xb[:, c, 0:n], in0=xb[:, c, 0:n], scalar=vecs[:, gcol + c:gcol + c + 1], in1=rstd[:, 0:n],
                    op0=ALU.mult, op1=ALU.mult),
                    reads=[(xr, c), "rstd", "vecs"], writes=[(xr, c)])
            else:
                P.op("dve", lambda e, c=c: e.scalar_tensor_tensor(
                    out=hb[:, c, 0:n], in0=xb[:, c, 0:n], scalar=vecs[:, gcol + c:gcol + c + 1], in1=rstd[:, 0:n],
                    op0=ALU.mult, op1=ALU.mult),
                    reads=[(xr, c), "rstd", "vecs"], writes=[("h", c)])
        return used_hp

    def proj_chunk(self, s, col0, kstride, nk, rhs_t, rhs_res, n):
        P, T = self.P, self.T
        ps, sl = T["ps"], T["slots"][s]
        b = self.bank()
        for k in range(nk):
            P.op("pe", lambda e, k=k: e.matmul(ps[b][:, 0:n], sl[:, k * kstride + col0:k * kstride + col0 + 128],
                                               rhs_t[:, k, 0:n], start=(k == 0), stop=(k == nk - 1)),
                 reads=[("slot", s), (rhs_res, k)], writes=[("ps", b)])
        return b

    def residual_add(self, b, m, n):
        P, T = self.P, self.T
        xb, ps = self.xb, T["ps"]
        xr = self.xr
        P.op("dve", lambda e: e.tensor_tensor(out=xb[:, m, 0:n], in0=ps[b][:, 0:n], in1=xb[:, m, 0:n], op=ALU.add),
             reads=[("ps", b), (xr, m)], writes=[(xr, m)])

    def a_mixer(self, l, n, halo):
        P, T = self.P, self.T
        ps, hb, qb, yb, vecs = T["ps"], T["hb"], T["qb"], T["yb"], T["vecs"]
        order = []
        for j in range(6):
            order += [("u", j), ("C", j), ("B", j)]
        order += [("q", 0), ("q", 1)]
        rstd = T["rstd"]
        s0 = self.wget(f"ain{l}_0")
        hpb = {}

        def pre():
            hpb[0] = self.proj_chunk(s0, 0, 512, 8, qb, "q", n)

        def post():
            hpb[1] = self.proj_chunk(s0, 128, 512, 8, qb, "q", n)

        used_hp = self.rmsnorm(n, G_MIX + 8 * l, pre=pre, post=post)
        self.dump(f"h{l}", hb[:, :, 0:n], n)
        for p in range(5):
            s = s0 if p == 0 else self.wget(f"ain{l}_{p}")
            for jj in range(4):
                kind, j = order[p * 4 + jj]
                if used_hp and p == 0 and jj < 2:
                    b = hpb[jj]
                else:
                    b = self.proj_chunk(s, jj * 128, 512, 8, hb, "h", n)
                r = j % 2
                usb, vb, accb = T["usb"][r], T["vb"][r], T["accb"][r]
                if kind == "u" and used_hp and j == 0:
                    P.op("dve", lambda e, b=b, usb=usb: e.tensor_tensor(
                        out=usb[:, 0:n], in0=ps[b][:, 0:n], in1=rstd[:, 0:n], op=ALU.mult),
                        reads=[("ps", b), "rstd"], writes=[("usb", r)])
                    P.op("dve", lambda e, usb=usb: e.tensor_tensor(
                        out=usb[:, 0:n], in0=usb[:, 0:n], in1=rstd[:, 0:n], op=ALU.mult),
                        reads=[("usb", r), "rstd"], writes=[("usb", r)])
                elif kind == "u":
                    P.op("act", lambda e, b=b, usb=usb: e.activation(out=usb[:, 0:n], in_=ps[b][:, 0:n], func=AF.Copy),
                         reads=[("ps", b)], writes=[("usb", r)])
                elif kind == "C":
                    ci = l * 6 + j
                    cw = V_CONV + ci * 3
                    P.op("dve", lambda e, vb=vb, ci=ci: e.tensor_copy(out=vb[:, 0:2], in_=T["carry"][:, ci, :]),
                         reads=[("carry", ci)], writes=[("vb", r)])
                    P.op("dve", lambda e, b=b, vb=vb, usb=usb: e.tensor_tensor(
                        out=vb[:, 2:2 + n], in0=ps[b][:, 0:n], in1=usb[:, 0:n], op=ALU.mult),
                        reads=[("ps", b), ("usb", r)], writes=[("vb", r)])
                    if halo:
                        P.op("dve", lambda e, vb=vb, ci=ci: e.tensor_scalar(
                            out=T["carry"][:, ci, :], in0=vb[:, n:n + 2], scalar1=vecs[:, V_FLAG:V_FLAG + 1],
                            scalar2=None, op0=ALU.mult),
                            reads=[("vb", r), "vecs"], writes=[("carry", ci)])
                    else:
                        P.op("dve", lambda e, vb=vb, ci=ci: e.tensor_copy(out=T["carry"][:, ci, :], in_=vb[:, n:n + 2]),
                             reads=[("vb", r)], writes=[("carry", ci)])
                    P.op("dve", lambda e, vb=vb, accb=accb, cw=cw: e.tensor_scalar(
                        out=accb[:, 0:n], in0=vb[:, 2:2 + n], scalar1=vecs[:, cw + 2:cw + 3], scalar2=None, op0=ALU.mult),
                        reads=[("vb", r), "vecs"], writes=[("accb", r)])
                    P.op("dve", lambda e, vb=vb, accb=accb, cw=cw: e.scalar_tensor_tensor(
                        out=accb[:, 0:n], in0=vb[:, 1:1 + n], scalar=vecs[:, cw + 1:cw + 2], in1=accb[:, 0:n],
                        op0=ALU.mult, op1=ALU.add),
                        reads=[("vb", r), ("accb", r), "vecs"], writes=[("accb", r)])
                    P.op("dve", lambda e, vb=vb, accb=accb, cw=cw: e.scalar_tensor_tensor(
                        out=accb[:, 0:n], in0=vb[:, 0:n], scalar=vecs[:, cw:cw + 1], in1=accb[:, 0:n],
                        op0=ALU.mult, op1=ALU.add),
                        reads=[("vb", r), ("accb", r), "vecs"], writes=[("accb", r)])
                elif kind == "B":
                    P.op("dve", lambda e, b=b, j=j, accb=accb: e.tensor_tensor(
                        out=yb[:, j, 0:n], in0=ps[b][:, 0:n], in1=accb[:, 0:n], op=ALU.mult),
                        reads=[("ps", b), ("accb", r)], writes=[("y", j)])
                else:
                    P.op("act", lambda e, b=b, j=j: e.activation(out=qb[:, 6 + j, 0:n], in_=ps[b][:, 0:n], func=AF.Copy),
                         reads=[("ps", b)], writes=[("q", 6 + j)])

    def mem_attn(self, l, n):
        P, T = self.P, self.T
        ps, qb, yb, pm, rc = T["ps"], T["qb"], T["yb"], T["pm"], T["rc"]
        mkT, mv = T["mkT"], T["mv"]

        def scores(hm):
            i, half = hm // 2, hm % 2
            r = hm % 2
            lo = half * 64
            for mc in range(2):
                b = self.bank()
                P.op("pe", lambda e, b=b, mc=mc: e.matmul(
                    ps[b][:, 0:n], mkT[lo:lo + 64, l, i, mc * 128:(mc + 1) * 128], qb[lo:lo + 64, 6 + i, 0:n],
                    start=True, stop=True),
                    reads=["mk", ("q", 6 + i)], writes=[("ps", b)])
                P.op("act", lambda e, b=b, mc=mc: e.activation(out=pm[r][:, mc, 0:n], in_=ps[b][:, 0:n],
                                                               func=AF.Exp, scale=0.125),
                     reads=[("ps", b)], writes=[("pm", r, mc)])

        def pv(hm):
            i, half = hm // 2, hm % 2
            r = hm % 2
            lo = half * 64
            bo, bd = self.bank(), self.bank()
            for mc in range(2):
                P.op("pe", lambda e, mc=mc: e.matmul(ps[bo][:, 0:n], mv[:, l, mc, i * 128:(i + 1) * 128],
                                                     pm[r][:, mc, 0:n], start=(mc == 0), stop=(mc == 1)),
                     reads=["mv", ("pm", r, mc)], writes=[("ps", bo)])
            for mc in range(2):
                P.op("pe", lambda e, mc=mc: e.matmul(ps[bd][:, 0:n], T["ones1"][:, :], pm[r][:, mc, 0:n],
                                                     start=(mc == 0), stop=(mc == 1)),
                     reads=["ones1", ("pm", r, mc)], writes=[("ps", bd)])
            P.op("act", lambda e: e.activation(out=rc[r][lo:lo + 64, 0:n], in_=ps[bd][lo:lo + 64, 0:n], func=AF.Ln),
                 reads=[("ps", bd)], writes=[("rc", r)])
            P.op("act", lambda e: e.activation(out=rc[r][lo:lo + 64, 0:n], in_=rc[r][lo:lo + 64, 0:n], func=AF.Exp,
                                               scale=-1.0),
                 reads=[("rc", r)], writes=[("rc", r)])
            P.op("dve", lambda e: e.tensor_tensor(out=yb[lo:lo + 64, 6 + i, 0:n], in0=ps[bo][lo:lo + 64, 0:n],
                                                  in1=rc[r][lo:lo + 64, 0:n], op=ALU.mult),
                 reads=[("ps", bo), ("rc", r)], writes=[("y", 6 + i)])

        for hm in range(5):
            if hm < 4:
                scores(hm)
            if hm >= 1:
                pv(hm - 1)

    def out_proj(self, prefix, n, next_gcol):
        T = self.T
        for p in range(2):
            s = self.wget(f"{prefix}_{p}")
            for jj in range(4):
                m = p * 4 + jj
                b = self.proj_chunk(s, jj * 128, 512, 8, T["yb"], "y", n)
                if m >= 1:
                    self.stat_mm(m - 1, n)
                self.residual_add(b, m, n)
                self.stat_square(m, n)
                self.hprime(m, n, next_gcol)
        self.hp_ready = True

    def ffn(self, l, n, next_gcol):
        P, T = self.P, self.T
        ps, hb, actb = T["ps"], T["hb"], T["actb"]
        qb, rstd = T["qb"], T["rstd"]
        sg0 = self.wget(f"g{l}_0")
        su0 = self.wget(f"u{l}_0")
        hpb = {}

        def pre():
            hpb["g"] = self.proj_chunk(sg0, 0, 512, 8, qb, "q", n)

        def post():
            hpb["u"] = self.proj_chunk(su0, 0, 512, 8, qb, "q", n)

        used_hp = self.rmsnorm(n, G_FFN + 8 * l, pre=pre, post=post)
        self.dump(f"ffnh{l}", hb[:, :, 0:n], n)
        for fg in range(6):
            ncol = 512 if fg < 5 else 256
            sg = sg0 if fg == 0 else self.wget(f"g{l}_{fg}")
            su = su0 if fg == 0 else self.wget(f"u{l}_{fg}")
            for jj in range(ncol // 128):
                f = fg * 4 + jj
                r = f % 2
                usb = T["usb"][r]
                if f == 0 and used_hp:
                    bg, bu = hpb["g"], hpb["u"]
                    tmp = T["accb"][0]
                    P.op("dve", lambda e, bg=bg, usb=usb: e.tensor_tensor(
                        out=usb[:, 0:n], in0=ps[bg][:, 0:n], in1=rstd[:, 0:n], op=ALU.mult),
                        reads=[("ps", bg), "rstd"], writes=[("usb", r)])
                    P.op("act", lambda e, usb=usb: e.activation(out=usb[:, 0:n], in_=usb[:, 0:n], func=AF.Silu),
                         reads=[("usb", r)], writes=[("usb", r)])
                    P.op("dve", lambda e, bu=bu, tmp=tmp: e.tensor_tensor(
                        out=tmp[:, 0:n], in0=ps[bu][:, 0:n], in1=rstd[:, 0:n], op=ALU.mult),
                        reads=[("ps", bu), "rstd"], writes=[("accb", 0)])
                    P.op("dve", lambda e, tmp=tmp, usb=usb: e.tensor_tensor(
                        out=actb[:, 0, 0:n], in0=tmp[:, 0:n], in1=usb[:, 0:n], op=ALU.mult),
                        reads=[("accb", 0), ("usb", r)], writes=[("act", 0)])
                    continue
                bg = self.proj_chunk(sg, jj * 128, ncol, 8, hb, "h", n)
                bu = self.proj_chunk(su, jj * 128, ncol, 8, hb, "h", n)
                P.op("act", lambda e, bg=bg, usb=usb: e.activation(out=usb[:, 0:n], in_=ps[bg][:, 0:n], func=AF.Silu),
                     reads=[("ps", bg)], writes=[("usb", r)])
                P.op("dve", lambda e, bu=bu, usb=usb, f=f: e.tensor_tensor(
                    out=actb[:, f, 0:n], in0=ps[bu][:, 0:n], in1=usb[:, 0:n], op=ALU.mult),
                    reads=[("ps", bu), ("usb", r)], writes=[("act", f)])
        self.dump(f"act{l}", actb[:, 0:8, 0:n], n)
        for m in range(8):
            s = self.wget(f"d{l}_{m}")
            b = self.proj_chunk(s, 0, 128, NF, actb, "act", n)
            if m >= 1:
                self.stat_mm(m - 1, n)
            self.residual_add(b, m, n)
            self.stat_square(m, n)
            if next_gcol is not None:
                self.hprime(m, n, next_gcol)
        if next_gcol is not None:
            self.hp_ready = True
        else:
            self.stat_mm(7, n)
            self.stats_ready = True

    def kv_proj(self, n, c0, blk0):
        P, T = self.P, self.T
        ps, hb, kT, vS = T["ps"], T["hb"], T["kT"], T["vS"]
        s = self.wget("kv")
        sl = T["slots"][s]
        qb, rstd = T["qb"], T["rstd"]
        hpb = {}

        def pre():
            hpb[0] = self.proj_chunk(s, 0, 512, 8, qb, "q", n)

        def post():
            hpb[1] = self.proj_chunk(s, 128, 512, 8, qb, "q", n)

        used_hp = self.rmsnorm(n, G_KV, pre=pre, post=post)
        self.rstd_valid = True
        nvalid = n - c0
        nblk = nvalid // 128
        for pr in range(2):
            if used_hp:
                b = hpb[pr]
                P.op("dve", lambda e, b=b, pr=pr: e.tensor_tensor(
                    out=kT[:, pr, blk0 * 128:blk0 * 128 + nvalid], in0=ps[b][:, c0:n], in1=rstd[:, c0:n], op=ALU.mult),
                    reads=[("ps", b), "rstd"], writes=[("kT", blk0 + t) for t in range(nblk)])
                continue
            b = self.proj_chunk(s, pr * 128, 512, 8, hb, "h", n)
            P.op("act", lambda e, b=b, pr=pr: e.activation(
                out=kT[:, pr, blk0 * 128:blk0 * 128 + nvalid], in_=ps[b][:, c0:n], func=AF.Copy),
                reads=[("ps", b)], writes=[("kT", blk0 + t) for t in range(nblk)])
        for tb in range(nblk):
            b = self.bank()
            for k in range(8):
                P.op("pe", lambda e, b=b, k=k, tb=tb: e.matmul(
                    ps[b][:, 0:256], hb[:, k, c0 + tb * 128:c0 + (tb + 1) * 128], sl[:, k * 512 + 256:k * 512 + 512],
                    start=(k == 0), stop=(k == 7)),
                    reads=[("slot", s), ("h", k)], writes=[("ps", b)])
            P.op("act", lambda e, b=b, tb=tb: e.activation(out=vS[:, blk0 + tb, :], in_=ps[b][:, 0:256], func=AF.Copy),
                 reads=[("ps", b)], writes=[("vS", blk0 + tb)])

    def swa(self, lj, ti):
        P, T = self.P, self.T
        ps, qb, yb, pS, rc = T["ps"], T["qb"], T["yb"], T["pS"], T["rc"]
        kT, vS, bias8, biasF, ident = T["kT"], T["vS"], T["bias8"], T["biasF"], T["ident"]

        def scores(s):
            c, half = s // 2, s % 2
            lo = half * 64
            pair = c // 3
            r = s % 3
            for hbk in range(2):
                b = self.bank()
                if ti == 0 and hbk == 0:
                    P.op("pe", lambda e, b=b: e.matmul(ps[b][:, 0:256], ident[:, :], biasF[:, s, :],
                                                       start=True, stop=False, skip_group_check=True),
                         reads=["ident", "biasF"], writes=[("ps", b)])
                    P.op("pe", lambda e, b=b: e.matmul(ps[b][:, 256:512], ident[:, :], bias8[:, s, :],
                                                       start=False, stop=False, skip_group_check=True),
                         reads=["ident", "bias8"], writes=[("ps", b)])
                    sgc = True
                else:
                    P.op("pe", lambda e, b=b: e.matmul(
                        ps[b][:, :].rearrange("p (a n) -> p a n", a=2), ident[:, :],
                        bias8[:, s:s + 1, :].broadcast_to([128, 2, 256]),
                        start=True, stop=False),
                        reads=["ident", "bias8"], writes=[("ps", b)])
                    sgc = False
                for qq in range(2):
                    qi = 2 * hbk + qq
                    nblk = ti * 4 + qi
                    for kb in range(2):
                        blk = nblk + kb
                        last = (qq == 1 and kb == 1)
                        P.op("pe", lambda e, b=b, qq=qq, kb=kb, blk=blk, qi=qi, last=last, sgc=sgc: e.matmul(
                            ps[b][:, qq * 256 + kb * 128:qq * 256 + (kb + 1) * 128],
                            kT[lo:lo + 64, pair, blk * 128:(blk + 1) * 128],
                            qb[lo:lo + 64, c, qi * 128:(qi + 1) * 128],
                            start=False, stop=last, skip_group_check=sgc),
                            reads=[("kT", blk), ("q", c)], writes=[("ps", b)])
                P.op("act", lambda e, b=b, hbk=hbk: e.activation(out=pS[r][:, hbk * 512:(hbk + 1) * 512], in_=ps[b][:, :],
                                                                 func=AF.Exp, scale=0.125),
                     reads=[("ps", b)], writes=[("pS", r, hbk)])

        def pv(s):
            c, half = s // 2, s % 2
            lo = half * 64
            pair = c // 3
            r = s % 2
            rp = s % 3
            bo, bd = self.bank(), self.bank()
            for qi in range(4):
                nblk = ti * 4 + qi
                for kb in range(2):
                    blk = nblk + kb
                    P.op("pe", lambda e, qi=qi, kb=kb, blk=blk: e.matmul(
                        ps[bo][:, qi * 128:(qi + 1) * 128], vS[:, blk, pair * 128:(pair + 1) * 128],
                        pS[rp][:, qi * 256 + kb * 128:qi * 256 + (kb + 1) * 128],
                        start=(kb == 0), stop=(kb == 1)),
                        reads=[("vS", blk), ("pS", rp, qi // 2)], writes=[("ps", bo)])
            for kb in range(2):
                P.op("pe", lambda e, kb=kb: e.matmul(
                    ps[bd][:, :].rearrange("p (a q) -> p a q", a=4), T["ones1"][:, :],
                    pS[rp][:, :].rearrange("p (a b q) -> p a b q", a=4, b=2)[:, :, kb, :],
                    start=(kb == 0), stop=(kb == 1)),
                    reads=["ones1", ("pS", rp, 0), ("pS", rp, 1)], writes=[("ps", bd)])
            es = T["esink"][lo:lo + 64, lj * 12 + s:lj * 12 + s + 1]
            P.op("act", lambda e: e.activation(out=rc[r][lo:lo + 64, :], in_=ps[bd][lo:lo + 64, :], func=AF.Ln, bias=es),
                 reads=[("ps", bd), "esink"], writes=[("rc", r)])
            P.op("act", lambda e: e.activation(out=rc[r][lo:lo + 64, :], in_=rc[r][lo:lo + 64, :], func=AF.Exp,
                                               scale=-1.0),
                 reads=[("rc", r)], writes=[("rc", r)])
            P.op("dve", lambda e: e.tensor_tensor(out=yb[lo:lo + 64, c, :], in0=ps[bo][lo:lo + 64, :],
                                                  in1=rc[r][lo:lo + 64, :], op=ALU.mult),
                 reads=[("ps", bo), ("rc", r)], writes=[("y", c)])

        for s in range(14):
            if s < 12:
                scores(s)
            if s >= 2:
                pv(s - 2)

    def b_layer(self, lj, ti):
        P, T = self.P, self.T
        l = 2 + lj
        n = TT
        ps, hb, qb = T["ps"], T["hb"], T["qb"]
        rstd = T["rstd"]
        s0 = self.wget(f"bq{lj}_0")
        hpb = {}

        def pre():
            hpb[0] = self.proj_chunk(s0, 0, 512, 8, qb, "q", n)

        def post():
            hpb[1] = self.proj_chunk(s0, 128, 512, 8, qb, "q", n)

        used_hp = self.rmsnorm(n, G_MIX + 8 * l, pre=pre, post=post)
        for p in range(2):
            s = s0 if p == 0 else self.wget(f"bq{lj}_{p}")
            for jj in range(4):
                c = p * 4 + jj
                if used_hp and c < 2:
                    b = hpb[c]
                    P.op("dve", lambda e, b=b, c=c: e.tensor_tensor(
                        out=qb[:, c, 0:n], in0=ps[b][:, 0:n], in1=rstd[:, 0:n], op=ALU.mult),
                        reads=[("ps", b), "rstd"], writes=[("q", c)])
                    continue
                b = self.proj_chunk(s, jj * 128, 512, 8, hb, "h", n)
                P.op("act", lambda e, b=b, c=c: e.activation(out=qb[:, c, 0:n], in_=ps[b][:, 0:n], func=AF.Copy),
                     reads=[("ps", b)], writes=[("q", c)])
        self.swa(lj, ti)
        self.mem_attn(l, n)
        self.out_proj(f"bout{lj}", n, G_FFN + 8 * l)
        self.ffn(l, n, self.next_gcol(l))

    def next_gcol(self, l):
        if l + 1 >= self.nlayers:
            return None
        return G_KV if l == 1 else G_MIX + 8 * (l + 1)

    def tile_body(self, ti, halo, n):
        T = self.T
        na = min(self.nlayers, 2)
        for l in range(na):
            self.a_mixer(l, n, halo)
            self.dump(f"amix{l}", T["yb"][:, :, 0:n], n)
            self.mem_attn(l, n)
            self.dump(f"y{l}", T["yb"][:, :, 0:n], n)
            self.out_proj(f"aout{l}", n, G_FFN + 8 * l)
            self.dump(f"xmix{l}", self.xb[:, :, 0:n], n)
            self.ffn(l, n, self.next_gcol(l))
        if self.nlayers > 2:
            if halo:
                self.kv_proj(n, HALO - 128, 0)
            else:
                self.kv_proj(n, 0, 1 + ti * 4)
        if halo:
            return
        for lj in range(self.nlayers - 2):
            self.b_layer(lj, ti)
        if self.final:
            self.rmsnorm(n, G_FIN, to_x=True)

    def xbuf_for(self, ti):
        if ti <= 0 or ti % 2 == 0:
            return self.T["xA"], "xA"
        return self.T["xB"], "xB"

    def emit_xload(self, ti):
        P, T = self.P, self.T
        halo = ti < 0
        n = HALO if halo else TT
        t0 = 0 if halo else HALO + ti * TT
        xb, xr = self.xbuf_for(ti)
        xsrc = T["xT"].rearrange("(c p) t -> p c t", p=128)[:, :, t0:t0 + n]
        P.dma("sp", "xl", lambda e: e.dma_start(out=xb[:, :, 0:n], in_=xsrc),
              reads=[], writes=[(xr, c) for c in range(8)])

    def tile(self, ti):
        P, T = self.P, self.T
        halo = ti < 0
        n = HALO if halo else TT
        self.cur_tile = ti
        self.xb, self.xr = self.xbuf_for(ti)
        self.stats_ready = False
        self.rstd_valid = False
        self.hp_ready = False
        if halo:
            self.emit_xload(-1)
        if ti >= 1 and ti + 1 < self.ntiles:
            self.emit_xload(ti + 1)
        try:
            self.tile_body(ti, halo, n)
        except StopTile:
            pass
        if halo:
            self.emit_xload(0)
            return
        xb, xr = self.xb, self.xr
        ydst = T["yT"].rearrange("(c p) t -> p c t", p=128)[:, :, ti * TT:(ti + 1) * TT]
        P.dma("act", "st", lambda e: e.dma_start(out=ydst, in_=xb[:, :, 0:n]),
              reads=[(xr, c) for c in range(8)], writes=["yT"])
        if ti == 0 and self.ntiles > 1:
            self.emit_xload(1)


def build_nc(nlayers=4, final=True, ntiles=NT, debug=None):
    nc = bass.Bass("TRN2", target_bir_lowering=False)
    table, total, order = piece_table()
    T = {}
    T["xT"] = nc.dram_tensor("xT", [D, HALO + TOK], F32, kind="ExternalInput").ap()
    T["memT"] = nc.dram_tensor("memT", [D, N_MEM], F32, kind="ExternalInput").ap()
    T["vecs_d"] = nc.dram_tensor("vecs", [128, NV], F32, kind="ExternalInput").ap()
    T["bias_d"] = nc.dram_tensor("biasT", [128, 3072], F32, kind="ExternalInput").ap()
    T["biasf_d"] = nc.dram_tensor("biasF", [128, 3072], F32, kind="ExternalInput").ap()
    T["ident_d"] = nc.dram_tensor("ident", [128, 128], F32, kind="ExternalInput").ap()
    T["wflat"] = nc.dram_tensor("wflat", [total], F32, kind="ExternalInput").ap()
    T["wscr"] = nc.dram_tensor("wscr", [total], BF16, kind="Internal").ap()
    T["yT"] = nc.dram_tensor("yT", [D, TOK], F32, kind="ExternalOutput").ap()

    stack = ExitStack()
    with stack:
        def sb(name, shape, dt):
            return stack.enter_context(nc.sbuf_tensor(name, shape, dt))

        T["xA"] = sb("xb", [128, 8, TT], F32)
        T["hb"] = sb("hb", [128, 8, TT], BF16)
        T["qb"] = sb("qb", [128, 8, TT], BF16)
        T["yb"] = sb("yb", [128, 8, TT], BF16)
        T["actb"] = sb("actb", [128, NF, TT], BF16)
        T["usb"] = [sb(f"usb{i}", [128, TT], F32) for i in range(2)]
        T["vb"] = [sb(f"vb{i}", [128, TT + 2], F32) for i in range(2)]
        T["accb"] = [sb(f"accb{i}", [128, TT], F32) for i in range(2)]
        T["rstd"] = sb("rstd", [128, TT], F32)
        T["pS"] = [sb(f"pS{i}", [128, 1024], BF16) for i in range(3)]
        T["pm"] = [sb(f"pm{i}", [128, 2, TT], BF16) for i in range(2)]
        T["rc"] = [sb(f"rc{i}", [128, TT], F32) for i in range(2)]
        T["kT"] = sb("kT", [128, 2, NBLK * 128], BF16)
        T["vS"] = sb("vS", [128, NBLK, 256], BF16)
        T["bias8"] = sb("bias8_sb", [128, 12, 256], BF16)
        T["biasF"] = sb("biasF_sb", [128, 12, 256], BF16)
        T["mkT"] = sb("mkT", [128, 4, 2, N_MEM], BF16)
        T["mv"] = sb("mv", [128, 4, 2, 256], BF16)
        T["vecs"] = sb("vecs_sb", [128, NV], F32)
        T["esink"] = sb("esink", [128, 24], F32)
        T["carry"] = sb("carry", [128, 12, 2], F32)
        T["onesD"] = sb("onesD", [128, 128], BF16)
        T["ones1"] = sb("ones1", [128, 128], BF16)
        T["ident"] = sb("ident_sb", [128, 128], BF16)
        T["epsc"] = sb("epsc", [128, 1], F32)
        T["slots"] = [sb(f"slot{i}", [128, SLOT_ELEMS], BF16) for i in range(NSLOT)]
        T["xB"] = sb("xB", [128, 8, TT], F32)
        xbf = T["xB"][:, :, :].rearrange("p c t -> p (c t)")
        T["stg"] = [xbf[:, i * STG:(i + 1) * STG] for i in range(2)]
        T["ps"] = [stack.enter_context(nc.psum_tensor(f"ps{i}", [128, 512], F32)) for i in range(8)]

        bld = Builder(nc, T, nlayers=nlayers, final=final, ntiles=ntiles)
        bld.debug = debug
        seq = bld.run(NullProg(), recording=True)
        P = Prog()
        P.op("dve", lambda e: e.memset(T["epsc"][:, :], EPS), writes=["epsc"])
        bld.run(P, recording=False, seq=seq)
        P.emit(nc, stack)
    return nc


def make_in_maps(inp):
    x = np.asarray(inp["x"], np.float32)
    mem = np.asarray(inp["mem"], np.float32)
    wflat = pack_weights(inp)
    biasT, biasFirst = make_bias_tables(inp["rel_bias"])
    ident = np.eye(128, dtype=np.float32)
    in_maps = []
    for c in range(NCORE):
        b, qtr = c // 4, c % 4
        xt = np.zeros((D, HALO + TOK), np.float32)
        lo = qtr * TOK
        if qtr > 0:
            xt[:, :] = x[b, lo - HALO:lo + TOK, :].T
        else:
            xt[:, HALO:] = x[b, lo:lo + TOK, :].T
        in_maps.append({
            "xT": xt,
            "memT": np.ascontiguousarray(mem[b].T),
            "vecs": make_vecs(inp, 0.0 if qtr == 0 else 1.0),
            "biasT": biasT,
            "biasF": biasFirst if qtr == 0 else biasT,
            "ident": ident,
            "wflat": wflat,
        })
    return in_maps


_NC_CACHE = {}


def kernel(**inputs):
    in_maps = make_in_maps(inputs)
    if "nc" not in _NC_CACHE:
        _NC_CACHE["nc"] = build_nc()
    nc = _NC_CACHE["nc"]
    res = run_bass_kernel_spmd(nc, in_maps, core_ids=list(range(NCORE)))
    out = np.empty((BATCH, SEQ, D), np.float32)
    for c in range(NCORE):
        b, qtr = c // 4, c % 4
        out[b, qtr * TOK:(qtr + 1) * TOK, :] = res.results[c]["yT"].T
    return out
```

```python
import math
from contextlib import ExitStack

import numpy as np
import concourse.bass as bass
import concourse.mybir as mybir
from concourse.bass_utils import run_bass_kernel_spmd

F32 = mybir.dt.float32
BF16 = mybir.dt.bfloat16
AF = mybir.ActivationFunctionType
ALU = mybir.AluOpType

D = 1024
NCH = 8
SEQ = 16384
BATCH = 2
NCORE = 8
TOK = 4096
HALO = 136
TT = 512
NT = TOK // TT
DFF = 2816
NF = DFF // 128
N_MEM = 256
NBLK = TOK // 128 + 1
EPS = 1e-5
NEG = -30000.0
NSLOT = 5
SLOT_ELEMS = 4096
STG = 2048

HEAD_LO = [0, 1, 2, 6, 7, 8]
HEAD_HI = [3, 4, 5, 9, 10, 11]
SLOT_HEAD = [HEAD_LO[s // 2] if s % 2 == 0 else HEAD_HI[s // 2] for s in range(12)]

G_MIX, G_FFN, G_KV, G_MEM, G_FIN = 0, 32, 64, 72, 80
V_CONV = 88
V_FLAG = 124
V_SINK = 125
NV = 152


def piece_table():
    names = []
    for i in range(4):
        names.append((f"memkv{i}", 4096))
    for l in range(2):
        for p in range(5):
            names.append((f"ain{l}_{p}", 4096))
        for p in range(2):
            names.append((f"aout{l}_{p}", 4096))
        for fg in range(6):
            n = 4096 if fg < 5 else 2048
            names.append((f"g{l}_{fg}", n))
            names.append((f"u{l}_{fg}", n))
        for m in range(8):
            names.append((f"d{l}_{m}", NF * 128))
    names.append(("kv", 4096))
    for j in range(2):
        l = 2 + j
        for p in range(2):
            names.append((f"bq{j}_{p}", 4096))
        for p in range(2):
            names.append((f"bout{j}_{p}", 4096))
        for fg in range(6):
            n = 4096 if fg < 5 else 2048
            names.append((f"g{l}_{fg}", n))
            names.append((f"u{l}_{fg}", n))
        for m in range(8):
            names.append((f"d{l}_{m}", NF * 128))
    table = {}
    off = 0
    for nm, n in names:
        table[nm] = (off, n)
        off += 128 * n
    return table, off, [nm for nm, _ in names]


def _as_piece(w2d):
    K, Fc = w2d.shape
    kc = K // 128
    return np.ascontiguousarray(w2d.reshape(kc, 128, Fc).transpose(1, 0, 2)).reshape(128, kc * Fc)


def pack_weights(inp):
    table, total, _ = piece_table()
    wflat = np.empty((total,), np.float32)

    def put(name, w2d):
        off, n = table[name]
        blk = _as_piece(np.asarray(w2d, np.float32))
        assert blk.shape == (128, n), (name, blk.shape, n)
        wflat[off:off + 128 * n] = blk.reshape(-1)

    for i in range(4):
        put(f"memkv{i}", inp["w_mem_kv"][i])
    a_chunks = [18, 19]
    for j in range(6):
        a_chunks += [j, 12 + j, 6 + j]
    a_cols = np.concatenate([np.arange(c * 128, (c + 1) * 128) for c in a_chunks])
    qperm = np.concatenate([np.arange(h * 64, (h + 1) * 64) for h in SLOT_HEAD] + [np.arange(768, 1024)])
    for l in range(4):
        for fg in range(6):
            lo, hi = fg * 512, min((fg + 1) * 512, DFF)
            put(f"g{l}_{fg}", inp["w_gate"][l][:, lo:hi])
            put(f"u{l}_{fg}", inp["w_up"][l][:, lo:hi])
        for m in range(8):
            put(f"d{l}_{m}", inp["w_down"][l][:, m * 128:(m + 1) * 128])
    for l in range(2):
        wp = np.asarray(inp["a_w_in"][l])[:, a_cols]
        for p in range(5):
            put(f"ain{l}_{p}", wp[:, p * 512:(p + 1) * 512])
        for p in range(2):
            put(f"aout{l}_{p}", inp["a_w_out"][l][:, p * 512:(p + 1) * 512])
    put("kv", inp["w_kv"])
    for j in range(2):
        wq = np.asarray(inp["b_w_q"][j])[:, qperm]
        wo = np.asarray(inp["b_w_out"][j])[qperm, :]
        for p in range(2):
            put(f"bq{j}_{p}", wq[:, p * 512:(p + 1) * 512])
            put(f"bout{j}_{p}", wo[:, p * 512:(p + 1) * 512])
    return wflat


def _rel_bucket_np(dist):
    max_exact = 16
    d = np.maximum(dist, 1).astype(np.float32)
    large = max_exact + (np.log(d / max_exact) / math.log(128 / max_exact) * (32 - max_exact)).astype(np.int32)
    large = np.minimum(large, 31)
    return np.where(dist < max_exact, dist, large)


def make_bias_tables(rel_bias):
    rel_bias = np.asarray(rel_bias, np.float32)
    kj = np.arange(128)[:, None, None]
    kb = np.arange(2)[None, :, None]
    qi = np.arange(128)[None, None, :]
    dist = qi + 128 - (kb * 128 + kj)
    inwin = (dist >= 0) & (dist < 128)
    bucket = _rel_bucket_np(np.maximum(dist, 0))
    fill = np.float32(NEG / 8.0)
    out = np.empty((128, 12, 2, 128), np.float32)
    for s in range(12):
        g = rel_bias[bucket, SLOT_HEAD[s]]
        out[:, s] = np.where(inwin, g, fill)
    first = out.copy()
    first[:, :, 0, :] = fill
    return out.reshape(128, 12 * 256), first.reshape(128, 12 * 256)


def make_vecs(inp, flag):
    v = np.zeros((128, NV), np.float32)

    def putg(col, g):
        v[:, col:col + 8] = np.asarray(g, np.float32).reshape(8, 128).T

    for l in range(4):
        putg(G_MIX + 8 * l, inp["norm_mix"][l])
        putg(G_FFN + 8 * l, inp["norm_ffn"][l])
    putg(G_KV, inp["kv_norm"])
    putg(G_MEM, inp["mem_norm"])
    putg(G_FIN, inp["final_norm"])
    cw = np.asarray(inp["a_conv_w"], np.float32)
    for l in range(2):
        for j in range(6):
            for t in range(3):
                v[:, V_CONV + (l * 6 + j) * 3 + t] = cw[l, t, j * 128:(j + 1) * 128]
    v[:, V_FLAG] = flag
    sk = np.asarray(inp["b_sinks"], np.float32)
    for lj in range(2):
        for s in range(12):
            v[:, V_SINK + lj * 12 + s] = sk[lj, SLOT_HEAD[s]]
    return v


class Prog:
    ENGS = ("pe", "act", "dve", "pool", "sp")
    EPOCH = 24000

    def __init__(self):
        self.ops = {e: [] for e in self.ENGS}
        self.cnt = {e: 0 for e in self.ENGS}
        self.scnt = {}
        self.lastw = {}
        self.readers = {}

    def _deps(self, reads, writes):
        deps = {}

        def add(tok):
            key = (tok[0], tok[1])
            if deps.get(key, -1) < tok[2]:
                deps[key] = tok[2]

        for r in reads:
            t = self.lastw.get(r)
            if t is not None:
                add(t)
        for w in writes:
            t = self.lastw.get(w)
            if t is not None:
                add(t)
            for key, idx in self.readers.get(w, {}).items():
                add((key[0], key[1], idx))
        return deps

    def _update(self, tok, reads, writes):
        key = (tok[0], tok[1])
        for r in reads:
            d = self.readers.setdefault(r, {})
            if d.get(key, -1) < tok[2]:
                d[key] = tok[2]
        for w in writes:
            self.lastw[w] = tok
            self.readers[w] = {}

    def op(self, eng, fn, reads=(), writes=()):
        deps = self._deps(reads, writes)
        idx = self.cnt[eng]
        self.cnt[eng] += 1
        tok = ("e", eng, idx)
        self.ops[eng].append((fn, deps, tok))
        self._update(tok, reads, writes)

    def dma(self, eng, stream, fn, reads=(), writes=()):
        deps = self._deps(reads, writes)
        idx = self.scnt.get(stream, 0)
        self.scnt[stream] = idx + 1
        if idx > 0:
            key = ("s", stream)
            if deps.get(key, -1) < idx - 1:
                deps[key] = idx - 1
        tok = ("s", stream, idx)
        self.ops[eng].append((fn, deps, tok))
        self._update(tok, reads, writes)

    def wait_all(self, eng, reads):
        deps = self._deps(reads, ())
        self.ops[eng].append((None, deps, None))

    def emit(self, nc, stack):
        n_ep = {e: max(1, -(-self.cnt[e] // self.EPOCH)) for e in self.ENGS}
        esem = {e: [stack.enter_context(nc.semaphore(f"e_{e}_{i}")) for i in range(n_ep[e])]
                for e in self.ENGS if self.cnt[e] > 0}
        ssem = {s: stack.enter_context(nc.semaphore(f"s_{s}")) for s in self.scnt}
        block = stack.enter_context(nc.Block())
        prog = self

        def run(engname, eng):
            known = {}
            for fn, deps, tok in prog.ops[engname]:
                for key, idx in deps.items():
                    if key[0] == "e" and key[1] == "pe" and engname == "pe":
                        continue
                    if known.get(key, -1) >= idx:
                        continue
                    known[key] = idx
                    if key[0] == "e":
                        eng.wait_ge(esem[key[1]][idx // prog.EPOCH], idx % prog.EPOCH + 1)
                    else:
                        eng.wait_ge(ssem[key[1]], 16 * (idx + 1))
                if fn is None:
                    continue
                ins = fn(eng)
                if tok[0] == "e":
                    ins.then_inc(esem[engname][tok[2] // prog.EPOCH], 1)
                else:
                    ins.then_inc(ssem[tok[1]], 16)

        @block.tensor
        def _(e):
            run("pe", e)

        @block.scalar
        def _(e):
            run("act", e)

        @block.vector
        def _(e):
            run("dve", e)

        @block.gpsimd
        def _(e):
            run("pool", e)

        @block.sync
        def _(e):
            run("sp", e)


class StopTile(Exception):
    pass


class NullProg:
    def op(self, *a, **k):
        pass

    def dma(self, *a, **k):
        pass

    def wait_all(self, *a, **k):
        pass


class Builder:
    def __init__(self, nc, T, nlayers=4, final=True, ntiles=NT):
        self.nc = nc
        self.T = T
        self.nlayers = nlayers
        self.final = final
        self.ntiles = ntiles
        self.table, self.total, self.order = piece_table()

    debug = None

    def dump(self, name, src, n, nch=8):
        if self.debug != name:
            return
        P, T = self.P, self.T
        xb = self.xb
        P.op("act", lambda e: e.activation(out=xb[:, 0:nch, 0:n], in_=src, func=AF.Copy),
             reads=[(self.xr, c) for c in range(8)] + [("h", c) for c in range(8)] + [("y", c) for c in range(8)]
             + [("q", c) for c in range(8)] + [("act", c) for c in range(NF)],
             writes=[(self.xr, c) for c in range(8)])
        raise StopTile()

    def bank(self):
        b = self._bank
        self._bank = (b + 1) % 7
        return b

    def wget(self, name):
        if self.recording:
            self.seq.append(name)
            return 0
        i = self.wpos
        assert self.seq[i] == name, (i, self.seq[i], name)
        P, Tn = self.P, self.T
        hi = min(i + NSLOT - 1, len(self.seq))
        while self.wloaded < hi:
            j = self.wloaded
            s = j % NSLOT
            nm = self.seq[j]
            off, n = self.table[nm]
            slot = Tn["slots"][s]
            if nm in self.converted:
                src = Tn["wscr"][off:off + 128 * n].rearrange("(p n) -> p n", p=128)
                P.dma("sp", f"ld{s}", lambda e, dst=slot[:, 0:n], src=src: e.dma_start(out=dst, in_=src),
                      reads=[("wscr", nm)], writes=[("slot", s)])
            else:
                self.converted.add(nm)
                src32 = Tn["wflat"][off:off + 128 * n].rearrange("(p n) -> p n", p=128)
                for c0 in range(0, n, STG):
                    c1 = min(c0 + STG, n)
                    k = self.stgk
                    self.stgk = (k + 1) % 2
                    stg = Tn["stg"][k]
                    assert self.cur_tile <= 0, "staging aliases the second x buffer"
                    sres = [("xB", 4 * k + q) for q in range(4)]
                    P.dma("sp", f"sg{k}", lambda e, stg=stg, src32=src32, c0=c0, c1=c1: e.dma_start(
                        out=stg[:, 0:c1 - c0], in_=src32[:, c0:c1]),
                        reads=[], writes=sres)
                    P.op("act", lambda e, stg=stg, slot=slot, c0=c0, c1=c1: e.activation(
                        out=slot[:, c0:c1], in_=stg[:, 0:c1 - c0], func=AF.Copy),
                        reads=sres, writes=[("slot", s)])
                dst = Tn["wscr"][off:off + 128 * n].rearrange("(p n) -> p n", p=128)
                P.dma("act", f"ws{self.wsk}", lambda e, dst=dst, slot=slot, n=n: e.dma_start(out=dst, in_=slot[:, 0:n]),
                      reads=[("slot", s)], writes=[("wscr", nm)])
                self.wsk = (self.wsk + 1) % 2
            self.wloaded += 1
        self.wpos += 1
        return i % NSLOT

    def run(self, P, recording, seq=None):
        self.P = P
        self.recording = recording
        self.seq = [] if recording else seq
        self.wpos = 0
        self.wloaded = 0
        self._bank = 0
        self.converted = set()
        self.stgk = 0
        self.wsk = 0
        self.xb, self.xr = self.T["xA"], "xA"
        self.stats_ready = False
        self.rstd_valid = False
        self.hp_ready = False
        self.cur_tile = -1
        self.setup()
        self.tile(-1)
        for ti in range(self.ntiles):
            self.tile(ti)
        P.wait_all("act", [(self.xr, c) for c in range(8)] + ["yT"])
        return self.seq

    def setup(self):
        P, T, nc = self.P, self.T, self.nc
        P.op("dve", lambda e: e.memset(T["onesD"][:, :], 1.0 / D), writes=["onesD"])
        P.op("dve", lambda e: e.memset(T["ones1"][:, :], 1.0), writes=["ones1"])
        P.op("dve", lambda e: e.memset(T["carry"][:, :, :], 0.0), writes=[("carry", i) for i in range(12)])
        P.dma("act", "misc", lambda e: e.dma_start(out=T["vecs"][:, :], in_=T["vecs_d"][:, :]), writes=["vecs"])
        xflat = self.xb[:, :, :].rearrange("p c t -> p (c t)")
        P.dma("act", "misc", lambda e: e.dma_start(out=xflat[:, 0:128], in_=T["ident_d"][:, :]),
              writes=[(self.xr, c) for c in range(8)])
        P.op("act", lambda e: e.activation(out=T["ident"][:, :], in_=xflat[:, 0:128], func=AF.Copy),
             reads=[(self.xr, c) for c in range(8)], writes=["ident"])
        for src_name, dst_name in (("bias_d", "bias8"), ("biasf_d", "biasF")):
            P.dma("act", "misc", lambda e, s=src_name: e.dma_start(out=xflat[:, 0:3072], in_=T[s][:, :]),
                  writes=[(self.xr, c) for c in range(8)])
            dstf = T[dst_name][:, :, :].rearrange("p s n -> p (s n)")
            P.op("act", lambda e, dstf=dstf: e.activation(out=dstf, in_=xflat[:, 0:3072], func=AF.Copy, scale=8.0),
                 reads=[(self.xr, c) for c in range(8)], writes=[dst_name])
        P.op("act", lambda e: e.activation(out=T["esink"][:, :], in_=T["vecs"][:, V_SINK:V_SINK + 24], func=AF.Exp),
             reads=["vecs"], writes=["esink"])
        P.dma("act", "misc",
              lambda e, xb=self.xb: e.dma_start(out=xb[:, :, 0:N_MEM],
                                    in_=T["memT"].rearrange("(c p) t -> p c t", p=128)),
              writes=[(self.xr, c) for c in range(8)])
        self.rmsnorm(N_MEM, G_MEM)
        hb, ps = T["hb"], T["ps"]
        for l in range(4):
            s = self.wget(f"memkv{l}")
            sl = T["slots"][s]
            for i in range(2):
                b = self.bank()
                for k in range(8):
                    P.op("pe", lambda e, b=b, k=k, i=i, sl=sl: e.matmul(
                        ps[b][:, 0:N_MEM], sl[:, k * 512 + i * 128:k * 512 + (i + 1) * 128], hb[:, k, 0:N_MEM],
                        start=(k == 0), stop=(k == 7)),
                        reads=[("slot", s), ("h", k)], writes=[("ps", b)])
                P.op("act", lambda e, b=b, l=l, i=i: e.activation(out=T["mkT"][:, l, i, :], in_=ps[b][:, 0:N_MEM], func=AF.Copy),
                     reads=[("ps", b)], writes=["mk"])
            for mc in range(2):
                b = self.bank()
                for k in range(8):
                    P.op("pe", lambda e, b=b, k=k, mc=mc, sl=sl: e.matmul(
                        ps[b][:, 0:256], hb[:, k, mc * 128:(mc + 1) * 128], sl[:, k * 512 + 256:k * 512 + 512],
                        start=(k == 0), stop=(k == 7)),
                        reads=[("slot", s), ("h", k)], writes=[("ps", b)])
                P.op("act", lambda e, b=b, l=l, mc=mc: e.activation(out=T["mv"][:, l, mc, :], in_=ps[b][:, 0:256], func=AF.Copy),
                     reads=[("ps", b)], writes=["mv"])

    def stat_square(self, m, n):
        P, T = self.P, self.T
        xb, hb = self.xb, T["hb"]
        P.op("act", lambda e: e.activation(out=hb[:, m, 0:n], in_=xb[:, m, 0:n], func=AF.Square),
             reads=[(self.xr, m)], writes=[("h", m)])

    def stat_mm(self, m, n):
        P, T = self.P, self.T
        P.op("pe", lambda e: e.matmul(T["ps"][7][:, 0:n], T["onesD"][:, :], T["hb"][:, m, 0:n],
                                      start=(m == 0), stop=(m == 7)),
             reads=[("h", m), "onesD"], writes=[("ps", 7)])

    def hprime(self, m, n, gcol):
        P, T = self.P, self.T
        xb, qb, vecs = self.xb, T["qb"], T["vecs"]
        P.op("dve", lambda e: e.tensor_scalar(out=qb[:, m, 0:n], in0=xb[:, m, 0:n],
                                              scalar1=vecs[:, gcol + m:gcol + m + 1], scalar2=None, op0=ALU.mult),
             reads=[(self.xr, m), "vecs"], writes=[("q", m)])

    def rmsnorm(self, n, gcol, to_x=False, pre=None, post=None):
        P, T = self.P, self.T
        xb, hb, actb, ps, rstd, vecs = self.xb, T["hb"], T["actb"], T["ps"], T["rstd"], T["vecs"]
        xr = self.xr
        used_hp = False
        if self.hp_ready:
            assert not self.rstd_valid and not to_x
            if pre is not None:
                pre()
            self.stat_mm(7, n)
            if post is not None:
                post()
            self.stats_ready = True
            self.hp_ready = False
            used_hp = pre is not None
        if not self.rstd_valid:
            if not self.stats_ready:
                P.op("act", lambda e: e.activation(out=actb[:, 0:8, 0:n], in_=xb[:, :, 0:n], func=AF.Square),
                     reads=[(xr, c) for c in range(8)], writes=[("act", c) for c in range(8)])
                for c in range(8):
                    P.op("pe", lambda e, c=c: e.matmul(ps[7][:, 0:n], T["onesD"][:, :], actb[:, c, 0:n],
                                                       start=(c == 0), stop=(c == 7)),
                         reads=[("act", c), "onesD"], writes=[("ps", 7)])
            P.op("act", lambda e: e.activation(out=rstd[:, 0:n], in_=ps[7][:, 0:n], func=AF.Ln, bias=T["epsc"][:, 0:1]),
                 reads=[("ps", 7), "epsc"], writes=["rstd"])
            P.op("act", lambda e: e.activation(out=rstd[:, 0:n], in_=rstd[:, 0:n], func=AF.Exp, scale=-0.5),
                 reads=["rstd"], writes=["rstd"])
        self.stats_ready = False
        self.rstd_valid = False
        for c in range(8):
            if to_x:
                P.op("dve", lambda e, c=c: e.scalar_tensor_tensor(
                    out=xb[:, c, 0:n], in0=xb[:, c, 0:n], scalar=vecs[:, gcol + c:gcol + c + 1], in1=rstd[:, 0:n],
                    op0=ALU.mult, op1=ALU.mult),
                    reads=[(xr, c), "rstd", "vecs"], writes=[(xr, c)])
            else:
                P.op("dve", lambda e, c=c: e.scalar_tensor_tensor(
                    out=hb[:, c, 0:n], in0=xb[:, c, 0:n], scalar=vecs[:, gcol + c:gcol + c + 1], in1=rstd[:, 0:n],
                    op0=ALU.mult, op1=ALU.mult),
                    reads=[(xr, c), "rstd", "vecs"], writes=[("h", c)])
        return used_hp

    def proj_chunk(self, s, col0, kstride, nk, rhs_t, rhs_res, n):
        P, T = self.P, self.T
        ps, sl = T["ps"], T["slots"][s]
        b = self.bank()
        for k in range(nk):
            P.op("pe", lambda e, k=k: e.matmul(ps[b][:, 0:n], sl[:, k * kstride + col0:k * kstride + col0 + 128],
                                               rhs_t[:, k, 0:n], start=(k == 0), stop=(k == nk - 1)),
                 reads=[("slot", s), (rhs_res, k)], writes=[("ps", b)])
        return b

    def residual_add(self, b, m, n):
        P, T = self.P, self.T
        xb, ps = self.xb, T["ps"]
        xr = self.xr
        P.op("dve", lambda e: e.tensor_tensor(out=xb[:, m, 0:n], in0=ps[b][:, 0:n], in1=xb[:, m, 0:n], op=ALU.add),
             reads=[("ps", b), (xr, m)], writes=[(xr, m)])

    def a_mixer(self, l, n, halo):
        P, T = self.P, self.T
        ps, hb, qb, yb, vecs = T["ps"], T["hb"], T["qb"], T["yb"], T["vecs"]
        order = [("q", 0), ("q", 1)]
        for j in range(6):
            order += [("u", j), ("C", j), ("B", j)]
        rstd = T["rstd"]
        s0 = self.wget(f"ain{l}_0")
        hpb = {}

        def pre():
            hpb[0] = self.proj_chunk(s0, 0, 512, 8, qb, "q", n)

        def post():
            hpb[1] = self.proj_chunk(s0, 128, 512, 8, qb, "q", n)

        used_hp = self.rmsnorm(n, G_MIX + 8 * l, pre=pre, post=post)
        self.dump(f"h{l}", hb[:, :, 0:n], n)
        for p in range(5):
            s = s0 if p == 0 else self.wget(f"ain{l}_{p}")
            for jj in range(4):
                kind, j = order[p * 4 + jj]
                if used_hp and p == 0 and jj < 2:
                    b = hpb[jj]
                else:
                    b = self.proj_chunk(s, jj * 128, 512, 8, hb, "h", n)
                r = j % 2
                usb, vb, accb = T["usb"][r], T["vb"][r], T["accb"][r]
                if kind == "q" and used_hp:
                    P.op("dve", lambda e, b=b, j=j: e.tensor_tensor(
                        out=qb[:, 6 + j, 0:n], in0=ps[b][:, 0:n], in1=rstd[:, 0:n], op=ALU.mult),
                        reads=[("ps", b), "rstd"], writes=[("q", 6 + j)])
                elif kind == "u":
                    P.op("act", lambda e, b=b, usb=usb: e.activation(out=usb[:, 0:n], in_=ps[b][:, 0:n], func=AF.Copy),
                         reads=[("ps", b)], writes=[("usb", r)])
                elif kind == "C":
                    ci = l * 6 + j
                    cw = V_CONV + ci * 3
                    P.op("dve", lambda e, vb=vb, ci=ci: e.tensor_copy(out=vb[:, 0:2], in_=T["carry"][:, ci, :]),
                         reads=[("carry", ci)], writes=[("vb", r)])
                    P.op("dve", lambda e, b=b, vb=vb, usb=usb: e.tensor_tensor(
                        out=vb[:, 2:2 + n], in0=ps[b][:, 0:n], in1=usb[:, 0:n], op=ALU.mult),
                        reads=[("ps", b), ("usb", r)], writes=[("vb", r)])
                    if halo:
                        P.op("dve", lambda e, vb=vb, ci=ci: e.tensor_scalar(
                            out=T["carry"][:, ci, :], in0=vb[:, n:n + 2], scalar1=vecs[:, V_FLAG:V_FLAG + 1],
                            scalar2=None, op0=ALU.mult),
                            reads=[("vb", r), "vecs"], writes=[("carry", ci)])
                    else:
                        P.op("dve", lambda e, vb=vb, ci=ci: e.tensor_copy(out=T["carry"][:, ci, :], in_=vb[:, n:n + 2]),
                             reads=[("vb", r)], writes=[("carry", ci)])
                    P.op("dve", lambda e, vb=vb, accb=accb, cw=cw: e.tensor_scalar(
                        out=accb[:, 0:n], in0=vb[:, 2:2 + n], scalar1=vecs[:, cw + 2:cw + 3], scalar2=None, op0=ALU.mult),
                        reads=[("vb", r), "vecs"], writes=[("accb", r)])
                    P.op("dve", lambda e, vb=vb, accb=accb, cw=cw: e.scalar_tensor_tensor(
                        out=accb[:, 0:n], in0=vb[:, 1:1 + n], scalar=vecs[:, cw + 1:cw + 2], in1=accb[:, 0:n],
                        op0=ALU.mult, op1=ALU.add),
                        reads=[("vb", r), ("accb", r), "vecs"], writes=[("accb", r)])
                    P.op("dve", lambda e, vb=vb, accb=accb, cw=cw: e.scalar_tensor_tensor(
                        out=accb[:, 0:n], in0=vb[:, 0:n], scalar=vecs[:, cw:cw + 1], in1=accb[:, 0:n],
                        op0=ALU.mult, op1=ALU.add),
                        reads=[("vb", r), ("accb", r), "vecs"], writes=[("accb", r)])
                elif kind == "B":
                    P.op("dve", lambda e, b=b, j=j, accb=accb: e.tensor_tensor(
                        out=yb[:, j, 0:n], in0=ps[b][:, 0:n], in1=accb[:, 0:n], op=ALU.mult),
                        reads=[("ps", b), ("accb", r)], writes=[("y", j)])
                else:
                    P.op("act", lambda e, b=b, j=j: e.activation(out=qb[:, 6 + j, 0:n], in_=ps[b][:, 0:n], func=AF.Copy),
                         reads=[("ps", b)], writes=[("q", 6 + j)])
            if p == 0:
                self.mem_attn(l, n)

    def mem_attn(self, l, n):
        P, T = self.P, self.T
        ps, qb, yb, pm, rc = T["ps"], T["qb"], T["yb"], T["pm"], T["rc"]
        mkT, mv = T["mkT"], T["mv"]

        def scores(hm):
            i, half = hm // 2, hm % 2
            r = hm % 3
            lo = half * 64
            for mc in range(2):
                b = self.bank()
                P.op("pe", lambda e, b=b, mc=mc: e.matmul(
                    ps[b][:, 0:n], mkT[lo:lo + 64, l, i, mc * 128:(mc + 1) * 128], qb[lo:lo + 64, 6 + i, 0:n],
                    start=True, stop=True),
                    reads=["mk", ("q", 6 + i)], writes=[("ps", b)])
                P.op("act", lambda e, b=b, mc=mc: e.activation(out=pm[r][:, mc, 0:n], in_=ps[b][:, 0:n],
                                                               func=AF.Exp, scale=0.125),
                     reads=[("ps", b)], writes=[("pm", r, mc)])

        def pv(hm):
            i, half = hm // 2, hm % 2
            r = hm % 2
            rp = hm % 3
            lo = half * 64
            bo, bd = self.bank(), self.bank()
            for mc in range(2):
                P.op("pe", lambda e, mc=mc: e.matmul(ps[bo][:, 0:n], mv[:, l, mc, i * 128:(i + 1) * 128],
                                                     pm[rp][:, mc, 0:n], start=(mc == 0), stop=(mc == 1)),
                     reads=["mv", ("pm", rp, mc)], writes=[("ps", bo)])
            for mc in range(2):
                P.op("pe", lambda e, mc=mc: e.matmul(ps[bd][:, 0:n], T["ones1"][:, :], pm[rp][:, mc, 0:n],
                                                     start=(mc == 0), stop=(mc == 1)),
                     reads=["ones1", ("pm", rp, mc)], writes=[("ps", bd)])
            P.op("act", lambda e: e.activation(out=rc[r][lo:lo + 64, 0:n], in_=ps[bd][lo:lo + 64, 0:n], func=AF.Ln),
                 reads=[("ps", bd)], writes=[("rc", r)])
            P.op("act", lambda e: e.activation(out=rc[r][lo:lo + 64, 0:n], in_=rc[r][lo:lo + 64, 0:n], func=AF.Exp,
                                               scale=-1.0),
                 reads=[("rc", r)], writes=[("rc", r)])
            P.op("dve", lambda e: e.tensor_tensor(out=yb[lo:lo + 64, 6 + i, 0:n], in0=ps[bo][lo:lo + 64, 0:n],
                                                  in1=rc[r][lo:lo + 64, 0:n], op=ALU.mult),
                 reads=[("ps", bo), ("rc", r)], writes=[("y", 6 + i)])

        for hm in range(6):
            if hm < 4:
                scores(hm)
            if hm >= 2:
                pv(hm - 2)

    def out_proj(self, prefix, n, next_gcol):
        T = self.T
        for p in range(2):
            s = self.wget(f"{prefix}_{p}")
            for jj in range(4):
                m = p * 4 + jj
                b = self.proj_chunk(s, jj * 128, 512, 8, T["yb"], "y", n)
                if m >= 1:
                    self.stat_mm(m - 1, n)
                self.residual_add(b, m, n)
                self.stat_square(m, n)
                self.hprime(m, n, next_gcol)
        self.hp_ready = True

    def ffn(self, l, n, next_gcol):
        P, T = self.P, self.T
        ps, hb, actb = T["ps"], T["hb"], T["actb"]
        qb, rstd = T["qb"], T["rstd"]
        sg0 = self.wget(f"g{l}_0")
        su0 = self.wget(f"u{l}_0")
        hpb = {}

        def pre():
            hpb["g"] = self.proj_chunk(sg0, 0, 512, 8, qb, "q", n)

        def post():
            hpb["u"] = self.proj_chunk(su0, 0, 512, 8, qb, "q", n)

        used_hp = self.rmsnorm(n, G_FFN + 8 * l, pre=pre, post=post)
        self.dump(f"ffnh{l}", hb[:, :, 0:n], n)
        for fg in range(6):
            ncol = 512 if fg < 5 else 256
            sg = sg0 if fg == 0 else self.wget(f"g{l}_{fg}")
            su = su0 if fg == 0 else self.wget(f"u{l}_{fg}")
            for jj in range(ncol // 128):
                f = fg * 4 + jj
                r = f % 2
                usb = T["usb"][r]
                if f == 0 and used_hp:
                    bg, bu = hpb["g"], hpb["u"]
                    tmp = T["accb"][0]
                    P.op("dve", lambda e, bg=bg, usb=usb: e.tensor_tensor(
                        out=usb[:, 0:n], in0=ps[bg][:, 0:n], in1=rstd[:, 0:n], op=ALU.mult),
                        reads=[("ps", bg), "rstd"], writes=[("usb", r)])
                    P.op("act", lambda e, usb=usb: e.activation(out=usb[:, 0:n], in_=usb[:, 0:n], func=AF.Silu),
                         reads=[("usb", r)], writes=[("usb", r)])
                    P.op("dve", lambda e, bu=bu, tmp=tmp: e.tensor_tensor(
                        out=tmp[:, 0:n], in0=ps[bu][:, 0:n], in1=rstd[:, 0:n], op=ALU.mult),
                        reads=[("ps", bu), "rstd"], writes=[("accb", 0)])
                    P.op("dve", lambda e, tmp=tmp, usb=usb: e.tensor_tensor(
                        out=actb[:, 0, 0:n], in0=tmp[:, 0:n], in1=usb[:, 0:n], op=ALU.mult),
                        reads=[("accb", 0), ("usb", r)], writes=[("act", 0)])
                    continue
                bg = self.proj_chunk(sg, jj * 128, ncol, 8, hb, "h", n)
                bu = self.proj_chunk(su, jj * 128, ncol, 8, hb, "h", n)
                P.op("act", lambda e, bg=bg, usb=usb: e.activation(out=usb[:, 0:n], in_=ps[bg][:, 0:n], func=AF.Silu),
                     reads=[("ps", bg)], writes=[("usb", r)])
                P.op("dve", lambda e, bu=bu, usb=usb, f=f: e.tensor_tensor(
                    out=actb[:, f, 0:n], in0=ps[bu][:, 0:n], in1=usb[:, 0:n], op=ALU.mult),
                    reads=[("ps", bu), ("usb", r)], writes=[("act", f)])
        self.dump(f"act{l}", actb[:, 0:8, 0:n], n)
        for m in range(8):
            s = self.wget(f"d{l}_{m}")
            b = self.proj_chunk(s, 0, 128, NF, actb, "act", n)
            if m >= 1:
                self.stat_mm(m - 1, n)
            self.residual_add(b, m, n)
            self.stat_square(m, n)
            if next_gcol is not None:
                self.hprime(m, n, next_gcol)
        if next_gcol is not None:
            self.hp_ready = True
        else:
            self.stat_mm(7, n)
            self.stats_ready = True

    def kv_proj(self, n, c0, blk0):
        P, T = self.P, self.T
        ps, hb, kT, vS = T["ps"], T["hb"], T["kT"], T["vS"]
        s = self.wget("kv")
        sl = T["slots"][s]
        qb, rstd = T["qb"], T["rstd"]
        hpb = {}

        def pre():
            hpb[0] = self.proj_chunk(s, 0, 512, 8, qb, "q", n)

        def post():
            hpb[1] = self.proj_chunk(s, 128, 512, 8, qb, "q", n)

        used_hp = self.rmsnorm(n, G_KV, pre=pre, post=post)
        self.rstd_valid = True
        nvalid = n - c0
        nblk = nvalid // 128
        for pr in range(2):
            if used_hp:
                b = hpb[pr]
                P.op("dve", lambda e, b=b, pr=pr: e.tensor_tensor(
                    out=kT[:, pr, blk0 * 128:blk0 * 128 + nvalid], in0=ps[b][:, c0:n], in1=rstd[:, c0:n], op=ALU.mult),
                    reads=[("ps", b), "rstd"], writes=[("kT", blk0 + t) for t in range(nblk)])
                continue
            b = self.proj_chunk(s, pr * 128, 512, 8, hb, "h", n)
            P.op("act", lambda e, b=b, pr=pr: e.activation(
                out=kT[:, pr, blk0 * 128:blk0 * 128 + nvalid], in_=ps[b][:, c0:n], func=AF.Copy),
                reads=[("ps", b)], writes=[("kT", blk0 + t) for t in range(nblk)])
        for tb in range(nblk):
            b = self.bank()
            for k in range(8):
                P.op("pe", lambda e, b=b, k=k, tb=tb: e.matmul(
                    ps[b][:, 0:256], hb[:, k, c0 + tb * 128:c0 + (tb + 1) * 128], sl[:, k * 512 + 256:k * 512 + 512],
                    start=(k == 0), stop=(k == 7)),
                    reads=[("slot", s), ("h", k)], writes=[("ps", b)])
            P.op("act", lambda e, b=b, tb=tb: e.activation(out=vS[:, blk0 + tb, :], in_=ps[b][:, 0:256], func=AF.Copy),
                 reads=[("ps", b)], writes=[("vS", blk0 + tb)])

    def swa(self, lj, ti):
        P, T = self.P, self.T
        ps, qb, yb, pS, rc = T["ps"], T["qb"], T["yb"], T["pS"], T["rc"]
        kT, vS, bias8, biasF, ident = T["kT"], T["vS"], T["bias8"], T["biasF"], T["ident"]

        def scores(s):
            c, half = s // 2, s % 2
            lo = half * 64
            pair = c // 3
            r = s % 3
            for hbk in range(2):
                b = self.bank()
                if ti == 0 and hbk == 0:
                    P.op("pe", lambda e, b=b: e.matmul(ps[b][:, 0:256], ident[:, :], biasF[:, s, :],
                                                       start=True, stop=False, skip_group_check=True),
                         reads=["ident", "biasF"], writes=[("ps", b)])
                    P.op("pe", lambda e, b=b: e.matmul(ps[b][:, 256:512], ident[:, :], bias8[:, s, :],
                                                       start=False, stop=False, skip_group_check=True),
                         reads=["ident", "bias8"], writes=[("ps", b)])
                    sgc = True
                else:
                    P.op("pe", lambda e, b=b: e.matmul(
                        ps[b][:, :].rearrange("p (a n) -> p a n", a=2), ident[:, :],
                        bias8[:, s:s + 1, :].broadcast_to([128, 2, 256]),
                        start=True, stop=False),
                        reads=["ident", "bias8"], writes=[("ps", b)])
                    sgc = False
                for qq in range(2):
                    qi = 2 * hbk + qq
                    nblk = ti * 4 + qi
                    for kb in range(2):
                        blk = nblk + kb
                        last = (qq == 1 and kb == 1)
                        P.op("pe", lambda e, b=b, qq=qq, kb=kb, blk=blk, qi=qi, last=last, sgc=sgc: e.matmul(
                            ps[b][:, qq * 256 + kb * 128:qq * 256 + (kb + 1) * 128],
                            kT[lo:lo + 64, pair, blk * 128:(blk + 1) * 128],
                            qb[lo:lo + 64, c, qi * 128:(qi + 1) * 128],
                            start=False, stop=last, skip_group_check=sgc),
                            reads=[("kT", blk), ("q", c)], writes=[("ps", b)])
                P.op("act", lambda e, b=b, hbk=hbk: e.activation(out=pS[r][:, hbk * 512:(hbk + 1) * 512], in_=ps[b][:, :],
                                                                 func=AF.Exp, scale=0.125),
                     reads=[("ps", b)], writes=[("pS", r, hbk)])

        def pv(s):
            c, half = s // 2, s % 2
            lo = half * 64
            pair = c // 3
            r = s % 2
            rp = s % 3
            bo, bd = self.bank(), self.bank()
            for qi in range(4):
                nblk = ti * 4 + qi
                for kb in range(2):
                    blk = nblk + kb
                    P.op("pe", lambda e, qi=qi, kb=kb, blk=blk: e.matmul(
                        ps[bo][:, qi * 128:(qi + 1) * 128], vS[:, blk, pair * 128:(pair + 1) * 128],
                        pS[rp][:, qi * 256 + kb * 128:qi * 256 + (kb + 1) * 128],
                        start=(kb == 0), stop=(kb == 1)),
                        reads=[("vS", blk), ("pS", rp, qi // 2)], writes=[("ps", bo)])
            for kb in range(2):
                P.op("pe", lambda e, kb=kb: e.matmul(
                    ps[bd][:, :].rearrange("p (a q) -> p a q", a=4), T["ones1"][:, :],
                    pS[rp][:, :].rearrange("p (a b q) -> p a b q", a=4, b=2)[:, :, kb, :],
                    start=(kb == 0), stop=(kb == 1)),
                    reads=["ones1", ("pS", rp, 0), ("pS", rp, 1)], writes=[("ps", bd)])
            es = T["esink"][lo:lo + 64, lj * 12 + s:lj * 12 + s + 1]
            P.op("act", lambda e: e.activation(out=rc[r][lo:lo + 64, :], in_=ps[bd][lo:lo + 64, :], func=AF.Ln, bias=es),
                 reads=[("ps", bd), "esink"], writes=[("rc", r)])
            P.op("act", lambda e: e.activation(out=rc[r][lo:lo + 64, :], in_=rc[r][lo:lo + 64, :], func=AF.Exp,
                                               scale=-1.0),
                 reads=[("rc", r)], writes=[("rc", r)])
            P.op("dve", lambda e: e.tensor_tensor(out=yb[lo:lo + 64, c, :], in0=ps[bo][lo:lo + 64, :],
                                                  in1=rc[r][lo:lo + 64, :], op=ALU.mult),
                 reads=[("ps", bo), ("rc", r)], writes=[("y", c)])

        for s in range(14):
            if s < 12:
                scores(s)
            if s >= 2:
                pv(s - 2)

    def b_layer(self, lj, ti):
        P, T = self.P, self.T
        l = 2 + lj
        n = TT
        ps, hb, qb = T["ps"], T["hb"], T["qb"]
        rstd = T["rstd"]
        s0 = self.wget(f"bq{lj}_0")
        hpb = {}

        def pre():
            hpb[0] = self.proj_chunk(s0, 0, 512, 8, qb, "q", n)

        def post():
            hpb[1] = self.proj_chunk(s0, 128, 512, 8, qb, "q", n)

        used_hp = self.rmsnorm(n, G_MIX + 8 * l, pre=pre, post=post)
        for p in range(2):
            s = s0 if p == 0 else self.wget(f"bq{lj}_{p}")
            for jj in range(4):
                c = p * 4 + jj
                if used_hp and c < 2:
                    b = hpb[c]
                    P.op("dve", lambda e, b=b, c=c: e.tensor_tensor(
                        out=qb[:, c, 0:n], in0=ps[b][:, 0:n], in1=rstd[:, 0:n], op=ALU.mult),
                        reads=[("ps", b), "rstd"], writes=[("q", c)])
                    continue
                b = self.proj_chunk(s, jj * 128, 512, 8, hb, "h", n)
                P.op("act", lambda e, b=b, c=c: e.activation(out=qb[:, c, 0:n], in_=ps[b][:, 0:n], func=AF.Copy),
                     reads=[("ps", b)], writes=[("q", c)])
        self.mem_attn(l, n)
        self.swa(lj, ti)
        self.out_proj(f"bout{lj}", n, G_FFN + 8 * l)
        self.ffn(l, n, self.next_gcol(l))

    def next_gcol(self, l):
        if l + 1 >= self.nlayers:
            return None
        return G_KV if l == 1 else G_MIX + 8 * (l + 1)

    def tile_body(self, ti, halo, n):
        T = self.T
        na = min(self.nlayers, 2)
        for l in range(na):
            self.a_mixer(l, n, halo)
            self.dump(f"y{l}", T["yb"][:, :, 0:n], n)
            self.out_proj(f"aout{l}", n, G_FFN + 8 * l)
            self.dump(f"xmix{l}", self.xb[:, :, 0:n], n)
            self.ffn(l, n, self.next_gcol(l))
        if self.nlayers > 2:
            if halo:
                self.kv_proj(n, HALO - 128, 0)
            else:
                self.kv_proj(n, 0, 1 + ti * 4)
        if halo:
            return
        for lj in range(self.nlayers - 2):
            self.b_layer(lj, ti)
        if self.final:
            self.rmsnorm(n, G_FIN, to_x=True)

    def xbuf_for(self, ti):
        if ti <= 0 or ti % 2 == 0:
            return self.T["xA"], "xA"
        return self.T["xB"], "xB"

    def emit_xload(self, ti):
        P, T = self.P, self.T
        halo = ti < 0
        n = HALO if halo else TT
        t0 = 0 if halo else HALO + ti * TT
        xb, xr = self.xbuf_for(ti)
        xsrc = T["xT"].rearrange("(c p) t -> p c t", p=128)[:, :, t0:t0 + n]
        P.dma("sp", "xl", lambda e: e.dma_start(out=xb[:, :, 0:n], in_=xsrc),
              reads=[], writes=[(xr, c) for c in range(8)])

    def tile(self, ti):
        P, T = self.P, self.T
        halo = ti < 0
        n = HALO if halo else TT
        self.cur_tile = ti
        self.xb, self.xr = self.xbuf_for(ti)
        self.stats_ready = False
        self.rstd_valid = False
        self.hp_ready = False
        if halo:
            self.emit_xload(-1)
        if ti >= 1 and ti + 1 < self.ntiles:
            self.emit_xload(ti + 1)
        try:
            self.tile_body(ti, halo, n)
        except StopTile:
            pass
        if halo:
            self.emit_xload(0)
            return
        xb, xr = self.xb, self.xr
        ydst = T["yT"].rearrange("(c p) t -> p c t", p=128)[:, :, ti * TT:(ti + 1) * TT]
        P.dma("act", "st", lambda e: e.dma_start(out=ydst, in_=xb[:, :, 0:n]),
              reads=[(xr, c) for c in range(8)], writes=["yT"])
        if ti == 0 and self.ntiles > 1:
            self.emit_xload(1)


def build_nc(nlayers=4, final=True, ntiles=NT, debug=None):
    nc = bass.Bass("TRN2", target_bir_lowering=False)
    table, total, order = piece_table()
    T = {}
    T["xT"] = nc.dram_tensor("xT", [D, HALO + TOK], F32, kind="ExternalInput").ap()
    T["memT"] = nc.dram_tensor("memT", [D, N_MEM], F32, kind="ExternalInput").ap()
    T["vecs_d"] = nc.dram_tensor("vecs", [128, NV], F32, kind="ExternalInput").ap()
    T["bias_d"] = nc.dram_tensor("biasT", [128, 3072], F32, kind="ExternalInput").ap()
    T["biasf_d"] = nc.dram_tensor("biasF", [128, 3072], F32, kind="ExternalInput").ap()
    T["ident_d"] = nc.dram_tensor("ident", [128, 128], F32, kind="ExternalInput").ap()
    T["wflat"] = nc.dram_tensor("wflat", [total], F32, kind="ExternalInput").ap()
    T["wscr"] = nc.dram_tensor("wscr", [total], BF16, kind="Internal").ap()
    T["yT"] = nc.dram_tensor("yT", [D, TOK], F32, kind="ExternalOutput").ap()

    stack = ExitStack()
    with stack:
        def sb(name, shape, dt):
            return stack.enter_context(nc.sbuf_tensor(name, shape, dt))

        T["xA"] = sb("xb", [128, 8, TT], F32)
        T["hb"] = sb("hb", [128, 8, TT], BF16)
        T["qb"] = sb("qb", [128, 8, TT], BF16)
        T["yb"] = sb("yb", [128, 8, TT], BF16)
        T["actb"] = sb("actb", [128, NF, TT], BF16)
        T["usb"] = [sb(f"usb{i}", [128, TT], F32) for i in range(2)]
        T["vb"] = [sb(f"vb{i}", [128, TT + 2], F32) for i in range(2)]
        T["accb"] = [sb(f"accb{i}", [128, TT], F32) for i in range(2)]
        T["rstd"] = sb("rstd", [128, TT], F32)
        T["pS"] = [sb(f"pS{i}", [128, 1024], BF16) for i in range(3)]
        T["pm"] = [sb(f"pm{i}", [128, 2, TT], BF16) for i in range(3)]
        T["rc"] = [sb(f"rc{i}", [128, TT], F32) for i in range(2)]
        T["kT"] = sb("kT", [128, 2, NBLK * 128], BF16)
        T["vS"] = sb("vS", [128, NBLK, 256], BF16)
        T["bias8"] = sb("bias8_sb", [128, 12, 256], BF16)
        T["biasF"] = sb("biasF_sb", [128, 12, 256], BF16)
        T["mkT"] = sb("mkT", [128, 4, 2, N_MEM], BF16)
        T["mv"] = sb("mv", [128, 4, 2, 256], BF16)
        T["vecs"] = sb("vecs_sb", [128, NV], F32)
        T["esink"] = sb("esink", [128, 24], F32)
        T["carry"] = sb("carry", [128, 12, 2], F32)
        T["onesD"] = sb("onesD", [128, 128], BF16)
        T["ones1"] = sb("ones1", [128, 128], BF16)
        T["ident"] = sb("ident_sb", [128, 128], BF16)
        T["epsc"] = sb("epsc", [128, 1], F32)
        T["slots"] = [sb(f"slot{i}", [128, SLOT_ELEMS], BF16) for i in range(NSLOT)]
        T["xB"] = sb("xB", [128, 8, TT], F32)
        xbf = T["xB"][:, :, :].rearrange("p c t -> p (c t)")
        T["stg"] = [xbf[:, i * STG:(i + 1) * STG] for i in range(2)]
        T["ps"] = [stack.enter_context(nc.psum_tensor(f"ps{i}", [128, 512], F32)) for i in range(8)]

        bld = Builder(nc, T, nlayers=nlayers, final=final, ntiles=ntiles)
        bld.debug = debug
        seq = bld.run(NullProg(), recording=True)
        P = Prog()
        P.op("dve", lambda e: e.memset(T["epsc"][:, :], EPS), writes=["epsc"])
        bld.run(P, recording=False, seq=seq)
        P.emit(nc, stack)
    return nc


def make_in_maps(inp):
    x = np.asarray(inp["x"], np.float32)
    mem = np.asarray(inp["mem"], np.float32)
    wflat = pack_weights(inp)
    biasT, biasFirst = make_bias_tables(inp["rel_bias"])
    ident = np.eye(128, dtype=np.float32)
    in_maps = []
    for c in range(NCORE):
        b, qtr = c // 4, c % 4
        xt = np.zeros((D, HALO + TOK), np.float32)
        lo = qtr * TOK
        if qtr > 0:
            xt[:, :] = x[b, lo - HALO:lo + TOK, :].T
        else:
            xt[:, HALO:] = x[b, lo:lo + TOK, :].T
        in_maps.append({
            "xT": xt,
            "memT": np.ascontiguousarray(mem[b].T),
            "vecs": make_vecs(inp, 0.0 if qtr == 0 else 1.0),
            "biasT": biasT,
            "biasF": biasFirst if qtr == 0 else biasT,
            "ident": ident,
            "wflat": wflat,
        })
    return in_maps


_NC_CACHE = {}


def kernel(**inputs):
    in_maps = make_in_maps(inputs)
    if "nc" not in _NC_CACHE:
        _NC_CACHE["nc"] = build_nc()
    nc = _NC_CACHE["nc"]
    res = run_bass_kernel_spmd(nc, in_maps, core_ids=list(range(NCORE)))
    out = np.empty((BATCH, SEQ, D), np.float32)
    for c in range(NCORE):
        b, qtr = c // 4, c % 4
        out[b, qtr * TOK:(qtr + 1) * TOK, :] = res.results[c]["yT"].T
    return out
```

```python
import math
from contextlib import ExitStack

import numpy as np
import concourse.bass as bass
import concourse.mybir as mybir
from concourse.bass_utils import run_bass_kernel_spmd

F32 = mybir.dt.float32
BF16 = mybir.dt.bfloat16
AF = mybir.ActivationFunctionType
ALU = mybir.AluOpType

D = 1024
NCH = 8
SEQ = 16384
BATCH = 2
NCORE = 8
TOK = 4096
HALO = 136
TT = 512
NT = TOK // TT
DFF = 2816
NF = DFF // 128
N_MEM = 256
NBLK = TOK // 128 + 1
EPS = 1e-5
NEG = -30000.0
NSLOT = 5
SLOT_ELEMS = 4096
STG = 2048

HEAD_LO = [0, 1, 2, 6, 7, 8]
HEAD_HI = [3, 4, 5, 9, 10, 11]
SLOT_HEAD = [HEAD_LO[s // 2] if s % 2 == 0 else HEAD_HI[s // 2] for s in range(12)]

G_MIX, G_FFN, G_KV, G_MEM, G_FIN = 0, 32, 64, 72, 80
V_CONV = 88
V_FLAG = 124
V_SINK = 125
NV = 152


def piece_table():
    names = []
    for i in range(4):
        names.append((f"memkv{i}", 4096))
    for l in range(2):
        for p in range(5):
            names.append((f"ain{l}_{p}", 4096))
        for p in range(2):
            names.append((f"aout{l}_{p}", 4096))
        for fg in range(6):
            n = 4096 if fg < 5 else 2048
            names.append((f"g{l}_{fg}", n))
            names.append((f"u{l}_{fg}", n))
        for m in range(8):
            names.append((f"d{l}_{m}", NF * 128))
    names.append(("kv", 4096))
    for j in range(2):
        l = 2 + j
        for p in range(2):
            names.append((f"bq{j}_{p}", 4096))
        for p in range(2):
            names.append((f"bout{j}_{p}", 4096))
        for fg in range(6):
            n = 4096 if fg < 5 else 2048
            names.append((f"g{l}_{fg}", n))
            names.append((f"u{l}_{fg}", n))
        for m in range(8):
            names.append((f"d{l}_{m}", NF * 128))
    table = {}
    off = 0
    for nm, n in names:
        table[nm] = (off, n)
        off += 128 * n
    return table, off, [nm for nm, _ in names]


def _as_piece(w2d):
    K, Fc = w2d.shape
    kc = K // 128
    return np.ascontiguousarray(w2d.reshape(kc, 128, Fc).transpose(1, 0, 2)).reshape(128, kc * Fc)


def pack_weights(inp):
    table, total, _ = piece_table()
    wflat = np.empty((total,), np.float32)

    def put(name, w2d):
        off, n = table[name]
        blk = _as_piece(np.asarray(w2d, np.float32))
        assert blk.shape == (128, n), (name, blk.shape, n)
        wflat[off:off + 128 * n] = blk.reshape(-1)

    for i in range(4):
        put(f"memkv{i}", inp["w_mem_kv"][i])
    a_chunks = [18, 19]
    for j in range(6):
        a_chunks += [j, 12 + j, 6 + j]
    a_cols = np.concatenate([np.arange(c * 128, (c + 1) * 128) for c in a_chunks])
    qperm = np.concatenate([np.arange(h * 64, (h + 1) * 64) for h in SLOT_HEAD] + [np.arange(768, 1024)])
    for l in range(4):
        for fg in range(6):
            lo, hi = fg * 512, min((fg + 1) * 512, DFF)
            put(f"g{l}_{fg}", inp["w_gate"][l][:, lo:hi])
            put(f"u{l}_{fg}", inp["w_up"][l][:, lo:hi])
        for m in range(8):
            put(f"d{l}_{m}", inp["w_down"][l][:, m * 128:(m + 1) * 128])
    for l in range(2):
        wp = np.asarray(inp["a_w_in"][l])[:, a_cols]
        for p in range(5):
            put(f"ain{l}_{p}", wp[:, p * 512:(p + 1) * 512])
        for p in range(2):
            put(f"aout{l}_{p}", inp["a_w_out"][l][:, p * 512:(p + 1) * 512])
    put("kv", inp["w_kv"])
    for j in range(2):
        wq = np.asarray(inp["b_w_q"][j])[:, qperm]
        wo = np.asarray(inp["b_w_out"][j])[qperm, :]
        for p in range(2):
            put(f"bq{j}_{p}", wq[:, p * 512:(p + 1) * 512])
            put(f"bout{j}_{p}", wo[:, p * 512:(p + 1) * 512])
    return wflat


def _rel_bucket_np(dist):
    max_exact = 16
    d = np.maximum(dist, 1).astype(np.float32)
    large = max_exact + (np.log(d / max_exact) / math.log(128 / max_exact) * (32 - max_exact)).astype(np.int32)
    large = np.minimum(large, 31)
    return np.where(dist < max_exact, dist, large)


def make_bias_tables(rel_bias):
    rel_bias = np.asarray(rel_bias, np.float32)
    kj = np.arange(128)[:, None, None]
    kb = np.arange(2)[None, :, None]
    qi = np.arange(128)[None, None, :]
    dist = qi + 128 - (kb * 128 + kj)
    inwin = (dist >= 0) & (dist < 128)
    bucket = _rel_bucket_np(np.maximum(dist, 0))
    fill = np.float32(NEG / 8.0)
    out = np.empty((128, 12, 2, 128), np.float32)
    for s in range(12):
        g = rel_bias[bucket, SLOT_HEAD[s]]
        out[:, s] = np.where(inwin, g, fill)
    first = out.copy()
    first[:, :, 0, :] = fill
    return out.reshape(128, 12 * 256), first.reshape(128, 12 * 256)


def make_vecs(inp, flag):
    v = np.zeros((128, NV), np.float32)

    def putg(col, g):
        v[:, col:col + 8] = np.asarray(g, np.float32).reshape(8, 128).T

    for l in range(4):
        putg(G_MIX + 8 * l, inp["norm_mix"][l])
        putg(G_FFN + 8 * l, inp["norm_ffn"][l])
    putg(G_KV, inp["kv_norm"])
    putg(G_MEM, inp["mem_norm"])
    putg(G_FIN, inp["final_norm"])
    cw = np.asarray(inp["a_conv_w"], np.float32)
    for l in range(2):
        for j in range(6):
            for t in range(3):
                v[:, V_CONV + (l * 6 + j) * 3 + t] = cw[l, t, j * 128:(j + 1) * 128]
    v[:, V_FLAG] = flag
    sk = np.asarray(inp["b_sinks"], np.float32)
    for lj in range(2):
        for s in range(12):
            v[:, V_SINK + lj * 12 + s] = sk[lj, SLOT_HEAD[s]]
    return v


class Prog:
    ENGS = ("pe", "act", "dve", "pool", "sp")
    EPOCH = 24000

    def __init__(self):
        self.ops = {e: [] for e in self.ENGS}
        self.cnt = {e: 0 for e in self.ENGS}
        self.scnt = {}
        self.lastw = {}
        self.readers = {}

    def _deps(self, reads, writes):
        deps = {}

        def add(tok):
            key = (tok[0], tok[1])
            if deps.get(key, -1) < tok[2]:
                deps[key] = tok[2]

        for r in reads:
            t = self.lastw.get(r)
            if t is not None:
                add(t)
        for w in writes:
            t = self.lastw.get(w)
            if t is not None:
                add(t)
            for key, idx in self.readers.get(w, {}).items():
                add((key[0], key[1], idx))
        return deps

    def _update(self, tok, reads, writes):
        key = (tok[0], tok[1])
        for r in reads:
            d = self.readers.setdefault(r, {})
            if d.get(key, -1) < tok[2]:
                d[key] = tok[2]
        for w in writes:
            self.lastw[w] = tok
            self.readers[w] = {}

    def op(self, eng, fn, reads=(), writes=()):
        deps = self._deps(reads, writes)
        idx = self.cnt[eng]
        self.cnt[eng] += 1
        tok = ("e", eng, idx)
        self.ops[eng].append((fn, deps, tok))
        self._update(tok, reads, writes)

    def dma(self, eng, stream, fn, reads=(), writes=()):
        deps = self._deps(reads, writes)
        idx = self.scnt.get(stream, 0)
        self.scnt[stream] = idx + 1
        if idx > 0:
            key = ("s", stream)
            if deps.get(key, -1) < idx - 1:
                deps[key] = idx - 1
        tok = ("s", stream, idx)
        self.ops[eng].append((fn, deps, tok))
        self._update(tok, reads, writes)

    def wait_all(self, eng, reads):
        deps = self._deps(reads, ())
        self.ops[eng].append((None, deps, None))

    def emit(self, nc, stack):
        n_ep = {e: max(1, -(-self.cnt[e] // self.EPOCH)) for e in self.ENGS}
        esem = {e: [stack.enter_context(nc.semaphore(f"e_{e}_{i}")) for i in range(n_ep[e])]
                for e in self.ENGS if self.cnt[e] > 0}
        ssem = {s: stack.enter_context(nc.semaphore(f"s_{s}")) for s in self.scnt}
        block = stack.enter_context(nc.Block())
        prog = self

        def run(engname, eng):
            known = {}
            for fn, deps, tok in prog.ops[engname]:
                for key, idx in deps.items():
                    if key[0] == "e" and key[1] == "pe" and engname == "pe":
                        continue
                    if known.get(key, -1) >= idx:
                        continue
                    known[key] = idx
                    if key[0] == "e":
                        eng.wait_ge(esem[key[1]][idx // prog.EPOCH], idx % prog.EPOCH + 1)
                    else:
                        eng.wait_ge(ssem[key[1]], 16 * (idx + 1))
                if fn is None:
                    continue
                ins = fn(eng)
                if tok[0] == "e":
                    ins.then_inc(esem[engname][tok[2] // prog.EPOCH], 1)
                else:
                    ins.then_inc(ssem[tok[1]], 16)

        @block.tensor
        def _(e):
            run("pe", e)

        @block.scalar
        def _(e):
            run("act", e)

        @block.vector
        def _(e):
            run("dve", e)

        @block.gpsimd
        def _(e):
            run("pool", e)

        @block.sync
        def _(e):
            run("sp", e)


class StopTile(Exception):
    pass


class NullProg:
    def op(self, *a, **k):
        pass

    def dma(self, *a, **k):
        pass

    def wait_all(self, *a, **k):
        pass


class Builder:
    def __init__(self, nc, T, nlayers=4, final=True, ntiles=NT):
        self.nc = nc
        self.T = T
        self.nlayers = nlayers
        self.final = final
        self.ntiles = ntiles
        self.table, self.total, self.order = piece_table()

    debug = None

    def dump(self, name, src, n, nch=8):
        if self.debug != name:
            return
        P, T = self.P, self.T
        xb = self.xb
        P.op("act", lambda e: e.activation(out=xb[:, 0:nch, 0:n], in_=src, func=AF.Copy),
             reads=[(self.xr, c) for c in range(8)] + [("h", c) for c in range(8)] + [("y", c) for c in range(8)]
             + [("q", c) for c in range(8)] + [("act", c) for c in range(NF)],
             writes=[(self.xr, c) for c in range(8)])
        raise StopTile()

    def bank(self):
        b = self._bank
        self._bank = (b + 1) % 7
        return b

    def wget(self, name):
        if self.recording:
            self.seq.append(name)
            return 0
        i = self.wpos
        assert self.seq[i] == name, (i, self.seq[i], name)
        P, Tn = self.P, self.T
        hi = min(i + NSLOT - 1, len(self.seq))
        while self.wloaded < hi:
            j = self.wloaded
            s = j % NSLOT
            nm = self.seq[j]
            off, n = self.table[nm]
            slot = Tn["slots"][s]
            if nm in self.converted:
                src = Tn["wscr"][off:off + 128 * n].rearrange("(p n) -> p n", p=128)
                P.dma("sp", f"ld{s}", lambda e, dst=slot[:, 0:n], src=src: e.dma_start(out=dst, in_=src),
                      reads=[("wscr", nm)], writes=[("slot", s)])
            else:
                self.converted.add(nm)
                src32 = Tn["wflat"][off:off + 128 * n].rearrange("(p n) -> p n", p=128)
                for c0 in range(0, n, STG):
                    c1 = min(c0 + STG, n)
                    k = self.stgk
                    self.stgk = (k + 1) % 2
                    stg = Tn["stg"][k]
                    assert self.cur_tile <= 0, "staging aliases the second x buffer"
                    sres = [("xB", 4 * k + q) for q in range(4)]
                    P.dma("sp", f"sg{k}", lambda e, stg=stg, src32=src32, c0=c0, c1=c1: e.dma_start(
                        out=stg[:, 0:c1 - c0], in_=src32[:, c0:c1]),
                        reads=[], writes=sres)
                    P.op("act", lambda e, stg=stg, slot=slot, c0=c0, c1=c1: e.activation(
                        out=slot[:, c0:c1], in_=stg[:, 0:c1 - c0], func=AF.Copy),
                        reads=sres, writes=[("slot", s)])
                dst = Tn["wscr"][off:off + 128 * n].rearrange("(p n) -> p n", p=128)
                P.dma("act", f"ws{self.wsk}", lambda e, dst=dst, slot=slot, n=n: e.dma_start(out=dst, in_=slot[:, 0:n]),
                      reads=[("slot", s)], writes=[("wscr", nm)])
                self.wsk = (self.wsk + 1) % 2
            self.wloaded += 1
        self.wpos += 1
        return i % NSLOT

    def run(self, P, recording, seq=None):
        self.P = P
        self.recording = recording
        self.seq = [] if recording else seq
        self.wpos = 0
        self.wloaded = 0
        self._bank = 0
        self.converted = set()
        self.stgk = 0
        self.wsk = 0
        self.xb, self.xr = self.T["xA"], "xA"
        self.stats_ready = False
        self.rstd_valid = False
        self.hp_ready = False
        self.cur_tile = -1
        self.setup()
        self.tile(-1)
        for ti in range(self.ntiles):
            self.tile(ti)
        P.wait_all("act", [(self.xr, c) for c in range(8)] + ["yT"])
        return self.seq

    def setup(self):
        P, T, nc = self.P, self.T, self.nc
        P.op("dve", lambda e: e.memset(T["onesD"][:, :], 1.0 / D), writes=["onesD"])
        P.op("dve", lambda e: e.memset(T["ones1"][:, :], 1.0), writes=["ones1"])
        P.op("dve", lambda e: e.memset(T["carry"][:, :, :], 0.0), writes=[("carry", i) for i in range(12)])
        P.dma("act", "misc", lambda e: e.dma_start(out=T["vecs"][:, :], in_=T["vecs_d"][:, :]), writes=["vecs"])
        xflat = self.xb[:, :, :].rearrange("p c t -> p (c t)")
        P.dma("act", "misc", lambda e: e.dma_start(out=xflat[:, 0:128], in_=T["ident_d"][:, :]),
              writes=[(self.xr, c) for c in range(8)])
        P.op("act", lambda e: e.activation(out=T["ident"][:, :], in_=xflat[:, 0:128], func=AF.Copy),
             reads=[(self.xr, c) for c in range(8)], writes=["ident"])
        for src_name, dst_name in (("bias_d", "bias8"), ("biasf_d", "biasF")):
            P.dma("act", "misc", lambda e, s=src_name: e.dma_start(out=xflat[:, 0:3072], in_=T[s][:, :]),
                  writes=[(self.xr, c) for c in range(8)])
            dstf = T[dst_name][:, :, :].rearrange("p s n -> p (s n)")
            P.op("act", lambda e, dstf=dstf: e.activation(out=dstf, in_=xflat[:, 0:3072], func=AF.Copy, scale=8.0),
                 reads=[(self.xr, c) for c in range(8)], writes=[dst_name])
        P.op("act", lambda e: e.activation(out=T["esink"][:, :], in_=T["vecs"][:, V_SINK:V_SINK + 24], func=AF.Exp),
             reads=["vecs"], writes=["esink"])
        P.dma("act", "misc",
              lambda e, xb=self.xb: e.dma_start(out=xb[:, :, 0:N_MEM],
                                    in_=T["memT"].rearrange("(c p) t -> p c t", p=128)),
              writes=[(self.xr, c) for c in range(8)])
        self.rmsnorm(N_MEM, G_MEM)
        hb, ps = T["hb"], T["ps"]
        for l in range(4):
            s = self.wget(f"memkv{l}")
            sl = T["slots"][s]
            for i in range(2):
                b = self.bank()
                for k in range(8):
                    P.op("pe", lambda e, b=b, k=k, i=i, sl=sl: e.matmul(
                        ps[b][:, 0:N_MEM], sl[:, k * 512 + i * 128:k * 512 + (i + 1) * 128], hb[:, k, 0:N_MEM],
                        start=(k == 0), stop=(k == 7)),
                        reads=[("slot", s), ("h", k)], writes=[("ps", b)])
                P.op("act", lambda e, b=b, l=l, i=i: e.activation(out=T["mkT"][:, l, i, :], in_=ps[b][:, 0:N_MEM], func=AF.Copy),
                     reads=[("ps", b)], writes=["mk"])
            for mc in range(2):
                b = self.bank()
                for k in range(8):
                    P.op("pe", lambda e, b=b, k=k, mc=mc, sl=sl: e.matmul(
                        ps[b][:, 0:256], hb[:, k, mc * 128:(mc + 1) * 128], sl[:, k * 512 + 256:k * 512 + 512],
                        start=(k == 0), stop=(k == 7)),
                        reads=[("slot", s), ("h", k)], writes=[("ps", b)])
                P.op("act", lambda e, b=b, l=l, mc=mc: e.activation(out=T["mv"][:, l, mc, :], in_=ps[b][:, 0:256], func=AF.Copy),
                     reads=[("ps", b)], writes=["mv"])

    def stat_square(self, m, n):
        P, T = self.P, self.T
        xb, hb = self.xb, T["hb"]
        P.op("act", lambda e: e.activation(out=hb[:, m, 0:n], in_=xb[:, m, 0:n], func=AF.Square),
             reads=[(self.xr, m)], writes=[("h", m)])

    def stat_mm(self, m, n):
        P, T = self.P, self.T
        P.op("pe", lambda e: e.matmul(T["ps"][7][:, 0:n], T["onesD"][:, :], T["hb"][:, m, 0:n],
                                      start=(m == 0), stop=(m == 7)),
             reads=[("h", m), "onesD"], writes=[("ps", 7)])

    def hprime(self, m, n, gcol):
        P, T = self.P, self.T
        xb, qb, vecs = self.xb, T["qb"], T["vecs"]
        P.op("dve", lambda e: e.tensor_scalar(out=qb[:, m, 0:n], in0=xb[:, m, 0:n],
                                              scalar1=vecs[:, gcol + m:gcol + m + 1], scalar2=None, op0=ALU.mult),
             reads=[(self.xr, m), "vecs"], writes=[("q", m)])

    def rmsnorm(self, n, gcol, to_x=False, pre=None, post=None):
        P, T = self.P, self.T
        xb, hb, actb, ps, vecs = self.xb, T["hb"], T["actb"], T["ps"], T["vecs"]
        rstd, rres = (T["rstd2"], "rstd2") if to_x else (T["rstd"], "rstd")
        xr = self.xr
        used_hp = False
        if self.hp_ready:
            assert not self.rstd_valid and not to_x
            if pre is not None:
                pre()
            self.stat_mm(7, n)
            if post is not None:
                post()
            self.stats_ready = True
            self.hp_ready = False
            used_hp = pre is not None
        if not self.rstd_valid:
            if not self.stats_ready:
                P.op("act", lambda e: e.activation(out=actb[:, 0:8, 0:n], in_=xb[:, :, 0:n], func=AF.Square),
                     reads=[(xr, c) for c in range(8)], writes=[("act", c) for c in range(8)])
                for c in range(8):
                    P.op("pe", lambda e, c=c: e.matmul(ps[7][:, 0:n], T["onesD"][:, :], actb[:, c, 0:n],
                                                       start=(c == 0), stop=(c == 7)),
                         reads=[("act", c), "onesD"], writes=[("ps", 7)])
            P.op("act", lambda e: e.activation(out=rstd[:, 0:n], in_=ps[7][:, 0:n], func=AF.Ln, bias=T["epsc"][:, 0:1]),
                 reads=[("ps", 7), "epsc"], writes=[rres])
            P.op("act", lambda e: e.activation(out=rstd[:, 0:n], in_=rstd[:, 0:n], func=AF.Exp, scale=-0.5),
                 reads=[rres], writes=[rres])
        self.stats_ready = False
        self.rstd_valid = False
        for c in range(8):
            if to_x:
                P.op("dve", lambda e, c=c: e.scalar_tensor_tensor(
                    out=xb[:, c, 0:n], in0=xb[:, c, 0:n], scalar=vecs[:, gcol + c:gcol + c + 1], in1=rstd[:, 0:n],
                    op0=ALU.mult, op1=ALU.mult),
                    reads=[(xr, c), rres, "vecs"], writes=[(xr, c)])
            else:
                P.op("dve", lambda e, c=c: e.scalar_tensor_tensor(
                    out=hb[:, c, 0:n], in0=xb[:, c, 0:n], scalar=vecs[:, gcol + c:gcol + c + 1], in1=rstd[:, 0:n],
                    op0=ALU.mult, op1=ALU.mult),
                    reads=[(xr, c), rres, "vecs"], writes=[("h", c)])
        return used_hp

    def proj_chunk(self, s, col0, kstride, nk, rhs_t, rhs_res, n):
        P, T = self.P, self.T
        ps, sl = T["ps"], T["slots"][s]
        b = self.bank()
        for k in range(nk):
            P.op("pe", lambda e, k=k: e.matmul(ps[b][:, 0:n], sl[:, k * kstride + col0:k * kstride + col0 + 128],
                                               rhs_t[:, k, 0:n], start=(k == 0), stop=(k == nk - 1)),
                 reads=[("slot", s), (rhs_res, k)], writes=[("ps", b)])
        return b

    def residual_add(self, b, m, n):
        P, T = self.P, self.T
        xb, ps = self.xb, T["ps"]
        xr = self.xr
        P.op("dve", lambda e: e.tensor_tensor(out=xb[:, m, 0:n], in0=ps[b][:, 0:n], in1=xb[:, m, 0:n], op=ALU.add),
             reads=[("ps", b), (xr, m)], writes=[(xr, m)])

    def a_mixer(self, l, n, halo):
        P, T = self.P, self.T
        ps, hb, qb, yb, vecs = T["ps"], T["hb"], T["qb"], T["yb"], T["vecs"]
        order = [("q", 0), ("q", 1)]
        for j in range(6):
            order += [("u", j), ("C", j), ("B", j)]
        rstd = T["rstd"]
        s0 = self.wget(f"ain{l}_0")
        hpb = {}

        def pre():
            hpb[0] = self.proj_chunk(s0, 0, 512, 8, qb, "q", n)

        def post():
            hpb[1] = self.proj_chunk(s0, 128, 512, 8, qb, "q", n)

        used_hp = self.rmsnorm(n, G_MIX + 8 * l, pre=pre, post=post)
        self.dump(f"h{l}", hb[:, :, 0:n], n)
        for p in range(5):
            s = s0 if p == 0 else self.wget(f"ain{l}_{p}")
            for jj in range(4):
                kind, j = order[p * 4 + jj]
                if used_hp and p == 0 and jj < 2:
                    b = hpb[jj]
                else:
                    b = self.proj_chunk(s, jj * 128, 512, 8, hb, "h", n)
                r = j % 2
                usb, vb, accb = T["usb"][r], T["vb"][r], T["accb"][r]
                if kind == "q" and used_hp:
                    P.op("dve", lambda e, b=b, j=j: e.tensor_tensor(
                        out=qb[:, 6 + j, 0:n], in0=ps[b][:, 0:n], in1=rstd[:, 0:n], op=ALU.mult),
                        reads=[("ps", b), "rstd"], writes=[("q", 6 + j)])
                elif kind == "u":
                    P.op("act", lambda e, b=b, usb=usb: e.activation(out=usb[:, 0:n], in_=ps[b][:, 0:n], func=AF.Copy),
                         reads=[("ps", b)], writes=[("usb", r)])
                elif kind == "C":
                    ci = l * 6 + j
                    cw = V_CONV + ci * 3
                    P.op("dve", lambda e, vb=vb, ci=ci: e.tensor_copy(out=vb[:, 0:2], in_=T["carry"][:, ci, :]),
                         reads=[("carry", ci)], writes=[("vb", r)])
                    P.op("dve", lambda e, b=b, vb=vb, usb=usb: e.tensor_tensor(
                        out=vb[:, 2:2 + n], in0=ps[b][:, 0:n], in1=usb[:, 0:n], op=ALU.mult),
                        reads=[("ps", b), ("usb", r)], writes=[("vb", r)])
                    if halo:
                        P.op("dve", lambda e, vb=vb, ci=ci: e.tensor_scalar(
                            out=T["carry"][:, ci, :], in0=vb[:, n:n + 2], scalar1=vecs[:, V_FLAG:V_FLAG + 1],
                            scalar2=None, op0=ALU.mult),
                            reads=[("vb", r), "vecs"], writes=[("carry", ci)])
                    else:
                        P.op("dve", lambda e, vb=vb, ci=ci: e.tensor_copy(out=T["carry"][:, ci, :], in_=vb[:, n:n + 2]),
                             reads=[("vb", r)], writes=[("carry", ci)])
                    P.op("dve", lambda e, vb=vb, accb=accb, cw=cw: e.tensor_scalar(
                        out=accb[:, 0:n], in0=vb[:, 2:2 + n], scalar1=vecs[:, cw + 2:cw + 3], scalar2=None, op0=ALU.mult),
                        reads=[("vb", r), "vecs"], writes=[("accb", r)])
                    P.op("dve", lambda e, vb=vb, accb=accb, cw=cw: e.scalar_tensor_tensor(
                        out=accb[:, 0:n], in0=vb[:, 1:1 + n], scalar=vecs[:, cw + 1:cw + 2], in1=accb[:, 0:n],
                        op0=ALU.mult, op1=ALU.add),
                        reads=[("vb", r), ("accb", r), "vecs"], writes=[("accb", r)])
                    P.op("dve", lambda e, vb=vb, accb=accb, cw=cw: e.scalar_tensor_tensor(
                        out=accb[:, 0:n], in0=vb[:, 0:n], scalar=vecs[:, cw:cw + 1], in1=accb[:, 0:n],
                        op0=ALU.mult, op1=ALU.add),
                        reads=[("vb", r), ("accb", r), "vecs"], writes=[("accb", r)])
                elif kind == "B":
                    P.op("dve", lambda e, b=b, j=j, accb=accb: e.tensor_tensor(
                        out=yb[:, j, 0:n], in0=ps[b][:, 0:n], in1=accb[:, 0:n], op=ALU.mult),
                        reads=[("ps", b), ("accb", r)], writes=[("y", j)])
                else:
                    P.op("act", lambda e, b=b, j=j: e.activation(out=qb[:, 6 + j, 0:n], in_=ps[b][:, 0:n], func=AF.Copy),
                         reads=[("ps", b)], writes=[("q", 6 + j)])
            if p == 0:
                self.mem_attn(l, n)

    def mem_attn(self, l, n):
        P, T = self.P, self.T
        ps, qb, yb, pm, rc = T["ps"], T["qb"], T["yb"], T["pm"], T["rc"]
        mkT, mv = T["mkT"], T["mv"]

        def scores(hm):
            i, half = hm // 2, hm % 2
            r = hm % 3
            lo = half * 64
            for mc in range(2):
                b = self.bank()
                P.op("pe", lambda e, b=b, mc=mc: e.matmul(
                    ps[b][:, 0:n], mkT[lo:lo + 64, l, i, mc * 128:(mc + 1) * 128], qb[lo:lo + 64, 6 + i, 0:n],
                    start=True, stop=True),
                    reads=["mk", ("q", 6 + i)], writes=[("ps", b)])
                P.op("act", lambda e, b=b, mc=mc: e.activation(out=pm[r][:, mc, 0:n], in_=ps[b][:, 0:n],
                                                               func=AF.Exp, scale=0.125),
                     reads=[("ps", b)], writes=[("pm", r, mc)])

        def pv(hm):
            i, half = hm // 2, hm % 2
            r = hm % 2
            rp = hm % 3
            lo = half * 64
            bo, bd = self.bank(), self.bank()
            for mc in range(2):
                P.op("pe", lambda e, mc=mc: e.matmul(ps[bo][:, 0:n], mv[:, l, mc, i * 128:(i + 1) * 128],
                                                     pm[rp][:, mc, 0:n], start=(mc == 0), stop=(mc == 1)),
                     reads=["mv", ("pm", rp, mc)], writes=[("ps", bo)])
            for mc in range(2):
                P.op("pe", lambda e, mc=mc: e.matmul(ps[bd][:, 0:n], T["ones1"][:, :], pm[rp][:, mc, 0:n],
                                                     start=(mc == 0), stop=(mc == 1)),
                     reads=["ones1", ("pm", rp, mc)], writes=[("ps", bd)])
            P.op("act", lambda e: e.activation(out=rc[r][lo:lo + 64, 0:n], in_=ps[bd][lo:lo + 64, 0:n], func=AF.Ln),
                 reads=[("ps", bd)], writes=[("rc", r)])
            P.op("act", lambda e: e.activation(out=rc[r][lo:lo + 64, 0:n], in_=rc[r][lo:lo + 64, 0:n], func=AF.Exp,
                                               scale=-1.0),
                 reads=[("rc", r)], writes=[("rc", r)])
            P.op("dve", lambda e: e.tensor_tensor(out=yb[lo:lo + 64, 6 + i, 0:n], in0=ps[bo][lo:lo + 64, 0:n],
                                                  in1=rc[r][lo:lo + 64, 0:n], op=ALU.mult),
                 reads=[("ps", bo), ("rc", r)], writes=[("y", 6 + i)])

        for hm in range(6):
            if hm < 4:
                scores(hm)
            if hm >= 2:
                pv(hm - 2)

    def out_proj(self, prefix, n, next_gcol):
        T = self.T
        for p in range(2):
            s = self.wget(f"{prefix}_{p}")
            for jj in range(4):
                m = p * 4 + jj
                b = self.proj_chunk(s, jj * 128, 512, 8, T["yb"], "y", n)
                if m >= 1:
                    self.stat_mm(m - 1, n)
                self.residual_add(b, m, n)
                self.stat_square(m, n)
                self.hprime(m, n, next_gcol)
        self.hp_ready = True

    def ffn(self, l, n, next_gcol):
        P, T = self.P, self.T
        ps, hb, actb = T["ps"], T["hb"], T["actb"]
        qb, rstd = T["qb"], T["rstd"]
        sg0 = self.wget(f"g{l}_0")
        su0 = self.wget(f"u{l}_0")
        hpb = {}

        def pre():
            hpb["g"] = self.proj_chunk(sg0, 0, 512, 8, qb, "q", n)

        def post():
            hpb["u"] = self.proj_chunk(su0, 0, 512, 8, qb, "q", n)

        used_hp = self.rmsnorm(n, G_FFN + 8 * l, pre=pre, post=post)
        self.dump(f"ffnh{l}", hb[:, :, 0:n], n)
        for fg in range(6):
            ncol = 512 if fg < 5 else 256
            sg = sg0 if fg == 0 else self.wget(f"g{l}_{fg}")
            su = su0 if fg == 0 else self.wget(f"u{l}_{fg}")
            for jj in range(ncol // 128):
                f = fg * 4 + jj
                r = f % 2
                usb = T["usb"][r]
                if f == 0 and used_hp:
                    bg, bu = hpb["g"], hpb["u"]
                    tmp = T["accb"][0]
                    P.op("dve", lambda e, bg=bg, usb=usb: e.tensor_tensor(
                        out=usb[:, 0:n], in0=ps[bg][:, 0:n], in1=rstd[:, 0:n], op=ALU.mult),
                        reads=[("ps", bg), "rstd"], writes=[("usb", r)])
                    P.op("act", lambda e, usb=usb: e.activation(out=usb[:, 0:n], in_=usb[:, 0:n], func=AF.Silu),
                         reads=[("usb", r)], writes=[("usb", r)])
                    P.op("dve", lambda e, bu=bu, tmp=tmp: e.tensor_tensor(
                        out=tmp[:, 0:n], in0=ps[bu][:, 0:n], in1=rstd[:, 0:n], op=ALU.mult),
                        reads=[("ps", bu), "rstd"], writes=[("accb", 0)])
                    P.op("dve", lambda e, tmp=tmp, usb=usb: e.tensor_tensor(
                        out=actb[:, 0, 0:n], in0=tmp[:, 0:n], in1=usb[:, 0:n], op=ALU.mult),
                        reads=[("accb", 0), ("usb", r)], writes=[("act", 0)])
                    continue
                bg = self.proj_chunk(sg, jj * 128, ncol, 8, hb, "h", n)
                bu = self.proj_chunk(su, jj * 128, ncol, 8, hb, "h", n)
                P.op("act", lambda e, bg=bg, usb=usb: e.activation(out=usb[:, 0:n], in_=ps[bg][:, 0:n], func=AF.Silu),
                     reads=[("ps", bg)], writes=[("usb", r)])
                P.op("dve", lambda e, bu=bu, usb=usb, f=f: e.tensor_tensor(
                    out=actb[:, f, 0:n], in0=ps[bu][:, 0:n], in1=usb[:, 0:n], op=ALU.mult),
                    reads=[("ps", bu), ("usb", r)], writes=[("act", f)])
        self.dump(f"act{l}", actb[:, 0:8, 0:n], n)
        for m in range(8):
            s = self.wget(f"d{l}_{m}")
            b = self.proj_chunk(s, 0, 128, NF, actb, "act", n)
            if m >= 1:
                self.stat_mm(m - 1, n)
            self.residual_add(b, m, n)
            self.stat_square(m, n)
            if next_gcol is not None:
                self.hprime(m, n, next_gcol)
        if next_gcol is not None:
            self.hp_ready = True
        else:
            self.stat_mm(7, n)
            self.stats_ready = True

    def kv_proj(self, n, c0, blk0):
        P, T = self.P, self.T
        ps, hb, kT, vS = T["ps"], T["hb"], T["kT"], T["vS"]
        s = self.wget("kv")
        sl = T["slots"][s]
        qb, rstd = T["qb"], T["rstd"]
        hpb = {}

        def pre():
            hpb[0] = self.proj_chunk(s, 0, 512, 8, qb, "q", n)

        def post():
            hpb[1] = self.proj_chunk(s, 128, 512, 8, qb, "q", n)

        used_hp = self.rmsnorm(n, G_KV, pre=pre, post=post)
        self.rstd_valid = True
        nvalid = n - c0
        nblk = nvalid // 128
        for pr in range(2):
            if used_hp:
                b = hpb[pr]
                P.op("dve", lambda e, b=b, pr=pr: e.tensor_tensor(
                    out=kT[:, pr, blk0 * 128:blk0 * 128 + nvalid], in0=ps[b][:, c0:n], in1=rstd[:, c0:n], op=ALU.mult),
                    reads=[("ps", b), "rstd"], writes=[("kT", blk0 + t) for t in range(nblk)])
                continue
            b = self.proj_chunk(s, pr * 128, 512, 8, hb, "h", n)
            P.op("act", lambda e, b=b, pr=pr: e.activation(
                out=kT[:, pr, blk0 * 128:blk0 * 128 + nvalid], in_=ps[b][:, c0:n], func=AF.Copy),
                reads=[("ps", b)], writes=[("kT", blk0 + t) for t in range(nblk)])
        for tb in range(nblk):
            b = self.bank()
            for k in range(8):
                P.op("pe", lambda e, b=b, k=k, tb=tb: e.matmul(
                    ps[b][:, 0:256], hb[:, k, c0 + tb * 128:c0 + (tb + 1) * 128], sl[:, k * 512 + 256:k * 512 + 512],
                    start=(k == 0), stop=(k == 7)),
                    reads=[("slot", s), ("h", k)], writes=[("ps", b)])
            P.op("act", lambda e, b=b, tb=tb: e.activation(out=vS[:, blk0 + tb, :], in_=ps[b][:, 0:256], func=AF.Copy),
                 reads=[("ps", b)], writes=[("vS", blk0 + tb)])

    def swa(self, lj, ti):
        P, T = self.P, self.T
        ps, qb, yb, pS, rc = T["ps"], T["qb"], T["yb"], T["pS"], T["rc"]
        kT, vS, bias8, biasF, ident = T["kT"], T["vS"], T["bias8"], T["biasF"], T["ident"]

        def scores(s):
            c, half = s // 2, s % 2
            lo = half * 64
            pair = c // 3
            r = s % 3
            for hbk in range(2):
                b = self.bank()
                if ti == 0 and hbk == 0:
                    P.op("pe", lambda e, b=b: e.matmul(ps[b][:, 0:256], ident[:, :], biasF[:, s, :],
                                                       start=True, stop=False, skip_group_check=True),
                         reads=["ident", "biasF"], writes=[("ps", b)])
                    P.op("pe", lambda e, b=b: e.matmul(ps[b][:, 256:512], ident[:, :], bias8[:, s, :],
                                                       start=False, stop=False, skip_group_check=True),
                         reads=["ident", "bias8"], writes=[("ps", b)])
                    sgc = True
                else:
                    P.op("pe", lambda e, b=b: e.matmul(
                        ps[b][:, :].rearrange("p (a n) -> p a n", a=2), ident[:, :],
                        bias8[:, s:s + 1, :].broadcast_to([128, 2, 256]),
                        start=True, stop=False),
                        reads=["ident", "bias8"], writes=[("ps", b)])
                    sgc = False
                for qq in range(2):
                    qi = 2 * hbk + qq
                    nblk = ti * 4 + qi
                    for kb in range(2):
                        blk = nblk + kb
                        last = (qq == 1 and kb == 1)
                        P.op("pe", lambda e, b=b, qq=qq, kb=kb, blk=blk, qi=qi, last=last, sgc=sgc: e.matmul(
                            ps[b][:, qq * 256 + kb * 128:qq * 256 + (kb + 1) * 128],
                            kT[lo:lo + 64, pair, blk * 128:(blk + 1) * 128],
                            qb[lo:lo + 64, c, qi * 128:(qi + 1) * 128],
                            start=False, stop=last, skip_group_check=sgc),
                            reads=[("kT", blk), ("q", c)], writes=[("ps", b)])
                P.op("act", lambda e, b=b, hbk=hbk: e.activation(out=pS[r][:, hbk * 512:(hbk + 1) * 512], in_=ps[b][:, :],
                                                                 func=AF.Exp, scale=0.125),
                     reads=[("ps", b)], writes=[("pS", r, hbk)])

        def pv(s):
            c, half = s // 2, s % 2
            lo = half * 64
            pair = c // 3
            r = s % 2
            rp = s % 3
            bo, bd = self.bank(), self.bank()
            for qi in range(4):
                nblk = ti * 4 + qi
                for kb in range(2):
                    blk = nblk + kb
                    P.op("pe", lambda e, qi=qi, kb=kb, blk=blk: e.matmul(
                        ps[bo][:, qi * 128:(qi + 1) * 128], vS[:, blk, pair * 128:(pair + 1) * 128],
                        pS[rp][:, qi * 256 + kb * 128:qi * 256 + (kb + 1) * 128],
                        start=(kb == 0), stop=(kb == 1)),
                        reads=[("vS", blk), ("pS", rp, qi // 2)], writes=[("ps", bo)])
            for kb in range(2):
                P.op("pe", lambda e, kb=kb: e.matmul(
                    ps[bd][:, :].rearrange("p (a q) -> p a q", a=4), T["ones1"][:, :],
                    pS[rp][:, :].rearrange("p (a b q) -> p a b q", a=4, b=2)[:, :, kb, :],
                    start=(kb == 0), stop=(kb == 1)),
                    reads=["ones1", ("pS", rp, 0), ("pS", rp, 1)], writes=[("ps", bd)])
            es = T["esink"][lo:lo + 64, lj * 12 + s:lj * 12 + s + 1]
            P.op("act", lambda e: e.activation(out=rc[r][lo:lo + 64, :], in_=ps[bd][lo:lo + 64, :], func=AF.Ln, bias=es),
                 reads=[("ps", bd), "esink"], writes=[("rc", r)])
            P.op("act", lambda e: e.activation(out=rc[r][lo:lo + 64, :], in_=rc[r][lo:lo + 64, :], func=AF.Exp,
                                               scale=-1.0),
                 reads=[("rc", r)], writes=[("rc", r)])
            P.op("dve", lambda e: e.tensor_tensor(out=yb[lo:lo + 64, c, :], in0=ps[bo][lo:lo + 64, :],
                                                  in1=rc[r][lo:lo + 64, :], op=ALU.mult),
                 reads=[("ps", bo), ("rc", r)], writes=[("y", c)])

        for s in range(14):
            if s < 12:
                scores(s)
            if s >= 2:
                pv(s - 2)

    def b_layer(self, lj, ti):
        P, T = self.P, self.T
        l = 2 + lj
        n = TT
        ps, hb, qb = T["ps"], T["hb"], T["qb"]
        rstd = T["rstd"]
        s0 = self.wget(f"bq{lj}_0")
        hpb = {}

        def pre():
            hpb[0] = self.proj_chunk(s0, 0, 512, 8, qb, "q", n)

        def post():
            hpb[1] = self.proj_chunk(s0, 128, 512, 8, qb, "q", n)

        used_hp = self.rmsnorm(n, G_MIX + 8 * l, pre=pre, post=post)
        for p in range(2):
            s = s0 if p == 0 else self.wget(f"bq{lj}_{p}")
            for jj in range(4):
                c = p * 4 + jj
                if used_hp and c < 2:
                    b = hpb[c]
                    P.op("dve", lambda e, b=b, c=c: e.tensor_tensor(
                        out=qb[:, c, 0:n], in0=ps[b][:, 0:n], in1=rstd[:, 0:n], op=ALU.mult),
                        reads=[("ps", b), "rstd"], writes=[("q", c)])
                    continue
                b = self.proj_chunk(s, jj * 128, 512, 8, hb, "h", n)
                P.op("act", lambda e, b=b, c=c: e.activation(out=qb[:, c, 0:n], in_=ps[b][:, 0:n], func=AF.Copy),
                     reads=[("ps", b)], writes=[("q", c)])
        self.mem_attn(l, n)
        self.swa(lj, ti)
        self.out_proj(f"bout{lj}", n, G_FFN + 8 * l)
        self.ffn(l, n, self.next_gcol(l))

    def next_gcol(self, l):
        if l + 1 >= self.nlayers:
            return None
        return G_KV if l == 1 else G_MIX + 8 * (l + 1)

    def tile_body(self, ti, halo, n):
        T = self.T
        na = min(self.nlayers, 2)
        for l in range(na):
            self.a_mixer(l, n, halo)
            self.dump(f"y{l}", T["yb"][:, :, 0:n], n)
            self.out_proj(f"aout{l}", n, G_FFN + 8 * l)
            self.dump(f"xmix{l}", self.xb[:, :, 0:n], n)
            self.ffn(l, n, self.next_gcol(l))
        if self.nlayers > 2:
            if halo:
                self.kv_proj(n, HALO - 128, 0)
            else:
                self.kv_proj(n, 0, 1 + ti * 4)
        if halo:
            return
        for lj in range(self.nlayers - 2):
            self.b_layer(lj, ti)
        if self.final:
            self.rmsnorm(n, G_FIN, to_x=True)

    def xbuf_for(self, ti):
        if ti <= 0 or ti % 2 == 0:
            return self.T["xA"], "xA"
        return self.T["xB"], "xB"

    def emit_xload(self, ti):
        P, T = self.P, self.T
        halo = ti < 0
        n = HALO if halo else TT
        t0 = 0 if halo else HALO + ti * TT
        xb, xr = self.xbuf_for(ti)
        xsrc = T["xT"].rearrange("(c p) t -> p c t", p=128)[:, :, t0:t0 + n]
        P.dma("sp", "xl", lambda e: e.dma_start(out=xb[:, :, 0:n], in_=xsrc),
              reads=[], writes=[(xr, c) for c in range(8)])

    def tile(self, ti):
        P, T = self.P, self.T
        halo = ti < 0
        n = HALO if halo else TT
        self.cur_tile = ti
        self.xb, self.xr = self.xbuf_for(ti)
        self.stats_ready = False
        self.rstd_valid = False
        self.hp_ready = False
        if halo:
            self.emit_xload(-1)
        if ti >= 1 and ti + 1 < self.ntiles:
            self.emit_xload(ti + 1)
        try:
            self.tile_body(ti, halo, n)
        except StopTile:
            pass
        if halo:
            self.emit_xload(0)
            return
        xb, xr = self.xb, self.xr
        ydst = T["yT"].rearrange("(c p) t -> p c t", p=128)[:, :, ti * TT:(ti + 1) * TT]
        P.dma("sp", "st", lambda e: e.dma_start(out=ydst, in_=xb[:, :, 0:n]),
              reads=[(xr, c) for c in range(8)], writes=["yT"])
        if ti == 0 and self.ntiles > 1:
            self.emit_xload(1)


def build_nc(nlayers=4, final=True, ntiles=NT, debug=None):
    nc = bass.Bass("TRN2", target_bir_lowering=False)
    table, total, order = piece_table()
    T = {}
    T["xT"] = nc.dram_tensor("xT", [D, HALO + TOK], F32, kind="ExternalInput").ap()
    T["memT"] = nc.dram_tensor("memT", [D, N_MEM], F32, kind="ExternalInput").ap()
    T["vecs_d"] = nc.dram_tensor("vecs", [128, NV], F32, kind="ExternalInput").ap()
    T["bias_d"] = nc.dram_tensor("biasT", [128, 3072], F32, kind="ExternalInput").ap()
    T["biasf_d"] = nc.dram_tensor("biasF", [128, 3072], F32, kind="ExternalInput").ap()
    T["ident_d"] = nc.dram_tensor("ident", [128, 128], F32, kind="ExternalInput").ap()
    T["wflat"] = nc.dram_tensor("wflat", [total], F32, kind="ExternalInput").ap()
    T["wscr"] = nc.dram_tensor("wscr", [total], BF16, kind="Internal").ap()
    T["yT"] = nc.dram_tensor("yT", [D, TOK], F32, kind="ExternalOutput").ap()

    stack = ExitStack()
    with stack:
        def sb(name, shape, dt):
            return stack.enter_context(nc.sbuf_tensor(name, shape, dt))

        T["xA"] = sb("xb", [128, 8, TT], F32)
        T["hb"] = sb("hb", [128, 8, TT], BF16)
        T["qb"] = sb("qb", [128, 8, TT], BF16)
        T["yb"] = sb("yb", [128, 8, TT], BF16)
        T["actb"] = sb("actb", [128, NF, TT], BF16)
        T["usb"] = [sb(f"usb{i}", [128, TT], F32) for i in range(2)]
        T["vb"] = [sb(f"vb{i}", [128, TT + 2], F32) for i in range(2)]
        T["accb"] = [sb(f"accb{i}", [128, TT], F32) for i in range(2)]
        T["rstd"] = sb("rstd", [128, TT], F32)
        T["rstd2"] = sb("rstd2", [128, TT], F32)
        T["pS"] = [sb(f"pS{i}", [128, 1024], BF16) for i in range(3)]
        T["pm"] = [sb(f"pm{i}", [128, 2, TT], BF16) for i in range(3)]
        T["rc"] = [sb(f"rc{i}", [128, TT], F32) for i in range(2)]
        T["kT"] = sb("kT", [128, 2, NBLK * 128], BF16)
        T["vS"] = sb("vS", [128, NBLK, 256], BF16)
        T["bias8"] = sb("bias8_sb", [128, 12, 256], BF16)
        T["biasF"] = sb("biasF_sb", [128, 12, 256], BF16)
        T["mkT"] = sb("mkT", [128, 4, 2, N_MEM], BF16)
        T["mv"] = sb("mv", [128, 4, 2, 256], BF16)
        T["vecs"] = sb("vecs_sb", [128, NV], F32)
        T["esink"] = sb("esink", [128, 24], F32)
        T["carry"] = sb("carry", [128, 12, 2], F32)
        T["onesD"] = sb("onesD", [128, 128], BF16)
        T["ones1"] = sb("ones1", [128, 128], BF16)
        T["ident"] = sb("ident_sb", [128, 128], BF16)
        T["epsc"] = sb("epsc", [128, 1], F32)
        T["slots"] = [sb(f"slot{i}", [128, SLOT_ELEMS], BF16) for i in range(NSLOT)]
        T["xB"] = sb("xB", [128, 8, TT], F32)
        xbf = T["xB"][:, :, :].rearrange("p c t -> p (c t)")
        T["stg"] = [xbf[:, i * STG:(i + 1) * STG] for i in range(2)]
        T["ps"] = [stack.enter_context(nc.psum_tensor(f"ps{i}", [128, 512], F32)) for i in range(8)]

        bld = Builder(nc, T, nlayers=nlayers, final=final, ntiles=ntiles)
        bld.debug = debug
        seq = bld.run(NullProg(), recording=True)
        P = Prog()
        P.op("dve", lambda e: e.memset(T["epsc"][:, :], EPS), writes=["epsc"])
        bld.run(P, recording=False, seq=seq)
        P.emit(nc, stack)
    return nc


def make_in_maps(inp):
    x = np.asarray(inp["x"], np.float32)
    mem = np.asarray(inp["mem"], np.float32)
    wflat = pack_weights(inp)
    biasT, biasFirst = make_bias_tables(inp["rel_bias"])
    ident = np.eye(128, dtype=np.float32)
    in_maps = []
    for c in range(NCORE):
        b, qtr = c // 4, c % 4
        xt = np.zeros((D, HALO + TOK), np.float32)
        lo = qtr * TOK
        if qtr > 0:
            xt[:, :] = x[b, lo - HALO:lo + TOK, :].T
        else:
            xt[:, HALO:] = x[b, lo:lo + TOK, :].T
        in_maps.append({
            "xT": xt,
            "memT": np.ascontiguousarray(mem[b].T),
            "vecs": make_vecs(inp, 0.0 if qtr == 0 else 1.0),
            "biasT": biasT,
            "biasF": biasFirst if qtr == 0 else biasT,
            "ident": ident,
            "wflat": wflat,
        })
    return in_maps


_NC_CACHE = {}


def kernel(**inputs):
    in_maps = make_in_maps(inputs)
    if "nc" not in _NC_CACHE:
        _NC_CACHE["nc"] = build_nc()
    nc = _NC_CACHE["nc"]
    res = run_bass_kernel_spmd(nc, in_maps, core_ids=list(range(NCORE)))
    out = np.empty((BATCH, SEQ, D), np.float32)
    for c in range(NCORE):
        b, qtr = c // 4, c % 4
        out[b, qtr * TOK:(qtr + 1) * TOK, :] = res.results[c]["yT"].T
    return out
```
